# Optimizing a Trainium2 kernel written in Bass

```python
import jax, jax.numpy as jnp
from jax import lax
import numpy as np

D_MODEL = 1024
BATCH = 8
SEQ = 2048
DEPTH = 1
DEC_BATCH = 128
DEC_SEQ = 1
PAST_LEN = 16384
PAGE_SIZE = 128

POOL_WINDOWS = (2, 4, 8, 16)
N_POOL_GROUPS = len(POOL_WINDOWS)
POOL_WIDTH = D_MODEL // 2
POOL_GROUP = POOL_WIDTH // N_POOL_GROUPS
POOL_STATE = max(POOL_WINDOWS) - 1
GMLP_WIDTH = D_MODEL // 2
N_GMLP_GROUPS = 4
GMLP_GROUP = GMLP_WIDTH // N_GMLP_GROUPS
CHUNK = 128
D_FF = -(-8 * D_MODEL // (3 * 256)) * 256
IN_WIDTH = POOL_WIDTH + 2 * GMLP_WIDTH + 2 * D_MODEL
EPS = 1e-6

kernel_name = "pool_gmlp_gated_hybrid_step"


def rmsnorm(x, g):
    xf = x.astype(jnp.float32)
    y = xf * lax.rsqrt(jnp.mean(xf * xf, axis=-1, keepdims=True) + EPS)
    return (y * g.astype(jnp.float32)).astype(x.dtype)


def pool_mixer(a, pos0, w_pool, s_pool):
    B, T, _ = a.shape
    af = a.astype(jnp.float32)
    pos = pos0 + jnp.arange(T)
    outs = []
    for gi, w in enumerate(POOL_WINDOWS):
        xg = af[..., gi * POOL_GROUP:(gi + 1) * POOL_GROUP]
        cs = jnp.cumsum(jnp.pad(xg, ((0, 0), (w, 0), (0, 0))), axis=1)
        wsum = cs[:, w:] - cs[:, :T]
        cnt = jnp.minimum(w, pos + 1).astype(jnp.float32)[None, :, None]
        outs.append(wsum / cnt - xg)
    p = jnp.stack(outs, axis=2).astype(a.dtype)
    p = jnp.einsum('btgc,gcd->btgd', p, w_pool)
    return p.reshape(B, T, POOL_WIDTH) * s_pool


def spatial_gating(z, w_s, b_s, g_v):
    B, T, _ = z.shape
    u = z[..., :GMLP_WIDTH]
    v = rmsnorm(z[..., GMLP_WIDTH:], g_v)
    pad = (-T) % CHUNK
    nc = (T + pad) // CHUNK
    vc = jnp.pad(v, ((0, 0), (0, pad), (0, 0))).reshape(B, nc, CHUNK, N_GMLP_GROUPS, GMLP_GROUP)
    mask = jnp.tril(jnp.ones((CHUNK, CHUNK), dtype=bool))
    w = jnp.where(mask[None], w_s, jnp.zeros_like(w_s))
    s = jnp.einsum('gij,bnjgc->bnigc', w, vc) + jnp.transpose(b_s)[None, None, :, :, None]
    s = s.reshape(B, nc * CHUNK, GMLP_WIDTH)[:, :T]
    return u * s, v


def layer(x, pool_prev, w_in, g_mix, w_pool, s_pool, w_s, b_s, g_v,
          w_pool_out, w_gmlp_out, w_out, g_ffn, w_gate, w_up, w_down):
    T = x.shape[1]
    h = rmsnorm(x, g_mix)
    proj = h @ w_in
    o1 = POOL_WIDTH
    o2 = o1 + 2 * GMLP_WIDTH
    o3 = o2 + D_MODEL
    a = proj[..., :o1]
    z = jax.nn.gelu(proj[..., o1:o2])
    gate_a = jax.nn.sigmoid(proj[..., o2:o3])
    gate_b = jax.nn.sigmoid(proj[..., o3:])
    if pool_prev is None:
        seq_a, pos0 = a, 0
    else:
        seq_a, pos0 = jnp.concatenate([pool_prev.astype(a.dtype), a], axis=1), PAST_LEN - POOL_STATE
    pa = pool_mixer(seq_a, pos0, w_pool, s_pool)[:, -T:]
    new_pool = seq_a[:, -POOL_STATE:]
    sg, v = spatial_gating(z, w_s, b_s, g_v)
    merged = gate_a * (pa @ w_pool_out) + gate_b * (sg @ w_gmlp_out)
    x = x + merged @ w_out
    h2 = rmsnorm(x, g_ffn)
    x = x + (jax.nn.silu(h2 @ w_gate) * (h2 @ w_up)) @ w_down
    return x, new_pool, v


def setup_inputs(seed: int = 0) -> dict:
    key = jax.random.key(seed)
    ks = jax.random.split(key, 20)
    f32 = jnp.float32

    def nrm(k, shape, scale):
        return jax.random.normal(k, shape, f32) * scale

    def gain(k, shape):
        return 1.0 + 0.02 * jax.random.normal(k, shape, f32)

    return {
        "x_prompt": nrm(ks[0], (BATCH, SEQ, D_MODEL), 1.0),
        "x_sample": nrm(ks[1], (DEC_BATCH, DEC_SEQ, D_MODEL), 1.0),
        "state_pool": nrm(ks[2], (DEPTH, DEC_BATCH, POOL_STATE, POOL_WIDTH), 1.0),
        "w_in": nrm(ks[3], (DEPTH, D_MODEL, IN_WIDTH), D_MODEL ** -0.5),
        "g_mix": gain(ks[4], (DEPTH, D_MODEL)),
        "w_pool": nrm(ks[5], (DEPTH, N_POOL_GROUPS, POOL_GROUP, POOL_GROUP), POOL_GROUP ** -0.5),
        "s_pool": gain(ks[6], (DEPTH, POOL_WIDTH)),
        "w_s": nrm(ks[7], (DEPTH, N_GMLP_GROUPS, CHUNK, CHUNK), CHUNK ** -0.5),
        "b_s": gain(ks[8], (DEPTH, N_GMLP_GROUPS, CHUNK)),
        "g_v": gain(ks[9], (DEPTH, GMLP_WIDTH)),
        "w_pool_out": nrm(ks[10], (DEPTH, POOL_WIDTH, D_MODEL), POOL_WIDTH ** -0.5),
        "w_gmlp_out": nrm(ks[11], (DEPTH, GMLP_WIDTH, D_MODEL), GMLP_WIDTH ** -0.5),
        "w_out": nrm(ks[12], (DEPTH, D_MODEL, D_MODEL), D_MODEL ** -0.5),
        "g_ffn": gain(ks[13], (DEPTH, D_MODEL)),
        "w_gate": nrm(ks[14], (DEPTH, D_MODEL, D_FF), D_MODEL ** -0.5),
        "w_up": nrm(ks[15], (DEPTH, D_MODEL, D_FF), D_MODEL ** -0.5),
        "w_down": nrm(ks[16], (DEPTH, D_FF, D_MODEL), D_FF ** -0.5),
        "g_final": gain(ks[17], (D_MODEL,)),
    }


def reference(x_prompt, x_sample, state_pool, w_in, g_mix, w_pool, s_pool, w_s, b_s, g_v,
              w_pool_out, w_gmlp_out, w_out, g_ffn, w_gate, w_up, w_down, g_final):
    xp, xs = x_prompt, x_sample
    pools_p, pools_s, vs_s = [], [], []
    for l in range(DEPTH):
        params = (w_in[l], g_mix[l], w_pool[l], s_pool[l], w_s[l], b_s[l], g_v[l],
                  w_pool_out[l], w_gmlp_out[l], w_out[l], g_ffn[l], w_gate[l], w_up[l], w_down[l])
        xp, pool_p, _ = layer(xp, None, *params)
        xs, pool_s, v_s = layer(xs, state_pool[l], *params)
        pools_p.append(pool_p)
        pools_s.append(pool_s)
        vs_s.append(v_s)
    y_prompt = rmsnorm(xp, g_final)
    y_sample = rmsnorm(xs, g_final)
    new_pool_prompt = jnp.stack(pools_p, axis=0)
    new_pool_sample = jnp.stack(pools_s, axis=0)
    new_v_sample = jnp.stack(vs_s, axis=0)
    return (y_prompt, y_sample, new_pool_prompt, new_pool_sample, new_v_sample)
```

```python
import numpy as np
from contextlib import ExitStack
import concourse.bass as bass
import concourse.mybir as mybir
from concourse.bass_utils import run_bass_kernel_spmd

F32 = mybir.dt.float32
BF16 = mybir.dt.bfloat16
AF = mybir.ActivationFunctionType
ALU = mybir.AluOpType

D = 1024
DFF = 2816
NF = DFF // 128
SEQ = 2048
NSMP = 16
NCORES = 8
TPS = 1024
NSUP = SEQ // TPS
TW = TPS + NSMP
EPS = 1e-6
NSLOT = 4
SLOT_ELEMS = 4096
NBANK = 8
ENGS = ('sp', 'pool', 'act', 'dve', 'pe')


class Prog:
    def __init__(self):
        self.streams = {e: [] for e in ENGS}
        self.seq = {e: 0 for e in ENGS}
        self.waited = {e: {} for e in ENGS}
        self.lw = {}
        self.rd = {}
        self.dcnt = {}

    def _deps(self, eng, reads, writes):
        deps = []
        for r in reads:
            t = self.lw.get(r)
            if t is not None:
                deps.append(t)
        for w in writes:
            t = self.lw.get(w)
            if t is not None:
                deps.append(t)
            deps.extend(self.rd.get(w, ()))
        need = {}
        for (sk, val, peng) in deps:
            if peng == 'pe' and eng == 'pe':
                continue
            if self.waited[eng].get(sk, 0) >= val:
                continue
            if need.get(sk, 0) < val:
                need[sk] = val
        for sk, val in need.items():
            self.waited[eng][sk] = val
        return list(need.items())

    def _commit(self, tok, reads, writes):
        for r in reads:
            self.rd.setdefault(r, []).append(tok)
        for w in writes:
            self.lw[w] = tok
            self.rd[w] = []

    def op(self, eng, fn, reads=(), writes=()):
        waits = self._deps(eng, reads, writes)
        self.seq[eng] += 1
        sk = 'E_' + eng
        tok = (sk, self.seq[eng], eng)
        self._commit(tok, reads, writes)
        self.streams[eng].append((waits, fn, sk, 1))
        return tok

    def dma(self, q, fn, sk, reads=(), writes=()):
        waits = self._deps(q, reads, writes)
        self.dcnt[sk] = self.dcnt.get(sk, 0) + 16
        tok = (sk, self.dcnt[sk], None)
        self._commit(tok, reads, writes)
        self.streams[q].append((waits, fn, sk, 16))
        return tok

    def wait_all(self, eng, sks):
        waits = [(sk, self.dcnt[sk]) for sk in sks if self.dcnt.get(sk, 0) > 0]
        self.streams[eng].append((waits, None, None, 0))

    def sem_keys(self):
        keys = ['E_' + e for e in ('pool', 'act', 'dve', 'pe')]
        keys += sorted(self.dcnt.keys())
        return keys

    def run(self, eng, e, sems):
        for (waits, fn, sk, inc) in self.streams[eng]:
            for (wk, val) in waits:
                e.wait_ge(sems[wk], val)
            if fn is None:
                continue
            ins = fn(e)
            ins.then_inc(sems[sk], inc)


def build_nc():
    nc = bass.Bass("TRN2", target_bir_lowering=False)

    def din(name, shape):
        return nc.dram_tensor(name, list(shape), F32, kind="ExternalInput").ap()

    def dout(name, shape):
        return nc.dram_tensor(name, list(shape), F32, kind="ExternalOutput").ap()

    xp = din("xp", [SEQ, D])
    xs = din("xs", [NSMP, D])
    st = din("st", [NSMP * 15, 512])
    w_in = din("w_in", [D, 3584])
    g_mix = din("g_mix", [1, D])
    w_pool = din("w_pool", [4, 128, 128])
    s_pool = din("s_pool", [1, 512])
    w_s = din("w_s", [4, 128, 128])
    b_s = din("b_s", [1, 512])
    g_v = din("g_v", [1, 512])
    w_po = din("w_po", [512, D])
    w_go = din("w_go", [512, D])
    w_out = din("w_out", [D, D])
    g_ffn = din("g_ffn", [1, D])
    w_gate = din("w_gate", [D, DFF])
    w_up = din("w_up", [D, DFF])
    w_down = din("w_down", [DFF, D])
    g_fin = din("g_fin", [1, D])
    c_ident = din("c_ident", [128, 128])
    c_mask = din("c_mask", [128, 128])
    c_invcnt = din("c_invcnt", [128, 64])
    c_sel = din("c_sel", [120, 128])

    yp = dout("yp", [SEQ, D])
    ys = dout("ys", [NSMP, D])
    npp = dout("npp", [15, 512])
    nps = dout("nps", [NSMP, 15, 512])
    nvs = dout("nvs", [NSMP, 512])

    win_v = w_in.rearrange("(k p) m -> p k m", p=128)
    wpo_v = w_po.rearrange("(k p) m -> p k m", p=128)
    wgo_v = w_go.rearrange("(k p) m -> p k m", p=128)
    wout_v = w_out.rearrange("(k p) m -> p k m", p=128)
    wgate_v = w_gate.rearrange("(k p) m -> p k m", p=128)
    wup_v = w_up.rearrange("(k p) m -> p k m", p=128)
    wdown_v = w_down.rearrange("(f p) m -> p f m", p=128)
    st3 = st.rearrange("(b k) c -> b k c", k=15)

    es = ExitStack()
    with es:
        def sb(name, shape, dt):
            return es.enter_context(nc.sbuf_tensor(name, list(shape), dt))

        def ps(name, shape, dt):
            return es.enter_context(nc.psum_tensor(name, list(shape), dt))

        xbuf = sb("xbuf", [128, 9, D], F32)
        hT = sb("hT", [128, 8, TW], BF16)
        ovl = sb("ovl", [128, 25408], BF16)
        AB = sb("AB", [128, 2, 16 + TPS], F32)
        shr = sb("shr", [128, 2 * (16 + TPS)], F32)
        t1 = shr[:, 0:16 + TPS]
        t2 = shr[:, 16 + TPS:2 * (16 + TPS)]
        ybuf = sb("ybuf", [128, 3, D], F32)
        ABf = AB[:, :, :].rearrange("p a t -> p (a t)")
        xstage = [ABf[:, 0:D], ABf[:, D:2 * D], shr[:, 0:D], shr[:, D:2 * D]]
        AB_RES = [('ab', a_, k_) for a_ in range(2) for k_ in ('h', 0, 512)]
        SHR_RES = ['t1', 't2'] + [('gab', a_, b_) for a_ in range(2) for b_ in range(2)]
        xstage_res = [AB_RES, AB_RES, SHR_RES, SHR_RES]
        ring = sb("ring", [128, NSLOT, SLOT_ELEMS], BF16)
        gmix_rep = sb("gmix_rep", [128, D], F32)
        gffn_rep = sb("gffn_rep", [128, D], F32)
        gfin_rep = sb("gfin_rep", [128, D], F32)
        gv_rep = sb("gv_rep", [128, 512], F32)
        ident_f = sb("ident_f", [128, 128], F32)
        ident_b = sb("ident_b", [128, 128], BF16)
        mask_f = sb("mask_f", [128, 128], F32)
        invcnt = sb("invcnt", [128, 4, 16], F32)
        sel = sb("sel", [120, 2, 4, 16], F32)
        stb = sb("stb", [120, 2, 512], F32)
        spT = sb("spT", [128, 4], F32)
        w00rep = sb("w00rep", [128, 4], F32)
        b0rep = sb("b0rep", [128, 4], F32)
        wpool_b = sb("wpool_b", [128, 4, 128], BF16)
        ws_nat = sb("ws_nat", [128, 4, 128], BF16)
        wsT = sb("wsT", [128, 4, 128], BF16)
        bs_rep = sb("bs_rep", [128, 4, 128], BF16)
        mhalf = sb("mhalf", [128, 1], F32)
        NCOL = 96
        ss = sb("ss", [128, NCOL], F32)
        ms = sb("ms", [128, NCOL], F32)
        rs = sb("rs", [128, NCOL], F32)
        halo = sb("halo", [128, 4, 16], F32)
        aTs = sb("aTs", [128, 4, 16], F32)
        stmp = sb("stmp", [128, 16], F32)
        tmp16 = sb("tmp16", [128, 16], F32)
        hb = sb("hb", [128, 2, D], BF16)
        junk = sb("junk", [128, 2, D], BF16)
        ftmp = sb("ftmp", [128, 2, 512], F32)
        gab = shr[:, 0:2048].rearrange("p (a b c) -> p a b c", a=2, b=2)
        vout = sb("vout", [16, 512], F32)
        nppbuf = sb("nppbuf", [16, 512], F32)
        npsbuf = sb("npsbuf", [16, 512], F32)

        o = 0
        pooled = ovl[:, o:o + 4 * TW].rearrange("p (g t) -> p g t", g=4); o += 4 * TW
        paT = ovl[:, o:o + 4 * TW].rearrange("p (g t) -> p g t", g=4); o += 4 * TW
        vtok = ovl[:, o:o + 9 * 512].rearrange("p (n c) -> p n c", n=9); o += 9 * 512
        sgT = ovl[:, o:o + 4 * TW].rearrange("p (g t) -> p g t", g=4); o += 4 * TW
        mergedT = ovl[:, o:o + 8 * TW].rearrange("p (g t) -> p g t", g=8); o += 8 * TW
        assert o <= 25408
        actT = ovl[:, 0:NF * TW].rearrange("p (f t) -> p f t", f=NF)

        PS = [ps("ps%d" % i, [128, 512], F32) for i in range(NBANK)]
        PSB = [p[:, :].bitcast(BF16).rearrange("p (k t) -> p k t", k=8) for p in PS]

        P = Prog()
        state = {'bank': 0, 'col': 0, 'hb': 0, 'ft': 0, 'ga': 0, 'cp': 0, 'jk': 0, 'yb': 0}

        held = set()

        def bank():
            while True:
                b = state['bank']
                state['bank'] = (b + 1) % NBANK
                if b not in held:
                    return b

        def newcol():
            c = state['col']
            state['col'] += 1
            assert c < NCOL
            return c

        def nxt(key, n):
            v = state[key]
            state[key] = (v + 1) % n
            return v

        def slot_view(slot, off, k, m):
            return ring[:, slot, off:off + k * m].rearrange("p (k m) -> p k m", k=k)

        blocks = []
        carry = {}
        for s in range(NSUP):
            for cb in (2, 0, 1):
                blocks.append([(0, 8, 512, win_v[:, :, cb * 512:(cb + 1) * 512])])
            for q in range(4):
                blocks.append([(0, 8, 256, win_v[:, :, 1536 + q * 256:1536 + (q + 1) * 256]),
                               (2048, 8, 256, win_v[:, :, 2560 + q * 256:2560 + (q + 1) * 256])])
                blocks.append([(0, 4, 256, wpo_v[:, :, q * 256:(q + 1) * 256]),
                               (1024, 4, 256, wgo_v[:, :, q * 256:(q + 1) * 256])])
            for half in range(2):
                blocks.append([(0, 8, 512, wout_v[:, :, half * 512:(half + 1) * 512])])
            for j in range(NF // 2):
                blocks.append([(0, 8, 256, wgate_v[:, :, j * 256:(j + 1) * 256]),
                               (2048, 8, 256, wup_v[:, :, j * 256:(j + 1) * 256])])
            for (f0, nf) in ((0, 6), (6, 8), (14, 8)):
                for half in range(2):
                    blocks.append([(0, nf, 512, wdown_v[:, f0:f0 + nf, half * 512:(half + 1) * 512])])
        rstate = {'issued': 0, 'next': 0}

        def ring_issue(j):
            slot = j % NSLOT
            for pi, (off, k, m, src) in enumerate(blocks[j]):
                dst = slot_view(slot, off, k, m)
                P.dma('pool',
                      (lambda e, dst=dst, src=src: e.dma_start(out=dst, in_=src)),
                      'w%d' % slot, reads=(), writes=[('ring', slot, pi)])
            fin = ('w%d' % slot, P.dcnt['w%d' % slot], None)
            for pi in range(len(blocks[j])):
                P.lw[('ring', slot, pi)] = fin
            rstate['issued'] = j + 1

        def ring_next():
            j = rstate['next']
            rstate['next'] += 1
            assert j < rstate['issued']
            slot = j % NSLOT
            return j, slot, [('ring', slot, 0), ('ring', slot, 1)]

        def ring_release(j):
            if j + NSLOT < len(blocks):
                assert rstate['issued'] == j + NSLOT
                ring_issue(j + NSLOT)

        cres = []

        def cdma(q, dst, src, res, noncontig=False, own=None):
            def fn(e, dst=dst, src=src):
                if noncontig:
                    with nc.allow_non_contiguous_dma(reason="tiny constant"):
                        return e.dma_start(out=dst, in_=src)
                return e.dma_start(out=dst, in_=src)
            if own is not None:
                P.dma(q, fn, own, reads=(), writes=[res])
                return
            sk = {'sp': 'cst', 'act': 'csta', 'pool': 'cstp'}[q]
            P.dma(q, fn, sk, reads=(), writes=[res])
            cres.append((sk, res))

        def xload(sidx, slot):
            R = 128 if slot < 8 else NSMP
            src = xp[sidx * TPS + slot * 128:sidx * TPS + (slot + 1) * 128, :] if slot < 8 else xs[:, :]
            P.dma('sp', (lambda e: e.dma_start(out=xbuf[:R, slot, :], in_=src)),
                  'x%d' % slot, reads=(), writes=[('x', slot, 0), ('x', slot, 1)])

        cdma('act', gmix_rep[:, :], g_mix[0:1, :].partition_broadcast(128), 'gmix', own='c_gmix')
        for slot in range(4):
            xload(0, slot)
        cdma('pool', ident_b[:, :], c_ident[:, :], 'ident_b', own='c_idb')
        for slot in range(4, 9):
            xload(0, slot)
        cdma('act', ident_f[:, :], c_ident[:, :], 'ident_f')
        cdma('act', mask_f[:, :], c_mask[:, :], 'mask_f')
        cdma('act', invcnt[:, :, :], c_invcnt.rearrange("p (g t) -> p g t", g=4), 'invcnt')
        cdma('act', sel[:, :, :, :], c_sel.rearrange("p (h g t) -> p h g t", h=2, g=4), 'sel')
        cdma('act', stb[:, 0, :], st[0:120, :], 'stb')
        cdma('act', stb[:, 1, :], st[120:240, :], 'stb1')
        cdma('sp', spT[:, :], s_pool.rearrange("o (g p) -> p (o g)", p=128), 'spT', True)
        cdma('sp', gv_rep[:, :], g_v[0:1, :].partition_broadcast(128), 'gv')
        cdma('sp', w00rep[:, :], w_s[:, 0, 0:1].rearrange("g o -> o g").partition_broadcast(128), 'w00', True)
        cdma('sp', b0rep[:, :], b_s.rearrange("o (g i) -> o g i", g=4)[:, :, 0].partition_broadcast(128), 'b0', True)
        def late_pool_consts():
            cdma('pool', wpool_b[:, :, :], w_pool.rearrange("g c d -> c g d"), 'wpool', own='c_wpool')
            cdma('pool', ws_nat[:, :, :], w_s.rearrange("g i j -> i g j"), 'ws_nat', own='c_wsnat')
            cdma('pool', bs_rep[:, :, :].rearrange("p g i -> p (g i)"), b_s[0:1, :].partition_broadcast(128),
                 'bs_rep', own='c_bsrep')
            for j in range(1, NSLOT):
                ring_issue(j)
        cdma('sp', gffn_rep[:, :], g_ffn[0:1, :].partition_broadcast(128), 'gffn')
        cdma('sp', gfin_rep[:, :], g_fin[0:1, :].partition_broadcast(128), 'gfin')
        for (sk, res) in cres:
            P.lw[res] = (sk, P.dcnt[sk], None)

        ring_issue(0)

        P.dma('sp', lambda e: e.dma_start(out=nps[:, 0:14, :], in_=st3[:, 1:15, :]), 'omisc')

        P.op('dve', lambda e: e.memset(ss[:, :], 0.0), writes=[('ss', c) for c in range(NCOL)])
        P.op('dve', lambda e: e.memset(mhalf[:, :], -0.5), writes=['mhalf'])

        def mm_group(out, lhs_fn, rhs_fn, n):
            def fn(e):
                ins = None
                for i in range(n):
                    ins = e.matmul(out=out, lhsT=lhs_fn(i), rhs=rhs_fn(i), start=(i == 0), stop=(i == n - 1))
                return ins
            return fn

        def rstd_stages(src_ap, R, width, src_res):
            k = newcol()

            def st_sq():
                ji = nxt('jk', 2)
                P.op('act', lambda e: e.activation(out=junk[:R, ji, 0:width], in_=src_ap, func=AF.Square,
                                                   accum_out=ss[:R, k:k + 1]),
                     reads=list(src_res), writes=[('ss', k), ('junk', ji)])

            def st_rs():
                P.op('dve', lambda e: e.tensor_scalar(out=ms[:R, k:k + 1], in0=ss[:R, k:k + 1],
                                                      scalar1=1.0 / width, scalar2=EPS,
                                                      op0=ALU.mult, op1=ALU.add),
                     reads=[('ss', k)], writes=[('ms', k)])
                P.op('pool', lambda e: e.tensor_tensor(out=rs[:R, k:k + 1], in0=ms[:R, k:k + 1],
                                                       in1=mhalf[:R, 0:1], op=ALU.pow),
                     reads=[('ms', k), 'mhalf'], writes=[('rs', k)])
            return k, st_sq, st_rs

        def rstd_chain(src_ap, R, width, src_res):
            k, a, b = rstd_stages(src_ap, R, width, src_res)
            a()
            b()
            return k

        def norm_stages(slot, R, col0, grep, gres, copy_eng=None, src=None, src_res=None):
            xres = [('x', slot, 0), ('x', slot, 1)] if src_res is None else list(src_res)
            xsrc = xbuf[:R, slot, :] if src is None else src
            k, st_sq, st_rs = rstd_stages(xsrc, R, D, xres)
            box = {}

            def st_h():
                hi = nxt('hb', 2)
                box['hi'] = hi
                P.op('dve', lambda e: e.scalar_tensor_tensor(out=hb[:R, hi, :], in0=xsrc,
                                                             scalar=rs[:R, k:k + 1], in1=grep[:R, :],
                                                             op0=ALU.mult, op1=ALU.mult),
                     reads=xres + [('rs', k), gres], writes=[('hb', hi)])

            def st_tr():
                hi = box['hi']
                b = bank()

                def trf(e):
                    ins = None
                    for kc in range(8):
                        ins = e.transpose(out=PSB[b][:, kc, :R], in_=hb[:R, hi, kc * 128:(kc + 1) * 128],
                                          identity=ident_b[:R, :R])
                    return ins
                P.op('pe', trf, reads=[('hb', hi), 'ident_b'], writes=[('ps', b)])
                src = PSB[b][:, :, :R]
                dst = hT[:, :, col0:col0 + R]
                ce = copy_eng if copy_eng is not None else ('act' if nxt('cp', 2) == 0 else 'dve')
                if ce == 'act':
                    P.op('act', lambda e: e.activation(out=dst, in_=src, func=AF.Copy),
                         writes=[('hT', slot, 0), ('hT', slot, 1), ('ps', b)])
                else:
                    P.op('dve', lambda e: e.tensor_copy(out=dst, in_=src),
                         writes=[('hT', slot, 0), ('hT', slot, 1), ('ps', b)])
            return [st_sq, st_rs, st_h, st_tr]

        def pipeline(steps, lags):
            n = len(steps)
            for i in range(n + max(lags)):
                for si in range(len(lags)):
                    t = i - lags[si]
                    if 0 <= t < n:
                        steps[t][si]()

        def hT_res(tl):
            r = []
            for t in tl:
                r += [('hT', t, 0), ('hT', t, 1)]
            return r

        def ws_prep():
            wb_ = bank()

            def ws_tr(e):
                ins = None
                for g in range(4):
                    ins = e.transpose(out=PSB[wb_][:, g, :], in_=ws_nat[:, g, :], identity=ident_b[:, :])
                return ins
            P.op('pe', ws_tr, reads=['ws_nat', 'ident_b'], writes=[('ps', wb_)])
            for g in range(4):
                P.op('dve', lambda e, g=g: e.tensor_tensor(out=wsT[:, g, :], in0=PSB[wb_][:, g, :], in1=mask_f[:, :],
                                                           op=ALU.mult),
                     reads=['mask_f'], writes=[('wsT', g), ('ps', wb_)])

        for s in range(NSUP):
            tts = [(i, 128, i * 128) for i in range(8)]
            nts = [(0, 512, [0, 1, 2, 3]), (512, 512, [4, 5, 6, 7])]
            if s == 0:
                tts.append((8, NSMP, TPS))
                nts.append((TPS, NSMP, [8]))
            t0 = s * TPS

            cj, cslot, cres_ = ring_next()
            wv = slot_view(cslot, 0, 8, 512)

            def c_tile(slot, R, col0, wv=wv, cres_=cres_):
                b = bank()
                P.op('pe', mm_group(PS[b][:R, :],
                                    lambda i, col0=col0, R=R: hT[:, i, col0:col0 + R],
                                    lambda i, wv=wv: wv[:, i, :], 8),
                     reads=cres_ + hT_res([slot]), writes=[('ps', b)])
                fi = nxt('ft', 2)
                P.op('act', lambda e, b=b, fi=fi, R=R: e.activation(out=ftmp[:R, fi, :], in_=PS[b][:R, :],
                                                                   func=AF.Gelu_apprx_tanh),
                     reads=[], writes=[('ftmp', fi), ('ps', b)])
                k = rstd_chain(ftmp[:R, fi, :], R, 512, [('ftmp', fi)])
                if slot < 8:
                    P.op('dve', lambda e, fi=fi, R=R, k=k, slot=slot: e.scalar_tensor_tensor(
                        out=vtok[:R, slot, :], in0=ftmp[:R, fi, :], scalar=rs[:R, k:k + 1], in1=gv_rep[:R, :],
                        op0=ALU.mult, op1=ALU.mult),
                         reads=[('ftmp', fi), ('rs', k), 'gv'], writes=[('vtok', slot)])
                else:
                    P.op('dve', lambda e, fi=fi, R=R, k=k: e.scalar_tensor_tensor(
                        out=vout[:R, :], in0=ftmp[:R, fi, :], scalar=rs[:R, k:k + 1], in1=gv_rep[:R, :],
                        op0=ALU.mult, op1=ALU.mult),
                         reads=[('ftmp', fi), ('rs', k), 'gv'], writes=['vout'])
                    P.op('dve', lambda e, R=R, slot=slot: e.tensor_copy(out=vtok[:R, slot, :], in_=vout[:R, :]),
                         reads=['vout'], writes=[('vtok', slot)])
                    P.dma('sp', lambda e: e.dma_start(out=nvs[:, :], in_=vout[:NSMP, :]), 'omisc', reads=['vout'])


            if s == 0:
                pipeline([norm_stages(slot, R, col0, gmix_rep, 'gmix') for (slot, R, col0) in tts],
                         [0, 1, 2, 3])
                late_pool_consts()
            if s == 0:
                for (slot, R, col0) in tts:
                    c_tile(slot, R, col0)
                ws_prep()
            else:
                nsd = carry['nsd45']
                for c_ in (0, 1):
                    c_tile(*tts[c_])
                for t_ in (6, 7):
                    nsd[t_][3]()
                for (slot, R, col0) in tts[2:]:
                    c_tile(slot, R, col0)
            ring_release(cj)

            bj, bslot, bres = ring_next()
            wv = slot_view(bslot, 0, 8, 512)
            b4_list = []
            deferred_ops = []
            npp_pending = []
            for g in range(4):
                w = 2 ** (g + 1)
                ai = g % 2
                abres = [('ab', ai, 'h'), ('ab', ai, 0), ('ab', ai, 512)]
                if s == 0:
                    P.op('dve', lambda e, ai=ai: e.memset(AB[:, ai, 0:16], 0.0), writes=[('ab', ai, 'h')])
                else:
                    P.op('dve', lambda e, ai=ai, g=g: e.tensor_copy(out=AB[:, ai, 0:16], in_=halo[:, g, :]),
                         reads=[('halo', g)], writes=[('ab', ai, 'h')])
                for (c0, W, tl) in nts:
                    b = bank()
                    P.op('pe', mm_group(PS[b][:, :W],
                                        lambda i, g=g, wv=wv: wv[:, i, g * 128:(g + 1) * 128],
                                        lambda i, c0=c0, W=W: hT[:, i, c0:c0 + W], 8),
                         reads=bres + hT_res(tl), writes=[('ps', b)])
                    if c0 < TPS:
                        P.op('act', lambda e, b=b, ai=ai, c0=c0, W=W: e.activation(
                            out=AB[:, ai, 16 + c0:16 + c0 + W], in_=PS[b][:, :W], func=AF.Copy),
                             reads=[], writes=[('ab', ai, c0), ('ps', b)])
                    else:
                        P.op('act', lambda e, b=b, g=g, W=W: e.activation(
                            out=aTs[:, g, :], in_=PS[b][:, :W], func=AF.Copy),
                             reads=[], writes=[('aTs', g), ('ps', b)])
                while npp_pending:
                    npp_pending.pop(0)()

                def pool_chain(g=g, w=w, ai=ai, abres=abres, defer=False):
                    def OP(*a_, **k_):
                        if defer:
                            deferred_ops.append(lambda: P.op(*a_, **k_))
                        else:
                            P.op(*a_, **k_)
                    L = TPS
                    A = AB[:, ai, :]
                    OP('dve', lambda e, A=A: e.tensor_tensor(out=t1[:, 1:16 + L], in0=A[:, 1:16 + L],
                                                               in1=A[:, 0:15 + L], op=ALU.add),
                         reads=abres, writes=['t1'])
                    if g >= 1:
                        OP('dve', lambda e: e.tensor_tensor(out=t2[:, 3:16 + L], in0=t1[:, 3:16 + L],
                                                              in1=t1[:, 1:14 + L], op=ALU.add),
                             reads=['t1'], writes=['t2'])
                    if g >= 2:
                        OP('dve', lambda e: e.tensor_tensor(out=t1[:, 7:16 + L], in0=t2[:, 7:16 + L],
                                                              in1=t2[:, 3:12 + L], op=ALU.add),
                             reads=['t2'], writes=['t1'])
                    if g >= 3:
                        OP('dve', lambda e: e.tensor_tensor(out=t2[:, 15:16 + L], in0=t1[:, 15:16 + L],
                                                              in1=t1[:, 7:8 + L], op=ALU.add),
                             reads=['t1'], writes=['t2'])
                    wsum = t1 if g in (0, 2) else t2
                    wres = 't1' if g in (0, 2) else 't2'
                    OP('dve', lambda e, wsum=wsum, A=A, g=g, w=w: e.scalar_tensor_tensor(
                        out=pooled[:, g, 0:L], in0=wsum[:, 16:16 + L], scalar=1.0 / w, in1=A[:, 16:16 + L],
                        op0=ALU.mult, op1=ALU.subtract),
                         reads=[wres] + abres, writes=[('pooled', g)])
                    if s == 0:
                        OP('dve', lambda e, wsum=wsum, g=g: e.tensor_tensor(
                            out=tmp16[:, :], in0=wsum[:, 16:32], in1=invcnt[:, g, :], op=ALU.mult),
                             reads=[wres, 'invcnt'], writes=['tmp16'])
                        OP('dve', lambda e, A=A, g=g: e.tensor_tensor(
                            out=pooled[:, g, 0:16], in0=tmp16[:, :], in1=A[:, 16:32], op=ALU.subtract),
                             reads=['tmp16'] + abres, writes=[('pooled', g)])
                    if s < NSUP - 1:
                        OP('dve', lambda e, A=A, g=g: e.tensor_copy(out=halo[:, g, :], in_=A[:, L:L + 16]),
                             reads=abres, writes=[('halo', g)])
                    else:
                        def npp_ops(A=A, g=g, abres=abres):
                            b = bank()
                            P.op('pe', lambda e: e.transpose(out=PS[b][:15, 0:128], in_=A[:, L + 1:L + 16],
                                                             identity=ident_f[:, :]),
                                 reads=abres + ['ident_f'], writes=[('ps', b)])
                            P.op('act', lambda e: e.activation(out=nppbuf[:15, g * 128:(g + 1) * 128],
                                                               in_=PS[b][:15, 0:128], func=AF.Copy),
                                 reads=[], writes=[('nppbuf', g), ('ps', b)])
                        if defer:
                            deferred_ops.append(npp_ops)
                        else:
                            npp_pending.append(npp_ops)
                    if s == 0:
                        b = bank()
                        OP('pe', mm_group(PS[b][:, 0:NSMP],
                                            lambda i, g=g: stb[:, i, g * 128:(g + 1) * 128],
                                            lambda i, g=g: sel[:, i, g, :], 2),
                             reads=['stb', 'stb1', 'sel'], writes=[('ps', b)])
                        OP('dve', lambda e, b=b, g=g, w=w: e.scalar_tensor_tensor(
                            out=pooled[:, g, TPS:TW], in0=aTs[:, g, :], scalar=(1.0 / w - 1.0),
                            in1=PS[b][:, 0:NSMP], op0=ALU.mult, op1=ALU.add),
                             reads=[('aTs', g)], writes=[('pooled', g, 's'), ('ps', b)])

                pool_chain(defer=(g == 3))

                def b4(g=g):
                    for (c0, W, tl) in nts:
                        b = bank()
                        pres = [('pooled', g)] if c0 < TPS else [('pooled', g, 's')]
                        P.op('pe', lambda e, b=b, g=g, c0=c0, W=W: e.matmul(
                            out=PS[b][:, :W], lhsT=wpool_b[:, g, :], rhs=pooled[:, g, c0:c0 + W],
                            start=True, stop=True),
                             reads=pres + ['wpool'], writes=[('ps', b)])
                        P.op('act', lambda e, b=b, g=g, c0=c0, W=W: e.activation(
                            out=paT[:, g, c0:c0 + W], in_=PS[b][:, :W], func=AF.Copy, scale=spT[:, g:g + 1]),
                             reads=['spT'], writes=[('paT', g, c0), ('ps', b)])
                b4_list.append(b4)
            while npp_pending:
                npp_pending.pop(0)()
            ring_release(bj)
            if s == 0:
                b = bank()

                def atr(e, b=b):
                    ins = None
                    for g in range(4):
                        ins = e.transpose(out=PS[b][:NSMP, g * 128:(g + 1) * 128], in_=aTs[:, g, :],
                                          identity=ident_f[:, :])
                    return ins
                P.op('pe', atr, reads=[('aTs', g) for g in range(4)] + ['ident_f'], writes=[('ps', b)])
                P.op('act', lambda e, b=b: e.activation(out=npsbuf[:NSMP, :], in_=PS[b][:NSMP, :], func=AF.Copy),
                     reads=[], writes=['npsbuf', ('ps', b)])
                P.dma('sp', lambda e: e.dma_start(out=nps[:, 14, :], in_=npsbuf[:NSMP, :]), 'omisc',
                      reads=['npsbuf'])

            dj, dslot, dres = ring_next()
            wv = slot_view(dslot, 0, 8, 512)
            for g in range(4):
                for (c0, W, tl) in nts:
                    bU = bank()
                    P.op('pe', mm_group(PS[bU][:, :W],
                                        lambda i, g=g, wv=wv: wv[:, i, g * 128:(g + 1) * 128],
                                        lambda i, c0=c0, W=W: hT[:, i, c0:c0 + W], 8),
                         reads=dres + hT_res(tl), writes=[('ps', bU)])
                    fi = nxt('ft', 2)
                    P.op('act', lambda e, bU=bU, fi=fi, W=W: e.activation(out=ftmp[:, fi, :W], in_=PS[bU][:, :W],
                                                                         func=AF.Gelu_apprx_tanh),
                         reads=[], writes=[('ftmp', fi), ('ps', bU)])
                    if c0 < TPS:
                        bS = bank()

                        def spat(e, bS=bS, g=g, tl=tl):
                            ins = None
                            for j, tslot in enumerate(tl):
                                ins = e.matmul(out=PS[bS][:, j * 128:(j + 1) * 128],
                                               lhsT=vtok[:, tslot, g * 128:(g + 1) * 128], rhs=wsT[:, g, :],
                                               start=True, stop=True)
                            return ins
                        P.op('pe', spat, reads=[('vtok', t) for t in tl] + [('wsT', g)],
                             writes=[('ps', bS)])
                        P.op('dve', lambda e, bS=bS, g=g: e.tensor_tensor(
                            out=PS[bS][:, :].rearrange("p (j i) -> p j i", j=4),
                            in0=PS[bS][:, :].rearrange("p (j i) -> p j i", j=4),
                            in1=bs_rep[:, g:g + 1, :].to_broadcast([128, 4, 128]), op=ALU.add),
                             reads=['bs_rep'], writes=[('ps', bS)])
                        P.op('dve', lambda e, bS=bS, fi=fi, g=g, c0=c0, W=W: e.tensor_tensor(
                            out=sgT[:, g, c0:c0 + W], in0=ftmp[:, fi, :W], in1=PS[bS][:, :W], op=ALU.mult),
                             reads=[('ftmp', fi)], writes=[('sgT', g, c0), ('ps', bS)])
                    else:
                        bS = bank()
                        P.op('pe', lambda e, g=g, bS=bS: e.transpose(out=PSB[bS][:, 0, :NSMP],
                                                                     in_=vtok[:NSMP, 8, g * 128:(g + 1) * 128],
                                                                     identity=ident_b[:NSMP, :NSMP]),
                             reads=[('vtok', 8), 'ident_b'], writes=[('ps', bS)])
                        P.op('dve', lambda e, g=g, bS=bS: e.tensor_scalar(
                            out=stmp[:, :], in0=PSB[bS][:, 0, :NSMP], scalar1=w00rep[:, g:g + 1],
                            scalar2=b0rep[:, g:g + 1], op0=ALU.mult, op1=ALU.add),
                             reads=['w00', 'b0'], writes=['stmp', ('ps', bS)])
                        P.op('dve', lambda e, fi=fi, g=g, c0=c0, W=W: e.tensor_tensor(
                            out=sgT[:, g, c0:c0 + W], in0=stmp[:, :W], in1=ftmp[:, fi, :W], op=ALU.mult),
                             reads=['stmp', ('ftmp', fi)], writes=[('sgT', g, c0)])
                    if deferred_ops:
                        deferred_ops.pop(0)()
                if g < 3:
                    b4_list[g]()
            while deferred_ops:
                deferred_ops.pop(0)()
            b4_list[3]()
            ring_release(dj)
            if s == NSUP - 1:
                P.dma('sp', lambda e: e.dma_start(out=npp[:, :], in_=nppbuf[:15, :]), 'omisc',
                      reads=[('nppbuf', g) for g in range(4)])

            for q in range(4):
                ej, eslot, eres = ring_next()
                fj, fslot, fres = ring_next()
                gaw = slot_view(eslot, 0, 8, 256)
                gbw = slot_view(eslot, 2048, 8, 256)
                pow_ = slot_view(fslot, 0, 4, 256)
                gow = slot_view(fslot, 1024, 4, 256)
                for dl in range(2):
                    d = 2 * q + dl
                    for (c0, W, tl) in nts:
                        gi = nxt('ga', 2)
                        bA = bank(); bB = bank(); bP = bank(); bQ = bank()
                        P.op('pe', mm_group(PS[bA][:, :W],
                                            lambda i, dl=dl, gaw=gaw: gaw[:, i, dl * 128:(dl + 1) * 128],
                                            lambda i, c0=c0, W=W: hT[:, i, c0:c0 + W], 8),
                             reads=eres + hT_res(tl), writes=[('ps', bA)])
                        P.op('act', lambda e, bA=bA, gi=gi, W=W: e.activation(
                            out=gab[:, gi, 0, :W], in_=PS[bA][:, :W], func=AF.Sigmoid),
                             reads=[], writes=[('gab', gi, 0), ('ps', bA)])
                        P.op('pe', mm_group(PS[bB][:, :W],
                                            lambda i, dl=dl, gbw=gbw: gbw[:, i, dl * 128:(dl + 1) * 128],
                                            lambda i, c0=c0, W=W: hT[:, i, c0:c0 + W], 8),
                             reads=eres + hT_res(tl), writes=[('ps', bB)])
                        P.op('act', lambda e, bB=bB, gi=gi, W=W: e.activation(
                            out=gab[:, gi, 1, :W], in_=PS[bB][:, :W], func=AF.Sigmoid),
                             reads=[], writes=[('gab', gi, 1), ('ps', bB)])
                        P.op('pe', mm_group(PS[bP][:, :W],
                                            lambda i, dl=dl, pow_=pow_: pow_[:, i, dl * 128:(dl + 1) * 128],
                                            lambda i, c0=c0, W=W: paT[:, i, c0:c0 + W], 4),
                             reads=fres + [('paT', gg, c0) for gg in range(4)], writes=[('ps', bP)])
                        P.op('dve', lambda e, bP=bP, gi=gi, W=W: e.tensor_tensor(
                            out=gab[:, gi, 0, :W], in0=gab[:, gi, 0, :W], in1=PS[bP][:, :W], op=ALU.mult),
                             reads=[('gab', gi, 0)], writes=[('gab', gi, 0), ('ps', bP)])
                        P.op('pe', mm_group(PS[bQ][:, :W],
                                            lambda i, dl=dl, gow=gow: gow[:, i, dl * 128:(dl + 1) * 128],
                                            lambda i, c0=c0, W=W: sgT[:, i, c0:c0 + W], 4),
                             reads=fres + [('sgT', gg, c0) for gg in range(4)], writes=[('ps', bQ)])
                        P.op('dve', lambda e, bQ=bQ, gi=gi, W=W: e.tensor_tensor(
                            out=gab[:, gi, 1, :W], in0=gab[:, gi, 1, :W], in1=PS[bQ][:, :W], op=ALU.mult),
                             reads=[('gab', gi, 1)], writes=[('gab', gi, 1), ('ps', bQ)])
                        P.op('dve', lambda e, gi=gi, d=d, c0=c0, W=W: e.tensor_tensor(
                            out=mergedT[:, d, c0:c0 + W], in0=gab[:, gi, 0, :W], in1=gab[:, gi, 1, :W], op=ALU.add),
                             reads=[('gab', gi, 0), ('gab', gi, 1)], writes=[('mergedT', d, c0)])
                ring_release(ej)
                ring_release(fj)

            def h_mm(gw, uw, hres, fl, c0, W, tl):
                bG = bank(); bU = bank()
                P.op('pe', mm_group(PS[bG][:, :W],
                                    lambda i: gw[:, i, fl * 128:(fl + 1) * 128],
                                    lambda i: hT[:, i, c0:c0 + W], 8),
                     reads=hres + hT_res(tl), writes=[('ps', bG)])
                P.op('pe', mm_group(PS[bU][:, :W],
                                    lambda i: uw[:, i, fl * 128:(fl + 1) * 128],
                                    lambda i: hT[:, i, c0:c0 + W], 8),
                     reads=hres + hT_res(tl), writes=[('ps', bU)])
                return bG, bU

            def h_evac(f, c0, W, bG, bU):
                fi = nxt('ft', 2)
                P.op('act', lambda e: e.activation(out=ftmp[:, fi, :W], in_=PS[bG][:, :W], func=AF.Silu),
                     reads=[], writes=[('ftmp', fi), ('ps', bG)])
                P.op('dve', lambda e: e.tensor_tensor(
                    out=actT[:, f, c0:c0 + W], in0=ftmp[:, fi, :W], in1=PS[bU][:, :W], op=ALU.mult),
                     reads=[('ftmp', fi)], writes=[('actT', f, c0), ('ps', bU)])

            h_first = {}

            wj0, wslot0, wres0 = ring_next()
            wj1, wslot1, wres1 = ring_next()
            wvs = [slot_view(wslot0, 0, 8, 512), slot_view(wslot1, 0, 8, 512)]
            wress = [wres0, wres1]
            hj0, hslot0, hres0 = ring_next()
            h_first = {'j': hj0, 'gw': slot_view(hslot0, 0, 8, 256),
                       'uw': slot_view(hslot0, 2048, 8, 256), 'res': hres0}
            steps = []
            for (slot, R, col0) in tts:
                def st_mm(slot=slot, R=R, col0=col0):
                    c0n = 0 if slot < 4 else (512 if slot < 8 else TPS)
                    for half in range(2):
                        b = bank()
                        P.op('pe', mm_group(PS[b][:R, :],
                                            lambda i: mergedT[:, i, col0:col0 + R],
                                            lambda i, wv_=wvs[half]: wv_[:, i, :], 8),
                             reads=wress[half] + [('mergedT', dd, c0n) for dd in range(8)], writes=[('ps', b)])
                        P.op('dve', lambda e, b=b, half=half: e.tensor_tensor(
                            out=xbuf[:R, slot, half * 512:(half + 1) * 512],
                            in0=xbuf[:R, slot, half * 512:(half + 1) * 512], in1=PS[b][:R, :], op=ALU.add),
                             reads=[('x', slot, half)], writes=[('x', slot, half), ('ps', b)])
                ns = norm_stages(slot, R, col0, gffn_rep, 'gffn', copy_eng='act')

                def st0(st_mm=st_mm, sq=ns[0]):
                    st_mm()
                    sq()
                steps.append([st0, ns[1], ns[2], ns[3]])
            lags = [0, 1, 2, 3]
            n_ = len(steps)
            for i_ in range(n_ + 3):
                for si_ in reversed(range(len(lags))):
                    t_ = i_ - lags[si_]
                    if 0 <= t_ < n_:
                        steps[t_][si_]()
                if i_ == n_:
                    c0_, W_, tl_ = nts[0]
                    bG_, bU_ = h_mm(h_first['gw'], h_first['uw'], h_first['res'], 0, c0_, W_, tl_)
                    held.add(bG_); held.add(bU_)
            held.discard(bG_); held.discard(bU_)
            h_evac(0, c0_, W_, bG_, bU_)
            ring_release(wj0)
            ring_release(wj1)

            for j in range(NF // 2):
                if j == 0:
                    hj, gw, uw, hres = h_first['j'], h_first['gw'], h_first['uw'], h_first['res']
                else:
                    hj, hslot, hres = ring_next()
                    gw = slot_view(hslot, 0, 8, 256)
                    uw = slot_view(hslot, 2048, 8, 256)
                for fl in range(2):
                    f = 2 * j + fl
                    for ni, (c0, W, tl) in enumerate(nts):
                        if j == 0 and fl == 0 and ni == 0:
                            continue
                        bG, bU = h_mm(gw, uw, hres, fl, c0, W, tl)
                        h_evac(f, c0, W, bG, bU)
                ring_release(hj)

            fgroups = ((0, 6), (6, 8), (14, 8))
            prefetch = (s + 1 < NSUP)
            for bk, (f0, nf) in enumerate(fgroups[:2]):
                for half in range(2):
                    ij, islot, ires = ring_next()
                    wv = slot_view(islot, 0, 8, 512)
                    pidx = bk * 2 + half
                    if prefetch and pidx == 2:
                        for i_ in range(4):
                            src_ = xp[(s + 1) * TPS + i_ * 128:(s + 1) * TPS + (i_ + 1) * 128, :]
                            P.dma('sp', (lambda e, i_=i_, src_=src_: e.dma_start(out=xstage[i_], in_=src_)),
                                  'xs%d' % i_, reads=(), writes=list(xstage_res[i_]) + [('xstage', i_)])
                    steps = []
                    for (slot, R, col0) in tts:
                        def st_mm(slot=slot, R=R, col0=col0, wv=wv, ires=ires, half=half, f0=f0, nf=nf):
                            c0n = 0 if slot < 4 else (512 if slot < 8 else TPS)
                            b = bank()
                            P.op('pe', mm_group(PS[b][:R, :],
                                                lambda i: actT[:, f0 + i, col0:col0 + R],
                                                lambda i: wv[:, i, :], nf),
                                 reads=ires + [('actT', f0 + i, c0n) for i in range(nf)], writes=[('ps', b)])
                            P.op('dve', lambda e: e.tensor_tensor(
                                out=xbuf[:R, slot, half * 512:(half + 1) * 512],
                                in0=xbuf[:R, slot, half * 512:(half + 1) * 512], in1=PS[b][:R, :], op=ALU.add),
                                 reads=[('x', slot, half)], writes=[('x', slot, half), ('ps', b)])
                        if prefetch and pidx == 3:
                            if slot < 4:
                                ns_ = norm_stages(slot, 128, slot * 128, gmix_rep, 'gmix',
                                                  src=xstage[slot],
                                                  src_res=list(xstage_res[slot]) + [('xstage', slot)])
                            else:
                                ns_ = [(lambda: None)] * 4
                            steps.append([st_mm] + ns_)
                        else:
                            steps.append([st_mm])
                    if prefetch and pidx == 3:
                        pipeline(steps, [0, 0, 1, 2, 3])
                    else:
                        pipeline(steps, [0])
                    ring_release(ij)

            f0, nf = fgroups[2]
            ij0, islot0, ires0 = ring_next()
            ij1, islot1, ires1 = ring_next()
            wvs_ = [slot_view(islot0, 0, 8, 512), slot_view(islot1, 0, 8, 512)]
            iress = [ires0, ires1]
            steps = []
            for (slot, R, col0) in (tts[4:] + tts[:4]):
                def st_mm(slot=slot, R=R, col0=col0, f0=f0, nf=nf, wvs_=wvs_, iress=iress):
                    c0n = 0 if slot < 4 else (512 if slot < 8 else TPS)
                    for half in range(2):
                        b = bank()
                        P.op('pe', mm_group(PS[b][:R, :],
                                            lambda i: actT[:, f0 + i, col0:col0 + R],
                                            lambda i, wv_=wvs_[half]: wv_[:, i, :], nf),
                             reads=iress[half] + [('actT', f0 + i, c0n) for i in range(nf)],
                             writes=[('ps', b)])
                        P.op('dve', lambda e, b=b, half=half: e.tensor_tensor(
                            out=xbuf[:R, slot, half * 512:(half + 1) * 512],
                            in0=xbuf[:R, slot, half * 512:(half + 1) * 512], in1=PS[b][:R, :], op=ALU.add),
                             reads=[('x', slot, half)], writes=[('x', slot, half), ('ps', b)])
                xres = [('x', slot, 0), ('x', slot, 1)]
                k, st_sq, st_rs = rstd_stages(xbuf[:R, slot, :], R, D, xres)

                def st0(st_mm=st_mm, st_sq=st_sq):
                    st_mm()
                    st_sq()

                def st_y(slot=slot, R=R, k=k, xres=xres):
                    yi = nxt('yb', 5)
                    if yi < 3:
                        ysl = ybuf[:R, yi, :]
                        yres = [('ybuf', yi)]
                    elif yi == 3:
                        ysl = shr[:R, 0:D]
                        yres = ['t1', ('gab', 0, 0), ('gab', 0, 1), ('xstage', 2)]
                    else:
                        ysl = shr[:R, 1056:1056 + D]
                        yres = ['t2', ('gab', 1, 0), ('gab', 1, 1), ('xstage', 3)]
                    P.op('dve', lambda e: e.scalar_tensor_tensor(
                        out=ysl, in0=xbuf[:R, slot, :], scalar=rs[:R, k:k + 1],
                        in1=gfin_rep[:R, :], op0=ALU.mult, op1=ALU.mult),
                         reads=xres + [('rs', k), 'gfin'], writes=yres)
                    dst = yp[t0 + slot * 128:t0 + (slot + 1) * 128, :] if slot < 8 else ys[:, :]
                    P.dma('sp', (lambda e: e.dma_start(out=dst, in_=ysl)),
                          'o%d' % yi, reads=yres)
                    if s + 1 < NSUP and slot < 8:
                        xload(s + 1, slot)
                steps.append([st0, st_rs, st_y])
            if prefetch:
                nsd45 = {t_: norm_stages(t_, 128, t_ * 128, gmix_rep, 'gmix') for t_ in (4, 5, 6, 7)}
                carry['nsd45'] = nsd45
                lags = [0, 1, 2]
                n_ = len(steps)
                sched = {-4: [(4, 0), (5, 0)], -3: [(4, 1), (5, 1)], -2: [(4, 2), (5, 2), (6, 0)],
                         -1: [(4, 3), (5, 3), (6, 1), (7, 0)], 0: [(6, 2), (7, 1)], 1: [(7, 2)]}
                for i_ in range(n_ + 2):
                    for si_ in range(3):
                        t_ = i_ - lags[si_]
                        if 0 <= t_ < n_:
                            steps[t_][si_]()
                    for (tt_, st_) in sched.get(i_ - n_, ()):
                        nsd45[tt_][st_]()
            else:
                pipeline(steps, [0, 1, 2])
            ring_release(ij0)
            ring_release(ij1)

        P.wait_all('sp', ['o0', 'o1', 'o2', 'o3', 'o4', 'omisc'])

        sems = {}
        for key in P.sem_keys():
            sems[key] = es.enter_context(nc.semaphore(key))
        with nc.Block() as block:
            @block.sync
            def _(e):
                P.run('sp', e, sems)

            @block.gpsimd
            def _(e):
                P.run('pool', e, sems)

            @block.scalar
            def _(e):
                P.run('act', e, sems)

            @block.vector
            def _(e):
                P.run('dve', e, sems)

            @block.tensor
            def _(e):
                P.run('pe', e, sems)
    return nc


def _consts():
    ident = np.eye(128, dtype=np.float32)
    mask = np.triu(np.ones((128, 128), dtype=np.float32))
    invcnt = np.zeros((128, 4, 16), dtype=np.float32)
    sel = np.zeros((120, 2, 4, 16), dtype=np.float32)
    for g in range(4):
        w = 2 ** (g + 1)
        for t in range(16):
            invcnt[:, g, t] = 1.0 / min(w, t + 1)
        for bl in range(8):
            for k in range(15):
                if k >= 16 - w:
                    for h in range(2):
                        sel[bl * 15 + k, h, g, 8 * h + bl] = 1.0 / w
    return ident, mask, invcnt.reshape(128, 64), sel.reshape(120, 128)


_NC_CACHE = {}


def kernel(x_prompt, x_sample, state_pool, w_in, g_mix, w_pool, s_pool, w_s, b_s, g_v,
           w_pool_out, w_gmlp_out, w_out, g_ffn, w_gate, w_up, w_down, g_final):
    f = lambda a: np.ascontiguousarray(np.asarray(a, dtype=np.float32))
    if 'nc' not in _NC_CACHE:
        _NC_CACHE['nc'] = build_nc()
    nc = _NC_CACHE['nc']
    ident, mask, invcnt, sel = _consts()
    x_prompt = f(x_prompt); x_sample = f(x_sample); state_pool = f(state_pool)
    shared = {
        "w_in": f(w_in)[0], "g_mix": f(g_mix).reshape(1, D), "w_pool": f(w_pool)[0],
        "s_pool": f(s_pool).reshape(1, 512), "w_s": f(w_s)[0], "b_s": f(b_s).reshape(1, 512),
        "g_v": f(g_v).reshape(1, 512), "w_po": f(w_pool_out)[0], "w_go": f(w_gmlp_out)[0],
        "w_out": f(w_out)[0], "g_ffn": f(g_ffn).reshape(1, D), "w_gate": f(w_gate)[0],
        "w_up": f(w_up)[0], "w_down": f(w_down)[0], "g_fin": f(g_final).reshape(1, D),
        "c_ident": ident, "c_mask": mask, "c_invcnt": invcnt, "c_sel": sel,
    }
    in_maps = []
    for c in range(NCORES):
        m = dict(shared)
        m["xp"] = x_prompt[c]
        m["xs"] = np.ascontiguousarray(x_sample[c * NSMP:(c + 1) * NSMP, 0, :])
        m["st"] = np.ascontiguousarray(state_pool[0, c * NSMP:(c + 1) * NSMP].reshape(NSMP * 15, 512))
        in_maps.append(m)
    res = run_bass_kernel_spmd(nc, in_maps, core_ids=list(range(NCORES)))
    rr = res.results
    y_prompt = np.stack([rr[c]["yp"] for c in range(NCORES)], axis=0).astype(np.float32)
    y_sample = np.concatenate([rr[c]["ys"] for c in range(NCORES)], axis=0).reshape(128, 1, D).astype(np.float32)
    new_pool_prompt = np.stack([rr[c]["npp"] for c in range(NCORES)], axis=0)[None].astype(np.float32)
    new_pool_sample = np.concatenate([rr[c]["nps"] for c in range(NCORES)], axis=0)[None].astype(np.float32)
    new_v_sample = np.concatenate([rr[c]["nvs"] for c in range(NCORES)], axis=0).reshape(1, 128, 1, 512).astype(np.float32)
    return (y_prompt, y_sample, new_pool_prompt, new_pool_sample, new_v_sample)
```

```python
import numpy as np
from contextlib import ExitStack
import concourse.bass as bass
import concourse.mybir as mybir
from concourse.bass_utils import run_bass_kernel_spmd

F32 = mybir.dt.float32
BF16 = mybir.dt.bfloat16
AF = mybir.ActivationFunctionType
ALU = mybir.AluOpType

D = 1024
DFF = 2816
NF = DFF // 128
SEQ = 2048
NSMP = 16
NCORES = 8
TPS = 1024
NSUP = SEQ // TPS
TW = TPS + NSMP
EPS = 1e-6
NSLOT = 4
SLOT_ELEMS = 4096
NBANK = 8
ENGS = ('sp', 'pool', 'act', 'dve', 'pe')


class Prog:
    def __init__(self):
        self.streams = {e: [] for e in ENGS}
        self.seq = {e: 0 for e in ENGS}
        self.waited = {e: {} for e in ENGS}
        self.lw = {}
        self.rd = {}
        self.dcnt = {}

    def _deps(self, eng, reads, writes):
        deps = []
        for r in reads:
            t = self.lw.get(r)
            if t is not None:
                deps.append(t)
        for w in writes:
            t = self.lw.get(w)
            if t is not None:
                deps.append(t)
            deps.extend(self.rd.get(w, ()))
        need = {}
        for (sk, val, peng) in deps:
            if peng == 'pe' and eng == 'pe':
                continue
            if self.waited[eng].get(sk, 0) >= val:
                continue
            if need.get(sk, 0) < val:
                need[sk] = val
        for sk, val in need.items():
            self.waited[eng][sk] = val
        return list(need.items())

    def _commit(self, tok, reads, writes):
        for r in reads:
            self.rd.setdefault(r, []).append(tok)
        for w in writes:
            self.lw[w] = tok
            self.rd[w] = []

    def op(self, eng, fn, reads=(), writes=()):
        waits = self._deps(eng, reads, writes)
        self.seq[eng] += 1
        sk = 'E_' + eng
        tok = (sk, self.seq[eng], eng)
        self._commit(tok, reads, writes)
        self.streams[eng].append((waits, fn, sk, 1))
        return tok

    def dma(self, q, fn, sk, reads=(), writes=()):
        waits = self._deps(q, reads, writes)
        self.dcnt[sk] = self.dcnt.get(sk, 0) + 16
        tok = (sk, self.dcnt[sk], None)
        self._commit(tok, reads, writes)
        self.streams[q].append((waits, fn, sk, 16))
        return tok

    def wait_all(self, eng, sks):
        waits = [(sk, self.dcnt[sk]) for sk in sks if self.dcnt.get(sk, 0) > 0]
        self.streams[eng].append((waits, None, None, 0))

    def sem_keys(self):
        keys = ['E_' + e for e in ('pool', 'act', 'dve', 'pe')]
        keys += sorted(self.dcnt.keys())
        return keys

    def run(self, eng, e, sems):
        for (waits, fn, sk, inc) in self.streams[eng]:
            for (wk, val) in waits:
                e.wait_ge(sems[wk], val)
            if fn is None:
                continue
            ins = fn(e)
            ins.then_inc(sems[sk], inc)


def build_nc():
    nc = bass.Bass("TRN2", target_bir_lowering=False)

    def din(name, shape):
        return nc.dram_tensor(name, list(shape), F32, kind="ExternalInput").ap()

    def dout(name, shape):
        return nc.dram_tensor(name, list(shape), F32, kind="ExternalOutput").ap()

    xp = din("xp", [SEQ, D])
    xs = din("xs", [NSMP, D])
    st = din("st", [NSMP * 15, 512])
    w_in = din("w_in", [D, 3584])
    g_mix = din("g_mix", [1, D])
    w_pool = din("w_pool", [4, 128, 128])
    s_pool = din("s_pool", [1, 512])
    w_s = din("w_s", [4, 128, 128])
    b_s = din("b_s", [1, 512])
    g_v = din("g_v", [1, 512])
    w_po = din("w_po", [512, D])
    w_go = din("w_go", [512, D])
    w_out = din("w_out", [D, D])
    g_ffn = din("g_ffn", [1, D])
    w_gate = din("w_gate", [D, DFF])
    w_up = din("w_up", [D, DFF])
    w_down = din("w_down", [DFF, D])
    g_fin = din("g_fin", [1, D])
    c_ident = din("c_ident", [128, 128])
    c_mask = din("c_mask", [128, 128])
    c_invcnt = din("c_invcnt", [128, 64])
    c_sel = din("c_sel", [120, 128])

    yp = dout("yp", [SEQ, D])
    ys = dout("ys", [NSMP, D])
    npp = dout("npp", [15, 512])
    nps = dout("nps", [NSMP, 15, 512])
    nvs = dout("nvs", [NSMP, 512])

    win_v = w_in.rearrange("(k p) m -> p k m", p=128)
    wpo_v = w_po.rearrange("(k p) m -> p k m", p=128)
    wgo_v = w_go.rearrange("(k p) m -> p k m", p=128)
    wout_v = w_out.rearrange("(k p) m -> p k m", p=128)
    wgate_v = w_gate.rearrange("(k p) m -> p k m", p=128)
    wup_v = w_up.rearrange("(k p) m -> p k m", p=128)
    wdown_v = w_down.rearrange("(f p) m -> p f m", p=128)
    st3 = st.rearrange("(b k) c -> b k c", k=15)

    es = ExitStack()
    with es:
        def sb(name, shape, dt):
            return es.enter_context(nc.sbuf_tensor(name, list(shape), dt))

        def ps(name, shape, dt):
            return es.enter_context(nc.psum_tensor(name, list(shape), dt))

        xbuf = sb("xbuf", [128, 9, D], F32)
        hT = sb("hT", [128, 8, TW], BF16)
        ovl = sb("ovl", [128, 25408], BF16)
        AB = sb("AB", [128, 2, 16 + TPS], F32)
        shr = sb("shr", [128, 2 * (16 + TPS)], F32)
        t1 = shr[:, 0:16 + TPS]
        t2 = shr[:, 16 + TPS:2 * (16 + TPS)]
        ybuf = sb("ybuf", [128, 3, D], F32)
        ABf = AB[:, :, :].rearrange("p a t -> p (a t)")
        xstage = [ABf[:, 0:D], ABf[:, D:2 * D], shr[:, 0:D], shr[:, D:2 * D]]
        AB_RES = [('ab', a_, k_) for a_ in range(2) for k_ in ('h', 0, 512)]
        SHR_RES = ['t1', 't2'] + [('gab', a_, b_) for a_ in range(2) for b_ in range(2)]
        xstage_res = [AB_RES, AB_RES, SHR_RES, SHR_RES]
        ring = sb("ring", [128, NSLOT, SLOT_ELEMS], BF16)
        gmix_rep = sb("gmix_rep", [128, D], F32)
        gffn_rep = sb("gffn_rep", [128, D], F32)
        gfin_rep = sb("gfin_rep", [128, D], F32)
        gv_rep = sb("gv_rep", [128, 512], F32)
        ident_f = sb("ident_f", [128, 128], F32)
        ident_b = sb("ident_b", [128, 128], BF16)
        mask_f = sb("mask_f", [128, 128], F32)
        invcnt = sb("invcnt", [128, 4, 16], F32)
        sel = sb("sel", [120, 2, 4, 16], F32)
        stb = sb("stb", [120, 2, 512], F32)
        spT = sb("spT", [128, 4], F32)
        w00rep = sb("w00rep", [128, 4], F32)
        b0rep = sb("b0rep", [128, 4], F32)
        wpool_b = sb("wpool_b", [128, 4, 128], BF16)
        ws_nat = sb("ws_nat", [128, 4, 128], BF16)
        wsT = sb("wsT", [128, 4, 128], BF16)
        bs_rep = sb("bs_rep", [128, 4, 128], BF16)
        mhalf = sb("mhalf", [128, 1], F32)
        NCOL = 96
        ss = sb("ss", [128, NCOL], F32)
        ms = sb("ms", [128, NCOL], F32)
        rs = sb("rs", [128, NCOL], F32)
        halo = sb("halo", [128, 4, 16], F32)
        aTs = sb("aTs", [128, 4, 16], F32)
        stmp = sb("stmp", [128, 16], F32)
        tmp16 = sb("tmp16", [128, 16], F32)
        hb = sb("hb", [128, 2, D], BF16)
        junk = sb("junk", [128, 2, D], BF16)
        ftmp = sb("ftmp", [128, 2, 512], F32)
        gab = shr[:, 0:2048].rearrange("p (a b c) -> p a b c", a=2, b=2)
        vout = sb("vout", [16, 512], F32)
        nppbuf = sb("nppbuf", [16, 512], F32)
        npsbuf = sb("npsbuf", [16, 512], F32)

        o = 0
        pooled = ovl[:, o:o + 4 * TW].rearrange("p (g t) -> p g t", g=4); o += 4 * TW
        paT = ovl[:, o:o + 4 * TW].rearrange("p (g t) -> p g t", g=4); o += 4 * TW
        vtok = ovl[:, o:o + 9 * 512].rearrange("p (n c) -> p n c", n=9); o += 9 * 512
        sgT = ovl[:, o:o + 4 * TW].rearrange("p (g t) -> p g t", g=4); o += 4 * TW
        mergedT = ovl[:, o:o + 8 * TW].rearrange("p (g t) -> p g t", g=8); o += 8 * TW
        assert o <= 25408
        actT = ovl[:, 0:NF * TW].rearrange("p (f t) -> p f t", f=NF)

        PS = [ps("ps%d" % i, [128, 512], F32) for i in range(NBANK)]
        PSB = [p[:, :].bitcast(BF16).rearrange("p (k t) -> p k t", k=8) for p in PS]

        P = Prog()
        state = {'bank': 0, 'col': 0, 'hb': 0, 'ft': 0, 'ga': 0, 'cp': 0, 'jk': 0, 'yb': 0}

        held = set()

        def bank():
            while True:
                b = state['bank']
                state['bank'] = (b + 1) % NBANK
                if b not in held:
                    return b

        def newcol():
            c = state['col']
            state['col'] += 1
            assert c < NCOL
            return c

        def nxt(key, n):
            v = state[key]
            state[key] = (v + 1) % n
            return v

        def slot_view(slot, off, k, m):
            return ring[:, slot, off:off + k * m].rearrange("p (k m) -> p k m", k=k)

        blocks = []
        carry = {}
        for s in range(NSUP):
            for cb in (2, 0, 1):
                blocks.append([(0, 8, 512, win_v[:, :, cb * 512:(cb + 1) * 512])])
            for q in range(4):
                blocks.append([(0, 8, 256, win_v[:, :, 1536 + q * 256:1536 + (q + 1) * 256]),
                               (2048, 8, 256, win_v[:, :, 2560 + q * 256:2560 + (q + 1) * 256])])
                blocks.append([(0, 4, 256, wpo_v[:, :, q * 256:(q + 1) * 256]),
                               (1024, 4, 256, wgo_v[:, :, q * 256:(q + 1) * 256])])
            for half in range(2):
                blocks.append([(0, 8, 512, wout_v[:, :, half * 512:(half + 1) * 512])])
            for j in range(NF // 2):
                blocks.append([(0, 8, 256, wgate_v[:, :, j * 256:(j + 1) * 256]),
                               (2048, 8, 256, wup_v[:, :, j * 256:(j + 1) * 256])])
            for (f0, nf) in ((0, 6), (6, 8), (14, 8)):
                for half in range(2):
                    blocks.append([(0, nf, 512, wdown_v[:, f0:f0 + nf, half * 512:(half + 1) * 512])])
        rstate = {'issued': 0, 'next': 0}

        def ring_issue(j):
            slot = j % NSLOT
            for pi, (off, k, m, src) in enumerate(blocks[j]):
                dst = slot_view(slot, off, k, m)
                P.dma('pool',
                      (lambda e, dst=dst, src=src: e.dma_start(out=dst, in_=src)),
                      'w%d' % slot, reads=(), writes=[('ring', slot, pi)])
            fin = ('w%d' % slot, P.dcnt['w%d' % slot], None)
            for pi in range(len(blocks[j])):
                P.lw[('ring', slot, pi)] = fin
            rstate['issued'] = j + 1

        def ring_next():
            j = rstate['next']
            rstate['next'] += 1
            assert j < rstate['issued']
            slot = j % NSLOT
            return j, slot, [('ring', slot, 0), ('ring', slot, 1)]

        def ring_release(j):
            if j + NSLOT < len(blocks):
                assert rstate['issued'] == j + NSLOT
                ring_issue(j + NSLOT)

        cres = []

        def cdma(q, dst, src, res, noncontig=False, own=None):
            def fn(e, dst=dst, src=src):
                if noncontig:
                    with nc.allow_non_contiguous_dma(reason="tiny constant"):
                        return e.dma_start(out=dst, in_=src)
                return e.dma_start(out=dst, in_=src)
            if own is not None:
                P.dma(q, fn, own, reads=(), writes=[res])
                return
            sk = {'sp': 'cst', 'act': 'csta', 'pool': 'cstp'}[q]
            P.dma(q, fn, sk, reads=(), writes=[res])
            cres.append((sk, res))

        def xload(sidx, slot):
            R = 128 if slot < 8 else NSMP
            src = xp[sidx * TPS + slot * 128:sidx * TPS + (slot + 1) * 128, :] if slot < 8 else xs[:, :]
            P.dma('sp', (lambda e: e.dma_start(out=xbuf[:R, slot, :], in_=src)),
                  'x%d' % slot, reads=(), writes=[('x', slot, 0), ('x', slot, 1)])

        cdma('act', gmix_rep[:, :], g_mix[0:1, :].partition_broadcast(128), 'gmix', own='c_gmix')
        for slot in range(4):
            xload(0, slot)
        cdma('pool', ident_b[:, :], c_ident[:, :], 'ident_b', own='c_idb')
        for slot in range(4, 9):
            xload(0, slot)
        cdma('act', ident_f[:, :], c_ident[:, :], 'ident_f')
        cdma('act', mask_f[:, :], c_mask[:, :], 'mask_f')
        cdma('act', invcnt[:, :, :], c_invcnt.rearrange("p (g t) -> p g t", g=4), 'invcnt')
        cdma('act', sel[:, :, :, :], c_sel.rearrange("p (h g t) -> p h g t", h=2, g=4), 'sel')
        cdma('act', stb[:, 0, :], st[0:120, :], 'stb')
        cdma('act', stb[:, 1, :], st[120:240, :], 'stb1')
        cdma('sp', spT[:, :], s_pool.rearrange("o (g p) -> p (o g)", p=128), 'spT', True)
        cdma('sp', gv_rep[:, :], g_v[0:1, :].partition_broadcast(128), 'gv')
        cdma('sp', w00rep[:, :], w_s[:, 0, 0:1].rearrange("g o -> o g").partition_broadcast(128), 'w00', True)
        cdma('sp', b0rep[:, :], b_s.rearrange("o (g i) -> o g i", g=4)[:, :, 0].partition_broadcast(128), 'b0', True)
        def late_pool_consts():
            cdma('pool', wpool_b[:, :, :], w_pool.rearrange("g c d -> c g d"), 'wpool', own='c_wpool')
            cdma('pool', ws_nat[:, :, :], w_s.rearrange("g i j -> i g j"), 'ws_nat', own='c_wsnat')
            cdma('pool', bs_rep[:, :, :].rearrange("p g i -> p (g i)"), b_s[0:1, :].partition_broadcast(128),
                 'bs_rep', own='c_bsrep')
            for j in range(1, NSLOT):
                ring_issue(j)
        cdma('sp', gffn_rep[:, :], g_ffn[0:1, :].partition_broadcast(128), 'gffn')
        cdma('sp', gfin_rep[:, :], g_fin[0:1, :].partition_broadcast(128), 'gfin')
        for (sk, res) in cres:
            P.lw[res] = (sk, P.dcnt[sk], None)

        ring_issue(0)

        P.dma('sp', lambda e: e.dma_start(out=nps[:, 0:14, :], in_=st3[:, 1:15, :]), 'omisc')

        P.op('dve', lambda e: e.memset(ss[:, :], 0.0), writes=[('ss', c) for c in range(NCOL)])
        P.op('dve', lambda e: e.memset(mhalf[:, :], -0.5), writes=['mhalf'])

        def mm_group(out, lhs_fn, rhs_fn, n):
            def fn(e):
                ins = None
                for i in range(n):
                    ins = e.matmul(out=out, lhsT=lhs_fn(i), rhs=rhs_fn(i), start=(i == 0), stop=(i == n - 1))
                return ins
            return fn

        def rstd_stages(src_ap, R, width, src_res):
            k = newcol()

            def st_sq():
                ji = nxt('jk', 2)
                P.op('act', lambda e: e.activation(out=junk[:R, ji, 0:width], in_=src_ap, func=AF.Square,
                                                   accum_out=ss[:R, k:k + 1]),
                     reads=list(src_res), writes=[('ss', k), ('junk', ji)])

            def st_rs():
                P.op('dve', lambda e: e.tensor_scalar(out=ms[:R, k:k + 1], in0=ss[:R, k:k + 1],
                                                      scalar1=1.0 / width, scalar2=EPS,
                                                      op0=ALU.mult, op1=ALU.add),
                     reads=[('ss', k)], writes=[('ms', k)])
                P.op('pool', lambda e: e.tensor_tensor(out=rs[:R, k:k + 1], in0=ms[:R, k:k + 1],
                                                       in1=mhalf[:R, 0:1], op=ALU.pow),
                     reads=[('ms', k), 'mhalf'], writes=[('rs', k)])
            return k, st_sq, st_rs

        def rstd_chain(src_ap, R, width, src_res):
            k, a, b = rstd_stages(src_ap, R, width, src_res)
            a()
            b()
            return k

        def norm_stages(slot, R, col0, grep, gres, copy_eng=None, src=None, src_res=None):
            xres = [('x', slot, 0), ('x', slot, 1)] if src_res is None else list(src_res)
            xsrc = xbuf[:R, slot, :] if src is None else src
            k, st_sq, st_rs = rstd_stages(xsrc, R, D, xres)
            box = {}

            def st_h():
                hi = nxt('hb', 2)
                box['hi'] = hi
                P.op('dve', lambda e: e.scalar_tensor_tensor(out=hb[:R, hi, :], in0=xsrc,
                                                             scalar=rs[:R, k:k + 1], in1=grep[:R, :],
                                                             op0=ALU.mult, op1=ALU.mult),
                     reads=xres + [('rs', k), gres], writes=[('hb', hi)])

            def st_tr():
                hi = box['hi']
                b = bank()

                def trf(e):
                    ins = None
                    for kc in range(8):
                        ins = e.transpose(out=PSB[b][:, kc, :R], in_=hb[:R, hi, kc * 128:(kc + 1) * 128],
                                          identity=ident_b[:R, :R])
                    return ins
                P.op('pe', trf, reads=[('hb', hi), 'ident_b'], writes=[('ps', b)])
                src = PSB[b][:, :, :R]
                dst = hT[:, :, col0:col0 + R]
                ce = copy_eng if copy_eng is not None else ('act' if nxt('cp', 2) == 0 else 'dve')
                if ce == 'act':
                    P.op('act', lambda e: e.activation(out=dst, in_=src, func=AF.Copy),
                         writes=[('hT', slot, 0), ('hT', slot, 1), ('ps', b)])
                else:
                    P.op('dve', lambda e: e.tensor_copy(out=dst, in_=src),
                         writes=[('hT', slot, 0), ('hT', slot, 1), ('ps', b)])
            return [st_sq, st_rs, st_h, st_tr]

        def pipeline(steps, lags):
            n = len(steps)
            for i in range(n + max(lags)):
                for si in range(len(lags)):
                    t = i - lags[si]
                    if 0 <= t < n:
                        steps[t][si]()

        def hT_res(tl):
            r = []
            for t in tl:
                r += [('hT', t, 0), ('hT', t, 1)]
            return r

        def ws_prep():
            wb_ = bank()

            def ws_tr(e):
                ins = None
                for g in range(4):
                    ins = e.transpose(out=PSB[wb_][:, g, :], in_=ws_nat[:, g, :], identity=ident_b[:, :])
                return ins
            P.op('pe', ws_tr, reads=['ws_nat', 'ident_b'], writes=[('ps', wb_)])
            for g in range(4):
                P.op('dve', lambda e, g=g: e.tensor_tensor(out=wsT[:, g, :], in0=PSB[wb_][:, g, :], in1=mask_f[:, :],
                                                           op=ALU.mult),
                     reads=['mask_f'], writes=[('wsT', g), ('ps', wb_)])

        for s in range(NSUP):
            tts = [(i, 128, i * 128) for i in range(8)]
            nts = [(0, 512, [0, 1, 2, 3]), (512, 512, [4, 5, 6, 7])]
            if s == 0:
                tts.append((8, NSMP, TPS))
                nts.append((TPS, NSMP, [8]))
            t0 = s * TPS

            cj, cslot, cres_ = ring_next()
            wv = slot_view(cslot, 0, 8, 512)

            def c_tile(slot, R, col0, wv=wv, cres_=cres_):
                b = bank()
                P.op('pe', mm_group(PS[b][:R, :],
                                    lambda i, col0=col0, R=R: hT[:, i, col0:col0 + R],
                                    lambda i, wv=wv: wv[:, i, :], 8),
                     reads=cres_ + hT_res([slot]), writes=[('ps', b)])
                fi = nxt('ft', 2)
                P.op('act', lambda e, b=b, fi=fi, R=R: e.activation(out=ftmp[:R, fi, :], in_=PS[b][:R, :],
                                                                   func=AF.Gelu_apprx_tanh),
                     reads=[], writes=[('ftmp', fi), ('ps', b)])
                k = rstd_chain(ftmp[:R, fi, :], R, 512, [('ftmp', fi)])
                if slot < 8:
                    P.op('dve', lambda e, fi=fi, R=R, k=k, slot=slot: e.scalar_tensor_tensor(
                        out=vtok[:R, slot, :], in0=ftmp[:R, fi, :], scalar=rs[:R, k:k + 1], in1=gv_rep[:R, :],
                        op0=ALU.mult, op1=ALU.mult),
                         reads=[('ftmp', fi), ('rs', k), 'gv'], writes=[('vtok', slot)])
                else:
                    P.op('dve', lambda e, fi=fi, R=R, k=k: e.scalar_tensor_tensor(
                        out=vout[:R, :], in0=ftmp[:R, fi, :], scalar=rs[:R, k:k + 1], in1=gv_rep[:R, :],
                        op0=ALU.mult, op1=ALU.mult),
                         reads=[('ftmp', fi), ('rs', k), 'gv'], writes=['vout'])
                    P.op('dve', lambda e, R=R, slot=slot: e.tensor_copy(out=vtok[:R, slot, :], in_=vout[:R, :]),
                         reads=['vout'], writes=[('vtok', slot)])
                    P.dma('sp', lambda e: e.dma_start(out=nvs[:, :], in_=vout[:NSMP, :]), 'omisc', reads=['vout'])


            if s == 0:
                pipeline([norm_stages(slot, R, col0, gmix_rep, 'gmix') for (slot, R, col0) in tts],
                         [0, 1, 2, 3])
                late_pool_consts()
            if s == 0:
                for (slot, R, col0) in tts:
                    c_tile(slot, R, col0)
                ws_prep()
            else:
                nsd = {slot: norm_stages(slot, R, col0, gmix_rep, 'gmix') for (slot, R, col0) in tts[6:8]}
                nsd.update(carry['nsd45'])
                for pair, ci in (((4, 5), (0, 1)), ((6, 7), (2, 3))):
                    for st_i in range(3):
                        for t_ in pair:
                            if t_ >= 6:
                                nsd[t_][st_i]()
                    for c_ in ci:
                        c_tile(*tts[c_])
                    for t_ in pair:
                        nsd[t_][3]()
                for (slot, R, col0) in tts[4:]:
                    c_tile(slot, R, col0)
            ring_release(cj)

            bj, bslot, bres = ring_next()
            wv = slot_view(bslot, 0, 8, 512)
            b4_list = []
            deferred_ops = []
            npp_pending = []
            for g in range(4):
                w = 2 ** (g + 1)
                ai = g % 2
                abres = [('ab', ai, 'h'), ('ab', ai, 0), ('ab', ai, 512)]
                if s == 0:
                    P.op('dve', lambda e, ai=ai: e.memset(AB[:, ai, 0:16], 0.0), writes=[('ab', ai, 'h')])
                else:
                    P.op('dve', lambda e, ai=ai, g=g: e.tensor_copy(out=AB[:, ai, 0:16], in_=halo[:, g, :]),
                         reads=[('halo', g)], writes=[('ab', ai, 'h')])
                for (c0, W, tl) in nts:
                    b = bank()
                    P.op('pe', mm_group(PS[b][:, :W],
                                        lambda i, g=g, wv=wv: wv[:, i, g * 128:(g + 1) * 128],
                                        lambda i, c0=c0, W=W: hT[:, i, c0:c0 + W], 8),
                         reads=bres + hT_res(tl), writes=[('ps', b)])
                    if c0 < TPS:
                        P.op('act', lambda e, b=b, ai=ai, c0=c0, W=W: e.activation(
                            out=AB[:, ai, 16 + c0:16 + c0 + W], in_=PS[b][:, :W], func=AF.Copy),
                             reads=[], writes=[('ab', ai, c0), ('ps', b)])
                    else:
                        P.op('act', lambda e, b=b, g=g, W=W: e.activation(
                            out=aTs[:, g, :], in_=PS[b][:, :W], func=AF.Copy),
                             reads=[], writes=[('aTs', g), ('ps', b)])
                while npp_pending:
                    npp_pending.pop(0)()

                def pool_chain(g=g, w=w, ai=ai, abres=abres, defer=False):
                    def OP(*a_, **k_):
                        if defer:
                            deferred_ops.append(lambda: P.op(*a_, **k_))
                        else:
                            P.op(*a_, **k_)
                    L = TPS
                    A = AB[:, ai, :]
                    OP('dve', lambda e, A=A: e.tensor_tensor(out=t1[:, 1:16 + L], in0=A[:, 1:16 + L],
                                                               in1=A[:, 0:15 + L], op=ALU.add),
                         reads=abres, writes=['t1'])
                    if g >= 1:
                        OP('dve', lambda e: e.tensor_tensor(out=t2[:, 3:16 + L], in0=t1[:, 3:16 + L],
                                                              in1=t1[:, 1:14 + L], op=ALU.add),
                             reads=['t1'], writes=['t2'])
                    if g >= 2:
                        OP('dve', lambda e: e.tensor_tensor(out=t1[:, 7:16 + L], in0=t2[:, 7:16 + L],
                                                              in1=t2[:, 3:12 + L], op=ALU.add),
                             reads=['t2'], writes=['t1'])
                    if g >= 3:
                        OP('dve', lambda e: e.tensor_tensor(out=t2[:, 15:16 + L], in0=t1[:, 15:16 + L],
                                                              in1=t1[:, 7:8 + L], op=ALU.add),
                             reads=['t1'], writes=['t2'])
                    wsum = t1 if g in (0, 2) else t2
                    wres = 't1' if g in (0, 2) else 't2'
                    OP('dve', lambda e, wsum=wsum, A=A, g=g, w=w: e.scalar_tensor_tensor(
                        out=pooled[:, g, 0:L], in0=wsum[:, 16:16 + L], scalar=1.0 / w, in1=A[:, 16:16 + L],
                        op0=ALU.mult, op1=ALU.subtract),
                         reads=[wres] + abres, writes=[('pooled', g)])
                    if s == 0:
                        OP('dve', lambda e, wsum=wsum, g=g: e.tensor_tensor(
                            out=tmp16[:, :], in0=wsum[:, 16:32], in1=invcnt[:, g, :], op=ALU.mult),
                             reads=[wres, 'invcnt'], writes=['tmp16'])
                        OP('dve', lambda e, A=A, g=g: e.tensor_tensor(
                            out=pooled[:, g, 0:16], in0=tmp16[:, :], in1=A[:, 16:32], op=ALU.subtract),
                             reads=['tmp16'] + abres, writes=[('pooled', g)])
                    if s < NSUP - 1:
                        OP('dve', lambda e, A=A, g=g: e.tensor_copy(out=halo[:, g, :], in_=A[:, L:L + 16]),
                             reads=abres, writes=[('halo', g)])
                    else:
                        def npp_ops(A=A, g=g, abres=abres):
                            b = bank()
                            P.op('pe', lambda e: e.transpose(out=PS[b][:15, 0:128], in_=A[:, L + 1:L + 16],
                                                             identity=ident_f[:, :]),
                                 reads=abres + ['ident_f'], writes=[('ps', b)])
                            P.op('act', lambda e: e.activation(out=nppbuf[:15, g * 128:(g + 1) * 128],
                                                               in_=PS[b][:15, 0:128], func=AF.Copy),
                                 reads=[], writes=[('nppbuf', g), ('ps', b)])
                        if defer:
                            deferred_ops.append(npp_ops)
                        else:
                            npp_pending.append(npp_ops)
                    if s == 0:
                        b = bank()
                        OP('pe', mm_group(PS[b][:, 0:NSMP],
                                            lambda i, g=g: stb[:, i, g * 128:(g + 1) * 128],
                                            lambda i, g=g: sel[:, i, g, :], 2),
                             reads=['stb', 'stb1', 'sel'], writes=[('ps', b)])
                        OP('dve', lambda e, b=b, g=g, w=w: e.scalar_tensor_tensor(
                            out=pooled[:, g, TPS:TW], in0=aTs[:, g, :], scalar=(1.0 / w - 1.0),
                            in1=PS[b][:, 0:NSMP], op0=ALU.mult, op1=ALU.add),
                             reads=[('aTs', g)], writes=[('pooled', g, 's'), ('ps', b)])

                pool_chain(defer=(g == 3))

                def b4(g=g):
                    for (c0, W, tl) in nts:
                        b = bank()
                        pres = [('pooled', g)] if c0 < TPS else [('pooled', g, 's')]
                        P.op('pe', lambda e, b=b, g=g, c0=c0, W=W: e.matmul(
                            out=PS[b][:, :W], lhsT=wpool_b[:, g, :], rhs=pooled[:, g, c0:c0 + W],
                            start=True, stop=True),
                             reads=pres + ['wpool'], writes=[('ps', b)])
                        P.op('act', lambda e, b=b, g=g, c0=c0, W=W: e.activation(
                            out=paT[:, g, c0:c0 + W], in_=PS[b][:, :W], func=AF.Copy, scale=spT[:, g:g + 1]),
                             reads=['spT'], writes=[('paT', g, c0), ('ps', b)])
                b4_list.append(b4)
            while npp_pending:
                npp_pending.pop(0)()
            ring_release(bj)
            if s == 0:
                b = bank()

                def atr(e, b=b):
                    ins = None
                    for g in range(4):
                        ins = e.transpose(out=PS[b][:NSMP, g * 128:(g + 1) * 128], in_=aTs[:, g, :],
                                          identity=ident_f[:, :])
                    return ins
                P.op('pe', atr, reads=[('aTs', g) for g in range(4)] + ['ident_f'], writes=[('ps', b)])
                P.op('act', lambda e, b=b: e.activation(out=npsbuf[:NSMP, :], in_=PS[b][:NSMP, :], func=AF.Copy),
                     reads=[], writes=['npsbuf', ('ps', b)])
                P.dma('sp', lambda e: e.dma_start(out=nps[:, 14, :], in_=npsbuf[:NSMP, :]), 'omisc',
                      reads=['npsbuf'])

            dj, dslot, dres = ring_next()
            wv = slot_view(dslot, 0, 8, 512)
            for g in range(4):
                for (c0, W, tl) in nts:
                    bU = bank()
                    P.op('pe', mm_group(PS[bU][:, :W],
                                        lambda i, g=g, wv=wv: wv[:, i, g * 128:(g + 1) * 128],
                                        lambda i, c0=c0, W=W: hT[:, i, c0:c0 + W], 8),
                         reads=dres + hT_res(tl), writes=[('ps', bU)])
                    fi = nxt('ft', 2)
                    P.op('act', lambda e, bU=bU, fi=fi, W=W: e.activation(out=ftmp[:, fi, :W], in_=PS[bU][:, :W],
                                                                         func=AF.Gelu_apprx_tanh),
                         reads=[], writes=[('ftmp', fi), ('ps', bU)])
                    if c0 < TPS:
                        bS = bank()

                        def spat(e, bS=bS, g=g, tl=tl):
                            ins = None
                            for j, tslot in enumerate(tl):
                                ins = e.matmul(out=PS[bS][:, j * 128:(j + 1) * 128],
                                               lhsT=vtok[:, tslot, g * 128:(g + 1) * 128], rhs=wsT[:, g, :],
                                               start=True, stop=True)
                            return ins
                        P.op('pe', spat, reads=[('vtok', t) for t in tl] + [('wsT', g)],
                             writes=[('ps', bS)])
                        P.op('dve', lambda e, bS=bS, g=g: e.tensor_tensor(
                            out=PS[bS][:, :].rearrange("p (j i) -> p j i", j=4),
                            in0=PS[bS][:, :].rearrange("p (j i) -> p j i", j=4),
                            in1=bs_rep[:, g:g + 1, :].to_broadcast([128, 4, 128]), op=ALU.add),
                             reads=['bs_rep'], writes=[('ps', bS)])
                        P.op('dve', lambda e, bS=bS, fi=fi, g=g, c0=c0, W=W: e.tensor_tensor(
                            out=sgT[:, g, c0:c0 + W], in0=ftmp[:, fi, :W], in1=PS[bS][:, :W], op=ALU.mult),
                             reads=[('ftmp', fi)], writes=[('sgT', g, c0), ('ps', bS)])
                    else:
                        bS = bank()
                        P.op('pe', lambda e, g=g, bS=bS: e.transpose(out=PSB[bS][:, 0, :NSMP],
                                                                     in_=vtok[:NSMP, 8, g * 128:(g + 1) * 128],
                                                                     identity=ident_b[:NSMP, :NSMP]),
                             reads=[('vtok', 8), 'ident_b'], writes=[('ps', bS)])
                        P.op('dve', lambda e, g=g, bS=bS: e.tensor_scalar(
                            out=stmp[:, :], in0=PSB[bS][:, 0, :NSMP], scalar1=w00rep[:, g:g + 1],
                            scalar2=b0rep[:, g:g + 1], op0=ALU.mult, op1=ALU.add),
                             reads=['w00', 'b0'], writes=['stmp', ('ps', bS)])
                        P.op('dve', lambda e, fi=fi, g=g, c0=c0, W=W: e.tensor_tensor(
                            out=sgT[:, g, c0:c0 + W], in0=stmp[:, :W], in1=ftmp[:, fi, :W], op=ALU.mult),
                             reads=['stmp', ('ftmp', fi)], writes=[('sgT', g, c0)])
                    if deferred_ops:
                        deferred_ops.pop(0)()
                if g < 3:
                    b4_list[g]()
            while deferred_ops:
                deferred_ops.pop(0)()
            b4_list[3]()
            ring_release(dj)
            if s == NSUP - 1:
                P.dma('sp', lambda e: e.dma_start(out=npp[:, :], in_=nppbuf[:15, :]), 'omisc',
                      reads=[('nppbuf', g) for g in range(4)])

            for q in range(4):
                ej, eslot, eres = ring_next()
                fj, fslot, fres = ring_next()
                gaw = slot_view(eslot, 0, 8, 256)
                gbw = slot_view(eslot, 2048, 8, 256)
                pow_ = slot_view(fslot, 0, 4, 256)
                gow = slot_view(fslot, 1024, 4, 256)
                for dl in range(2):
                    d = 2 * q + dl
                    for (c0, W, tl) in nts:
                        gi = nxt('ga', 2)
                        bA = bank(); bB = bank(); bP = bank(); bQ = bank()
                        P.op('pe', mm_group(PS[bA][:, :W],
                                            lambda i, dl=dl, gaw=gaw: gaw[:, i, dl * 128:(dl + 1) * 128],
                                            lambda i, c0=c0, W=W: hT[:, i, c0:c0 + W], 8),
                             reads=eres + hT_res(tl), writes=[('ps', bA)])
                        P.op('act', lambda e, bA=bA, gi=gi, W=W: e.activation(
                            out=gab[:, gi, 0, :W], in_=PS[bA][:, :W], func=AF.Sigmoid),
                             reads=[], writes=[('gab', gi, 0), ('ps', bA)])
                        P.op('pe', mm_group(PS[bB][:, :W],
                                            lambda i, dl=dl, gbw=gbw: gbw[:, i, dl * 128:(dl + 1) * 128],
                                            lambda i, c0=c0, W=W: hT[:, i, c0:c0 + W], 8),
                             reads=eres + hT_res(tl), writes=[('ps', bB)])
                        P.op('act', lambda e, bB=bB, gi=gi, W=W: e.activation(
                            out=gab[:, gi, 1, :W], in_=PS[bB][:, :W], func=AF.Sigmoid),
                             reads=[], writes=[('gab', gi, 1), ('ps', bB)])
                        P.op('pe', mm_group(PS[bP][:, :W],
                                            lambda i, dl=dl, pow_=pow_: pow_[:, i, dl * 128:(dl + 1) * 128],
                                            lambda i, c0=c0, W=W: paT[:, i, c0:c0 + W], 4),
                             reads=fres + [('paT', gg, c0) for gg in range(4)], writes=[('ps', bP)])
                        P.op('dve', lambda e, bP=bP, gi=gi, W=W: e.tensor_tensor(
                            out=gab[:, gi, 0, :W], in0=gab[:, gi, 0, :W], in1=PS[bP][:, :W], op=ALU.mult),
                             reads=[('gab', gi, 0)], writes=[('gab', gi, 0), ('ps', bP)])
                        P.op('pe', mm_group(PS[bQ][:, :W],
                                            lambda i, dl=dl, gow=gow: gow[:, i, dl * 128:(dl + 1) * 128],
                                            lambda i, c0=c0, W=W: sgT[:, i, c0:c0 + W], 4),
                             reads=fres + [('sgT', gg, c0) for gg in range(4)], writes=[('ps', bQ)])
                        P.op('dve', lambda e, bQ=bQ, gi=gi, W=W: e.tensor_tensor(
                            out=gab[:, gi, 1, :W], in0=gab[:, gi, 1, :W], in1=PS[bQ][:, :W], op=ALU.mult),
                             reads=[('gab', gi, 1)], writes=[('gab', gi, 1), ('ps', bQ)])
                        P.op('dve', lambda e, gi=gi, d=d, c0=c0, W=W: e.tensor_tensor(
                            out=mergedT[:, d, c0:c0 + W], in0=gab[:, gi, 0, :W], in1=gab[:, gi, 1, :W], op=ALU.add),
                             reads=[('gab', gi, 0), ('gab', gi, 1)], writes=[('mergedT', d, c0)])
                ring_release(ej)
                ring_release(fj)

            def h_mm(gw, uw, hres, fl, c0, W, tl):
                bG = bank(); bU = bank()
                P.op('pe', mm_group(PS[bG][:, :W],
                                    lambda i: gw[:, i, fl * 128:(fl + 1) * 128],
                                    lambda i: hT[:, i, c0:c0 + W], 8),
                     reads=hres + hT_res(tl), writes=[('ps', bG)])
                P.op('pe', mm_group(PS[bU][:, :W],
                                    lambda i: uw[:, i, fl * 128:(fl + 1) * 128],
                                    lambda i: hT[:, i, c0:c0 + W], 8),
                     reads=hres + hT_res(tl), writes=[('ps', bU)])
                return bG, bU

            def h_evac(f, c0, W, bG, bU):
                fi = nxt('ft', 2)
                P.op('act', lambda e: e.activation(out=ftmp[:, fi, :W], in_=PS[bG][:, :W], func=AF.Silu),
                     reads=[], writes=[('ftmp', fi), ('ps', bG)])
                P.op('dve', lambda e: e.tensor_tensor(
                    out=actT[:, f, c0:c0 + W], in0=ftmp[:, fi, :W], in1=PS[bU][:, :W], op=ALU.mult),
                     reads=[('ftmp', fi)], writes=[('actT', f, c0), ('ps', bU)])

            h_first = {}

            wj0, wslot0, wres0 = ring_next()
            wj1, wslot1, wres1 = ring_next()
            wvs = [slot_view(wslot0, 0, 8, 512), slot_view(wslot1, 0, 8, 512)]
            wress = [wres0, wres1]
            hj0, hslot0, hres0 = ring_next()
            h_first = {'j': hj0, 'gw': slot_view(hslot0, 0, 8, 256),
                       'uw': slot_view(hslot0, 2048, 8, 256), 'res': hres0}
            steps = []
            for (slot, R, col0) in tts:
                def st_mm(slot=slot, R=R, col0=col0):
                    c0n = 0 if slot < 4 else (512 if slot < 8 else TPS)
                    for half in range(2):
                        b = bank()
                        P.op('pe', mm_group(PS[b][:R, :],
                                            lambda i: mergedT[:, i, col0:col0 + R],
                                            lambda i, wv_=wvs[half]: wv_[:, i, :], 8),
                             reads=wress[half] + [('mergedT', dd, c0n) for dd in range(8)], writes=[('ps', b)])
                        P.op('dve', lambda e, b=b, half=half: e.tensor_tensor(
                            out=xbuf[:R, slot, half * 512:(half + 1) * 512],
                            in0=xbuf[:R, slot, half * 512:(half + 1) * 512], in1=PS[b][:R, :], op=ALU.add),
                             reads=[('x', slot, half)], writes=[('x', slot, half), ('ps', b)])
                ns = norm_stages(slot, R, col0, gffn_rep, 'gffn', copy_eng='act')

                def st0(st_mm=st_mm, sq=ns[0]):
                    st_mm()
                    sq()
                steps.append([st0, ns[1], ns[2], ns[3]])
            lags = [0, 1, 2, 3]
            n_ = len(steps)
            for i_ in range(n_ + 3):
                for si_ in reversed(range(len(lags))):
                    t_ = i_ - lags[si_]
                    if 0 <= t_ < n_:
                        steps[t_][si_]()
                if i_ == n_:
                    c0_, W_, tl_ = nts[0]
                    bG_, bU_ = h_mm(h_first['gw'], h_first['uw'], h_first['res'], 0, c0_, W_, tl_)
                    held.add(bG_); held.add(bU_)
            held.discard(bG_); held.discard(bU_)
            h_evac(0, c0_, W_, bG_, bU_)
            ring_release(wj0)
            ring_release(wj1)

            for j in range(NF // 2):
                if j == 0:
                    hj, gw, uw, hres = h_first['j'], h_first['gw'], h_first['uw'], h_first['res']
                else:
                    hj, hslot, hres = ring_next()
                    gw = slot_view(hslot, 0, 8, 256)
                    uw = slot_view(hslot, 2048, 8, 256)
                for fl in range(2):
                    f = 2 * j + fl
                    for ni, (c0, W, tl) in enumerate(nts):
                        if j == 0 and fl == 0 and ni == 0:
                            continue
                        bG, bU = h_mm(gw, uw, hres, fl, c0, W, tl)
                        h_evac(f, c0, W, bG, bU)
                ring_release(hj)

            fgroups = ((0, 6), (6, 8), (14, 8))
            prefetch = (s + 1 < NSUP)
            for bk, (f0, nf) in enumerate(fgroups[:2]):
                for half in range(2):
                    ij, islot, ires = ring_next()
                    wv = slot_view(islot, 0, 8, 512)
                    pidx = bk * 2 + half
                    if prefetch and pidx == 2:
                        for i_ in range(4):
                            src_ = xp[(s + 1) * TPS + i_ * 128:(s + 1) * TPS + (i_ + 1) * 128, :]
                            P.dma('sp', (lambda e, i_=i_, src_=src_: e.dma_start(out=xstage[i_], in_=src_)),
                                  'xs%d' % i_, reads=(), writes=list(xstage_res[i_]) + [('xstage', i_)])
                    steps = []
                    for (slot, R, col0) in tts:
                        def st_mm(slot=slot, R=R, col0=col0, wv=wv, ires=ires, half=half, f0=f0, nf=nf):
                            c0n = 0 if slot < 4 else (512 if slot < 8 else TPS)
                            b = bank()
                            P.op('pe', mm_group(PS[b][:R, :],
                                                lambda i: actT[:, f0 + i, col0:col0 + R],
                                                lambda i: wv[:, i, :], nf),
                                 reads=ires + [('actT', f0 + i, c0n) for i in range(nf)], writes=[('ps', b)])
                            P.op('dve', lambda e: e.tensor_tensor(
                                out=xbuf[:R, slot, half * 512:(half + 1) * 512],
                                in0=xbuf[:R, slot, half * 512:(half + 1) * 512], in1=PS[b][:R, :], op=ALU.add),
                                 reads=[('x', slot, half)], writes=[('x', slot, half), ('ps', b)])
                        if prefetch and pidx == 3:
                            if slot < 4:
                                ns_ = norm_stages(slot, 128, slot * 128, gmix_rep, 'gmix',
                                                  src=xstage[slot],
                                                  src_res=list(xstage_res[slot]) + [('xstage', slot)])
                            else:
                                ns_ = [(lambda: None)] * 4
                            steps.append([st_mm] + ns_)
                        else:
                            steps.append([st_mm])
                    if prefetch and pidx == 3:
                        pipeline(steps, [0, 0, 1, 2, 3])
                    else:
                        pipeline(steps, [0])
                    ring_release(ij)

            f0, nf = fgroups[2]
            ij0, islot0, ires0 = ring_next()
            ij1, islot1, ires1 = ring_next()
            wvs_ = [slot_view(islot0, 0, 8, 512), slot_view(islot1, 0, 8, 512)]
            iress = [ires0, ires1]
            steps = []
            for (slot, R, col0) in (tts[4:] + tts[:4]):
                def st_mm(slot=slot, R=R, col0=col0, f0=f0, nf=nf, wvs_=wvs_, iress=iress):
                    c0n = 0 if slot < 4 else (512 if slot < 8 else TPS)
                    for half in range(2):
                        b = bank()
                        P.op('pe', mm_group(PS[b][:R, :],
                                            lambda i: actT[:, f0 + i, col0:col0 + R],
                                            lambda i, wv_=wvs_[half]: wv_[:, i, :], nf),
                             reads=iress[half] + [('actT', f0 + i, c0n) for i in range(nf)],
                             writes=[('ps', b)])
                        P.op('dve', lambda e, b=b, half=half: e.tensor_tensor(
                            out=xbuf[:R, slot, half * 512:(half + 1) * 512],
                            in0=xbuf[:R, slot, half * 512:(half + 1) * 512], in1=PS[b][:R, :], op=ALU.add),
                             reads=[('x', slot, half)], writes=[('x', slot, half), ('ps', b)])
                xres = [('x', slot, 0), ('x', slot, 1)]
                k, st_sq, st_rs = rstd_stages(xbuf[:R, slot, :], R, D, xres)

                def st0(st_mm=st_mm, st_sq=st_sq):
                    st_mm()
                    st_sq()

                def st_y(slot=slot, R=R, k=k, xres=xres):
                    yi = nxt('yb', 5)
                    if yi < 3:
                        ysl = ybuf[:R, yi, :]
                        yres = [('ybuf', yi)]
                    elif yi == 3:
                        ysl = shr[:R, 0:D]
                        yres = ['t1', ('gab', 0, 0), ('gab', 0, 1), ('xstage', 2)]
                    else:
                        ysl = shr[:R, 1056:1056 + D]
                        yres = ['t2', ('gab', 1, 0), ('gab', 1, 1), ('xstage', 3)]
                    P.op('dve', lambda e: e.scalar_tensor_tensor(
                        out=ysl, in0=xbuf[:R, slot, :], scalar=rs[:R, k:k + 1],
                        in1=gfin_rep[:R, :], op0=ALU.mult, op1=ALU.mult),
                         reads=xres + [('rs', k), 'gfin'], writes=yres)
                    dst = yp[t0 + slot * 128:t0 + (slot + 1) * 128, :] if slot < 8 else ys[:, :]
                    P.dma('sp', (lambda e: e.dma_start(out=dst, in_=ysl)),
                          'o%d' % yi, reads=yres)
                    if s + 1 < NSUP and slot < 8:
                        xload(s + 1, slot)
                steps.append([st0, st_rs, st_y])
            if prefetch:
                nsd45 = {t_: norm_stages(t_, 128, t_ * 128, gmix_rep, 'gmix') for t_ in (4, 5)}
                carry['nsd45'] = nsd45
                lags = [0, 1, 2]
                n_ = len(steps)
                for i_ in range(n_ + 2):
                    for si_ in range(3):
                        t_ = i_ - lags[si_]
                        if 0 <= t_ < n_:
                            steps[t_][si_]()
                    k_ = i_ - (n_ - 3)
                    if 0 <= k_ < 3:
                        nsd45[4][k_]()
                        nsd45[5][k_]()
            else:
                pipeline(steps, [0, 1, 2])
            ring_release(ij0)
            ring_release(ij1)

        P.wait_all('sp', ['o0', 'o1', 'o2', 'o3', 'o4', 'omisc'])

        sems = {}
        for key in P.sem_keys():
            sems[key] = es.enter_context(nc.semaphore(key))
        with nc.Block() as block:
            @block.sync
            def _(e):
                P.run('sp', e, sems)

            @block.gpsimd
            def _(e):
                P.run('pool', e, sems)

            @block.scalar
            def _(e):
                P.run('act', e, sems)

            @block.vector
            def _(e):
                P.run('dve', e, sems)

            @block.tensor
            def _(e):
                P.run('pe', e, sems)
    return nc


def _consts():
    ident = np.eye(128, dtype=np.float32)
    mask = np.triu(np.ones((128, 128), dtype=np.float32))
    invcnt = np.zeros((128, 4, 16), dtype=np.float32)
    sel = np.zeros((120, 2, 4, 16), dtype=np.float32)
    for g in range(4):
        w = 2 ** (g + 1)
        for t in range(16):
            invcnt[:, g, t] = 1.0 / min(w, t + 1)
        for bl in range(8):
            for k in range(15):
                if k >= 16 - w:
                    for h in range(2):
                        sel[bl * 15 + k, h, g, 8 * h + bl] = 1.0 / w
    return ident, mask, invcnt.reshape(128, 64), sel.reshape(120, 128)


_NC_CACHE = {}


def kernel(x_prompt, x_sample, state_pool, w_in, g_mix, w_pool, s_pool, w_s, b_s, g_v,
           w_pool_out, w_gmlp_out, w_out, g_ffn, w_gate, w_up, w_down, g_final):
    f = lambda a: np.ascontiguousarray(np.asarray(a, dtype=np.float32))
    if 'nc' not in _NC_CACHE:
        _NC_CACHE['nc'] = build_nc()
    nc = _NC_CACHE['nc']
    ident, mask, invcnt, sel = _consts()
    x_prompt = f(x_prompt); x_sample = f(x_sample); state_pool = f(state_pool)
    shared = {
        "w_in": f(w_in)[0], "g_mix": f(g_mix).reshape(1, D), "w_pool": f(w_pool)[0],
        "s_pool": f(s_pool).reshape(1, 512), "w_s": f(w_s)[0], "b_s": f(b_s).reshape(1, 512),
        "g_v": f(g_v).reshape(1, 512), "w_po": f(w_pool_out)[0], "w_go": f(w_gmlp_out)[0],
        "w_out": f(w_out)[0], "g_ffn": f(g_ffn).reshape(1, D), "w_gate": f(w_gate)[0],
        "w_up": f(w_up)[0], "w_down": f(w_down)[0], "g_fin": f(g_final).reshape(1, D),
        "c_ident": ident, "c_mask": mask, "c_invcnt": invcnt, "c_sel": sel,
    }
    in_maps = []
    for c in range(NCORES):
        m = dict(shared)
        m["xp"] = x_prompt[c]
        m["xs"] = np.ascontiguousarray(x_sample[c * NSMP:(c + 1) * NSMP, 0, :])
        m["st"] = np.ascontiguousarray(state_pool[0, c * NSMP:(c + 1) * NSMP].reshape(NSMP * 15, 512))
        in_maps.append(m)
    res = run_bass_kernel_spmd(nc, in_maps, core_ids=list(range(NCORES)))
    rr = res.results
    y_prompt = np.stack([rr[c]["yp"] for c in range(NCORES)], axis=0).astype(np.float32)
    y_sample = np.concatenate([rr[c]["ys"] for c in range(NCORES)], axis=0).reshape(128, 1, D).astype(np.float32)
    new_pool_prompt = np.stack([rr[c]["npp"] for c in range(NCORES)], axis=0)[None].astype(np.float32)
    new_pool_sample = np.concatenate([rr[c]["nps"] for c in range(NCORES)], axis=0)[None].astype(np.float32)
    new_v_sample = np.concatenate([rr[c]["nvs"] for c in range(NCORES)], axis=0).reshape(1, 128, 1, 512).astype(np.float32)
    return (y_prompt, y_sample, new_pool_prompt, new_pool_sample, new_v_sample)
```

```python
import numpy as np
from contextlib import ExitStack
import concourse.bass as bass
import concourse.mybir as mybir
from concourse.bass_utils import run_bass_kernel_spmd

F32 = mybir.dt.float32
BF16 = mybir.dt.bfloat16
AF = mybir.ActivationFunctionType
ALU = mybir.AluOpType

D = 1024
DFF = 2816
NF = DFF // 128
SEQ = 2048
NSMP = 16
NCORES = 8
TPS = 1024
NSUP = SEQ // TPS
TW = TPS + NSMP
EPS = 1e-6
NSLOT = 4
SLOT_ELEMS = 4096
NBANK = 8
ENGS = ('sp', 'pool', 'act', 'dve', 'pe')


class Prog:
    def __init__(self):
        self.streams = {e: [] for e in ENGS}
        self.seq = {e: 0 for e in ENGS}
        self.waited = {e: {} for e in ENGS}
        self.lw = {}
        self.rd = {}
        self.dcnt = {}

    def _deps(self, eng, reads, writes):
        deps = []
        for r in reads:
            t = self.lw.get(r)
            if t is not None:
                deps.append(t)
        for w in writes:
            t = self.lw.get(w)
            if t is not None:
                deps.append(t)
            deps.extend(self.rd.get(w, ()))
        need = {}
        for (sk, val, peng) in deps:
            if peng == 'pe' and eng == 'pe':
                continue
            if self.waited[eng].get(sk, 0) >= val:
                continue
            if need.get(sk, 0) < val:
                need[sk] = val
        for sk, val in need.items():
            self.waited[eng][sk] = val
        return list(need.items())

    def _commit(self, tok, reads, writes):
        for r in reads:
            self.rd.setdefault(r, []).append(tok)
        for w in writes:
            self.lw[w] = tok
            self.rd[w] = []

    def op(self, eng, fn, reads=(), writes=()):
        waits = self._deps(eng, reads, writes)
        self.seq[eng] += 1
        sk = 'E_' + eng
        tok = (sk, self.seq[eng], eng)
        self._commit(tok, reads, writes)
        self.streams[eng].append((waits, fn, sk, 1))
        return tok

    def dma(self, q, fn, sk, reads=(), writes=()):
        waits = self._deps(q, reads, writes)
        self.dcnt[sk] = self.dcnt.get(sk, 0) + 16
        tok = (sk, self.dcnt[sk], None)
        self._commit(tok, reads, writes)
        self.streams[q].append((waits, fn, sk, 16))
        return tok

    def wait_all(self, eng, sks):
        waits = [(sk, self.dcnt[sk]) for sk in sks if self.dcnt.get(sk, 0) > 0]
        self.streams[eng].append((waits, None, None, 0))

    def sem_keys(self):
        keys = ['E_' + e for e in ('pool', 'act', 'dve', 'pe')]
        keys += sorted(self.dcnt.keys())
        return keys

    def run(self, eng, e, sems):
        for (waits, fn, sk, inc) in self.streams[eng]:
            for (wk, val) in waits:
                e.wait_ge(sems[wk], val)
            if fn is None:
                continue
            ins = fn(e)
            ins.then_inc(sems[sk], inc)


def build_nc():
    nc = bass.Bass("TRN2", target_bir_lowering=False)

    def din(name, shape):
        return nc.dram_tensor(name, list(shape), F32, kind="ExternalInput").ap()

    def dout(name, shape):
        return nc.dram_tensor(name, list(shape), F32, kind="ExternalOutput").ap()

    xp = din("xp", [SEQ, D])
    xs = din("xs", [NSMP, D])
    st = din("st", [NSMP * 15, 512])
    w_in = din("w_in", [D, 3584])
    g_mix = din("g_mix", [1, D])
    w_pool = din("w_pool", [4, 128, 128])
    s_pool = din("s_pool", [1, 512])
    w_s = din("w_s", [4, 128, 128])
    b_s = din("b_s", [1, 512])
    g_v = din("g_v", [1, 512])
    w_po = din("w_po", [512, D])
    w_go = din("w_go", [512, D])
    w_out = din("w_out", [D, D])
    g_ffn = din("g_ffn", [1, D])
    w_gate = din("w_gate", [D, DFF])
    w_up = din("w_up", [D, DFF])
    w_down = din("w_down", [DFF, D])
    g_fin = din("g_fin", [1, D])
    c_ident = din("c_ident", [128, 128])
    c_mask = din("c_mask", [128, 128])
    c_invcnt = din("c_invcnt", [128, 64])
    c_sel = din("c_sel", [120, 128])

    yp = dout("yp", [SEQ, D])
    ys = dout("ys", [NSMP, D])
    npp = dout("npp", [15, 512])
    nps = dout("nps", [NSMP, 15, 512])
    nvs = dout("nvs", [NSMP, 512])

    win_v = w_in.rearrange("(k p) m -> p k m", p=128)
    wpo_v = w_po.rearrange("(k p) m -> p k m", p=128)
    wgo_v = w_go.rearrange("(k p) m -> p k m", p=128)
    wout_v = w_out.rearrange("(k p) m -> p k m", p=128)
    wgate_v = w_gate.rearrange("(k p) m -> p k m", p=128)
    wup_v = w_up.rearrange("(k p) m -> p k m", p=128)
    wdown_v = w_down.rearrange("(f p) m -> p f m", p=128)
    st3 = st.rearrange("(b k) c -> b k c", k=15)

    es = ExitStack()
    with es:
        def sb(name, shape, dt):
            return es.enter_context(nc.sbuf_tensor(name, list(shape), dt))

        def ps(name, shape, dt):
            return es.enter_context(nc.psum_tensor(name, list(shape), dt))

        xbuf = sb("xbuf", [128, 9, D], F32)
        hT = sb("hT", [128, 8, TW], BF16)
        ovl = sb("ovl", [128, 25408], BF16)
        AB = sb("AB", [128, 2, 16 + TPS], F32)
        shr = sb("shr", [128, 2 * (16 + TPS)], F32)
        t1 = shr[:, 0:16 + TPS]
        t2 = shr[:, 16 + TPS:2 * (16 + TPS)]
        ybuf = sb("ybuf", [128, 3, D], F32)
        ABf = AB[:, :, :].rearrange("p a t -> p (a t)")
        xstage = [ABf[:, 0:D], ABf[:, D:2 * D], shr[:, 0:D], shr[:, D:2 * D]]
        AB_RES = [('ab', a_, k_) for a_ in range(2) for k_ in ('h', 0, 512)]
        SHR_RES = ['t1', 't2'] + [('gab', a_, b_) for a_ in range(2) for b_ in range(2)]
        xstage_res = [AB_RES, AB_RES, SHR_RES, SHR_RES]
        ring = sb("ring", [128, NSLOT, SLOT_ELEMS], BF16)
        gmix_rep = sb("gmix_rep", [128, D], F32)
        gffn_rep = sb("gffn_rep", [128, D], F32)
        gfin_rep = sb("gfin_rep", [128, D], F32)
        gv_rep = sb("gv_rep", [128, 512], F32)
        ident_f = sb("ident_f", [128, 128], F32)
        ident_b = sb("ident_b", [128, 128], BF16)
        mask_f = sb("mask_f", [128, 128], F32)
        invcnt = sb("invcnt", [128, 4, 16], F32)
        sel = sb("sel", [120, 2, 4, 16], F32)
        stb = sb("stb", [120, 2, 512], F32)
        spT = sb("spT", [128, 4], F32)
        w00rep = sb("w00rep", [128, 4], F32)
        b0rep = sb("b0rep", [128, 4], F32)
        wpool_b = sb("wpool_b", [128, 4, 128], BF16)
        ws_nat = sb("ws_nat", [128, 4, 128], BF16)
        wsT = sb("wsT", [128, 4, 128], BF16)
        bs_rep = sb("bs_rep", [128, 4, 128], BF16)
        mhalf = sb("mhalf", [128, 1], F32)
        NCOL = 96
        ss = sb("ss", [128, NCOL], F32)
        ms = sb("ms", [128, NCOL], F32)
        rs = sb("rs", [128, NCOL], F32)
        halo = sb("halo", [128, 4, 16], F32)
        aTs = sb("aTs", [128, 4, 16], F32)
        stmp = sb("stmp", [128, 16], F32)
        tmp16 = sb("tmp16", [128, 16], F32)
        hb = sb("hb", [128, 2, D], BF16)
        junk = sb("junk", [128, 2, D], BF16)
        ftmp = sb("ftmp", [128, 2, 512], F32)
        gab = shr[:, 0:2048].rearrange("p (a b c) -> p a b c", a=2, b=2)
        vout = sb("vout", [16, 512], F32)
        nppbuf = sb("nppbuf", [16, 512], F32)
        npsbuf = sb("npsbuf", [16, 512], F32)

        o = 0
        pooled = ovl[:, o:o + 4 * TW].rearrange("p (g t) -> p g t", g=4); o += 4 * TW
        paT = ovl[:, o:o + 4 * TW].rearrange("p (g t) -> p g t", g=4); o += 4 * TW
        vtok = ovl[:, o:o + 9 * 512].rearrange("p (n c) -> p n c", n=9); o += 9 * 512
        sgT = ovl[:, o:o + 4 * TW].rearrange("p (g t) -> p g t", g=4); o += 4 * TW
        mergedT = ovl[:, o:o + 8 * TW].rearrange("p (g t) -> p g t", g=8); o += 8 * TW
        assert o <= 25408
        actT = ovl[:, 0:NF * TW].rearrange("p (f t) -> p f t", f=NF)

        PS = [ps("ps%d" % i, [128, 512], F32) for i in range(NBANK)]
        PSB = [p[:, :].bitcast(BF16).rearrange("p (k t) -> p k t", k=8) for p in PS]

        P = Prog()
        state = {'bank': 0, 'col': 0, 'hb': 0, 'ft': 0, 'ga': 0, 'cp': 0, 'jk': 0, 'yb': 0}

        held = set()

        def bank():
            while True:
                b = state['bank']
                state['bank'] = (b + 1) % NBANK
                if b not in held:
                    return b

        def newcol():
            c = state['col']
            state['col'] += 1
            assert c < NCOL
            return c

        def nxt(key, n):
            v = state[key]
            state[key] = (v + 1) % n
            return v

        def slot_view(slot, off, k, m):
            return ring[:, slot, off:off + k * m].rearrange("p (k m) -> p k m", k=k)

        blocks = []
        carry = {}
        for s in range(NSUP):
            for cb in (2, 0, 1):
                blocks.append([(0, 8, 512, win_v[:, :, cb * 512:(cb + 1) * 512])])
            for q in range(4):
                blocks.append([(0, 8, 256, win_v[:, :, 1536 + q * 256:1536 + (q + 1) * 256]),
                               (2048, 8, 256, win_v[:, :, 2560 + q * 256:2560 + (q + 1) * 256])])
                blocks.append([(0, 4, 256, wpo_v[:, :, q * 256:(q + 1) * 256]),
                               (1024, 4, 256, wgo_v[:, :, q * 256:(q + 1) * 256])])
            for half in range(2):
                blocks.append([(0, 8, 512, wout_v[:, :, half * 512:(half + 1) * 512])])
            for j in range(NF // 2):
                blocks.append([(0, 8, 256, wgate_v[:, :, j * 256:(j + 1) * 256]),
                               (2048, 8, 256, wup_v[:, :, j * 256:(j + 1) * 256])])
            for (f0, nf) in ((0, 6), (6, 8), (14, 8)):
                for half in range(2):
                    blocks.append([(0, nf, 512, wdown_v[:, f0:f0 + nf, half * 512:(half + 1) * 512])])
        rstate = {'issued': 0, 'next': 0}

        def ring_issue(j):
            slot = j % NSLOT
            for pi, (off, k, m, src) in enumerate(blocks[j]):
                dst = slot_view(slot, off, k, m)
                P.dma('pool',
                      (lambda e, dst=dst, src=src: e.dma_start(out=dst, in_=src)),
                      'w%d' % slot, reads=(), writes=[('ring', slot, pi)])
            fin = ('w%d' % slot, P.dcnt['w%d' % slot], None)
            for pi in range(len(blocks[j])):
                P.lw[('ring', slot, pi)] = fin
            rstate['issued'] = j + 1

        def ring_next():
            j = rstate['next']
            rstate['next'] += 1
            assert j < rstate['issued']
            slot = j % NSLOT
            return j, slot, [('ring', slot, 0), ('ring', slot, 1)]

        def ring_release(j):
            if j + NSLOT < len(blocks):
                assert rstate['issued'] == j + NSLOT
                ring_issue(j + NSLOT)

        cres = []

        def cdma(q, dst, src, res, noncontig=False, own=None):
            def fn(e, dst=dst, src=src):
                if noncontig:
                    with nc.allow_non_contiguous_dma(reason="tiny constant"):
                        return e.dma_start(out=dst, in_=src)
                return e.dma_start(out=dst, in_=src)
            if own is not None:
                P.dma(q, fn, own, reads=(), writes=[res])
                return
            sk = {'sp': 'cst', 'act': 'csta', 'pool': 'cstp'}[q]
            P.dma(q, fn, sk, reads=(), writes=[res])
            cres.append((sk, res))

        def xload(sidx, slot):
            R = 128 if slot < 8 else NSMP
            src = xp[sidx * TPS + slot * 128:sidx * TPS + (slot + 1) * 128, :] if slot < 8 else xs[:, :]
            P.dma('sp', (lambda e: e.dma_start(out=xbuf[:R, slot, :], in_=src)),
                  'x%d' % slot, reads=(), writes=[('x', slot, 0), ('x', slot, 1)])

        cdma('act', gmix_rep[:, :], g_mix[0:1, :].partition_broadcast(128), 'gmix', own='c_gmix')
        for slot in range(4):
            xload(0, slot)
        cdma('pool', ident_b[:, :], c_ident[:, :], 'ident_b', own='c_idb')
        for slot in range(4, 9):
            xload(0, slot)
        cdma('act', ident_f[:, :], c_ident[:, :], 'ident_f')
        cdma('act', mask_f[:, :], c_mask[:, :], 'mask_f')
        cdma('act', invcnt[:, :, :], c_invcnt.rearrange("p (g t) -> p g t", g=4), 'invcnt')
        cdma('act', sel[:, :, :, :], c_sel.rearrange("p (h g t) -> p h g t", h=2, g=4), 'sel')
        cdma('act', stb[:, 0, :], st[0:120, :], 'stb')
        cdma('act', stb[:, 1, :], st[120:240, :], 'stb1')
        cdma('sp', spT[:, :], s_pool.rearrange("o (g p) -> p (o g)", p=128), 'spT', True)
        cdma('sp', gv_rep[:, :], g_v[0:1, :].partition_broadcast(128), 'gv')
        cdma('sp', w00rep[:, :], w_s[:, 0, 0:1].rearrange("g o -> o g").partition_broadcast(128), 'w00', True)
        cdma('sp', b0rep[:, :], b_s.rearrange("o (g i) -> o g i", g=4)[:, :, 0].partition_broadcast(128), 'b0', True)
        def late_pool_consts():
            cdma('pool', wpool_b[:, :, :], w_pool.rearrange("g c d -> c g d"), 'wpool', own='c_wpool')
            cdma('pool', ws_nat[:, :, :], w_s.rearrange("g i j -> i g j"), 'ws_nat', own='c_wsnat')
            cdma('pool', bs_rep[:, :, :].rearrange("p g i -> p (g i)"), b_s[0:1, :].partition_broadcast(128),
                 'bs_rep', own='c_bsrep')
            for j in range(1, NSLOT):
                ring_issue(j)
        cdma('sp', gffn_rep[:, :], g_ffn[0:1, :].partition_broadcast(128), 'gffn')
        cdma('sp', gfin_rep[:, :], g_fin[0:1, :].partition_broadcast(128), 'gfin')
        for (sk, res) in cres:
            P.lw[res] = (sk, P.dcnt[sk], None)

        ring_issue(0)

        P.dma('sp', lambda e: e.dma_start(out=nps[:, 0:14, :], in_=st3[:, 1:15, :]), 'omisc')

        P.op('dve', lambda e: e.memset(ss[:, :], 0.0), writes=[('ss', c) for c in range(NCOL)])
        P.op('dve', lambda e: e.memset(mhalf[:, :], -0.5), writes=['mhalf'])

        def mm_group(out, lhs_fn, rhs_fn, n):
            def fn(e):
                ins = None
                for i in range(n):
                    ins = e.matmul(out=out, lhsT=lhs_fn(i), rhs=rhs_fn(i), start=(i == 0), stop=(i == n - 1))
                return ins
            return fn

        def rstd_stages(src_ap, R, width, src_res):
            k = newcol()

            def st_sq():
                ji = nxt('jk', 2)
                P.op('act', lambda e: e.activation(out=junk[:R, ji, 0:width], in_=src_ap, func=AF.Square,
                                                   accum_out=ss[:R, k:k + 1]),
                     reads=list(src_res), writes=[('ss', k), ('junk', ji)])

            def st_rs():
                P.op('dve', lambda e: e.tensor_scalar(out=ms[:R, k:k + 1], in0=ss[:R, k:k + 1],
                                                      scalar1=1.0 / width, scalar2=EPS,
                                                      op0=ALU.mult, op1=ALU.add),
                     reads=[('ss', k)], writes=[('ms', k)])
                P.op('pool', lambda e: e.tensor_tensor(out=rs[:R, k:k + 1], in0=ms[:R, k:k + 1],
                                                       in1=mhalf[:R, 0:1], op=ALU.pow),
                     reads=[('ms', k), 'mhalf'], writes=[('rs', k)])
            return k, st_sq, st_rs

        def rstd_chain(src_ap, R, width, src_res):
            k, a, b = rstd_stages(src_ap, R, width, src_res)
            a()
            b()
            return k

        def norm_stages(slot, R, col0, grep, gres, copy_eng=None, src=None, src_res=None):
            xres = [('x', slot, 0), ('x', slot, 1)] if src_res is None else list(src_res)
            xsrc = xbuf[:R, slot, :] if src is None else src
            k, st_sq, st_rs = rstd_stages(xsrc, R, D, xres)
            box = {}

            def st_h():
                hi = nxt('hb', 2)
                box['hi'] = hi
                P.op('dve', lambda e: e.scalar_tensor_tensor(out=hb[:R, hi, :], in0=xsrc,
                                                             scalar=rs[:R, k:k + 1], in1=grep[:R, :],
                                                             op0=ALU.mult, op1=ALU.mult),
                     reads=xres + [('rs', k), gres], writes=[('hb', hi)])

            def st_tr():
                hi = box['hi']
                b = bank()

                def trf(e):
                    ins = None
                    for kc in range(8):
                        ins = e.transpose(out=PSB[b][:, kc, :R], in_=hb[:R, hi, kc * 128:(kc + 1) * 128],
                                          identity=ident_b[:R, :R])
                    return ins
                P.op('pe', trf, reads=[('hb', hi), 'ident_b'], writes=[('ps', b)])
                src = PSB[b][:, :, :R]
                dst = hT[:, :, col0:col0 + R]
                ce = copy_eng if copy_eng is not None else 'act'
                if ce == 'act':
                    P.op('act', lambda e: e.activation(out=dst, in_=src, func=AF.Copy),
                         writes=[('hT', slot, 0), ('hT', slot, 1), ('ps', b)])
                else:
                    P.op('dve', lambda e: e.tensor_copy(out=dst, in_=src),
                         writes=[('hT', slot, 0), ('hT', slot, 1), ('ps', b)])
            return [st_sq, st_rs, st_h, st_tr]

        def pipeline(steps, lags):
            n = len(steps)
            for i in range(n + max(lags)):
                for si in range(len(lags)):
                    t = i - lags[si]
                    if 0 <= t < n:
                        steps[t][si]()

        def hT_res(tl):
            r = []
            for t in tl:
                r += [('hT', t, 0), ('hT', t, 1)]
            return r

        def ws_prep():
            wb_ = bank()

            def ws_tr(e):
                ins = None
                for g in range(4):
                    ins = e.transpose(out=PSB[wb_][:, g, :], in_=ws_nat[:, g, :], identity=ident_b[:, :])
                return ins
            P.op('pe', ws_tr, reads=['ws_nat', 'ident_b'], writes=[('ps', wb_)])
            for g in range(4):
                P.op('dve', lambda e, g=g: e.tensor_tensor(out=wsT[:, g, :], in0=PSB[wb_][:, g, :], in1=mask_f[:, :],
                                                           op=ALU.mult),
                     reads=['mask_f'], writes=[('wsT', g), ('ps', wb_)])

        for s in range(NSUP):
            tts = [(i, 128, i * 128) for i in range(8)]
            nts = [(0, 512, [0, 1, 2, 3]), (512, 512, [4, 5, 6, 7])]
            if s == 0:
                tts.append((8, NSMP, TPS))
                nts.append((TPS, NSMP, [8]))
            t0 = s * TPS

            cj, cslot, cres_ = ring_next()
            wv = slot_view(cslot, 0, 8, 512)

            def c_tile(slot, R, col0, wv=wv, cres_=cres_):
                b = bank()
                P.op('pe', mm_group(PS[b][:R, :],
                                    lambda i, col0=col0, R=R: hT[:, i, col0:col0 + R],
                                    lambda i, wv=wv: wv[:, i, :], 8),
                     reads=cres_ + hT_res([slot]), writes=[('ps', b)])
                fi = nxt('ft', 2)
                P.op('act', lambda e, b=b, fi=fi, R=R: e.activation(out=ftmp[:R, fi, :], in_=PS[b][:R, :],
                                                                   func=AF.Gelu_apprx_tanh),
                     reads=[], writes=[('ftmp', fi), ('ps', b)])
                k = rstd_chain(ftmp[:R, fi, :], R, 512, [('ftmp', fi)])
                if slot < 8:
                    P.op('dve', lambda e, fi=fi, R=R, k=k, slot=slot: e.scalar_tensor_tensor(
                        out=vtok[:R, slot, :], in0=ftmp[:R, fi, :], scalar=rs[:R, k:k + 1], in1=gv_rep[:R, :],
                        op0=ALU.mult, op1=ALU.mult),
                         reads=[('ftmp', fi), ('rs', k), 'gv'], writes=[('vtok', slot)])
                else:
                    P.op('dve', lambda e, fi=fi, R=R, k=k: e.scalar_tensor_tensor(
                        out=vout[:R, :], in0=ftmp[:R, fi, :], scalar=rs[:R, k:k + 1], in1=gv_rep[:R, :],
                        op0=ALU.mult, op1=ALU.mult),
                         reads=[('ftmp', fi), ('rs', k), 'gv'], writes=['vout'])
                    P.op('dve', lambda e, R=R, slot=slot: e.tensor_copy(out=vtok[:R, slot, :], in_=vout[:R, :]),
                         reads=['vout'], writes=[('vtok', slot)])
                    P.dma('sp', lambda e: e.dma_start(out=nvs[:, :], in_=vout[:NSMP, :]), 'omisc', reads=['vout'])


            if s == 0:
                pipeline([norm_stages(slot, R, col0, gmix_rep, 'gmix') for (slot, R, col0) in tts],
                         [0, 1, 2, 3])
                late_pool_consts()
            if s == 0:
                for (slot, R, col0) in tts:
                    c_tile(slot, R, col0)
                ws_prep()
            else:
                nsd = {slot: norm_stages(slot, R, col0, gmix_rep, 'gmix') for (slot, R, col0) in tts[6:8]}
                nsd.update(carry['nsd45'])
                for pair, ci in (((4, 5), (0, 1)), ((6, 7), (2, 3))):
                    for st_i in range(3):
                        for t_ in pair:
                            if t_ >= 6:
                                nsd[t_][st_i]()
                    for c_ in ci:
                        c_tile(*tts[c_])
                    for t_ in pair:
                        nsd[t_][3]()
                for (slot, R, col0) in tts[4:]:
                    c_tile(slot, R, col0)
            ring_release(cj)

            bj, bslot, bres = ring_next()
            wv = slot_view(bslot, 0, 8, 512)
            b4_list = []
            deferred_ops = []
            npp_pending = []
            for g in range(4):
                w = 2 ** (g + 1)
                ai = g % 2
                abres = [('ab', ai, 'h'), ('ab', ai, 0), ('ab', ai, 512)]
                if s == 0:
                    P.op('dve', lambda e, ai=ai: e.memset(AB[:, ai, 0:16], 0.0), writes=[('ab', ai, 'h')])
                else:
                    P.op('dve', lambda e, ai=ai, g=g: e.tensor_copy(out=AB[:, ai, 0:16], in_=halo[:, g, :]),
                         reads=[('halo', g)], writes=[('ab', ai, 'h')])
                for (c0, W, tl) in nts:
                    b = bank()
                    P.op('pe', mm_group(PS[b][:, :W],
                                        lambda i, g=g, wv=wv: wv[:, i, g * 128:(g + 1) * 128],
                                        lambda i, c0=c0, W=W: hT[:, i, c0:c0 + W], 8),
                         reads=bres + hT_res(tl), writes=[('ps', b)])
                    if c0 < TPS:
                        P.op('act', lambda e, b=b, ai=ai, c0=c0, W=W: e.activation(
                            out=AB[:, ai, 16 + c0:16 + c0 + W], in_=PS[b][:, :W], func=AF.Copy),
                             reads=[], writes=[('ab', ai, c0), ('ps', b)])
                    else:
                        P.op('act', lambda e, b=b, g=g, W=W: e.activation(
                            out=aTs[:, g, :], in_=PS[b][:, :W], func=AF.Copy),
                             reads=[], writes=[('aTs', g), ('ps', b)])
                while npp_pending:
                    npp_pending.pop(0)()

                def pool_chain(g=g, w=w, ai=ai, abres=abres, defer=False):
                    def OP(*a_, **k_):
                        if defer:
                            deferred_ops.append(lambda: P.op(*a_, **k_))
                        else:
                            P.op(*a_, **k_)
                    L = TPS
                    A = AB[:, ai, :]
                    OP('dve', lambda e, A=A: e.tensor_tensor(out=t1[:, 1:16 + L], in0=A[:, 1:16 + L],
                                                               in1=A[:, 0:15 + L], op=ALU.add),
                         reads=abres, writes=['t1'])
                    if g >= 1:
                        OP('dve', lambda e: e.tensor_tensor(out=t2[:, 3:16 + L], in0=t1[:, 3:16 + L],
                                                              in1=t1[:, 1:14 + L], op=ALU.add),
                             reads=['t1'], writes=['t2'])
                    if g >= 2:
                        OP('dve', lambda e: e.tensor_tensor(out=t1[:, 7:16 + L], in0=t2[:, 7:16 + L],
                                                              in1=t2[:, 3:12 + L], op=ALU.add),
                             reads=['t2'], writes=['t1'])
                    if g >= 3:
                        OP('dve', lambda e: e.tensor_tensor(out=t2[:, 15:16 + L], in0=t1[:, 15:16 + L],
                                                              in1=t1[:, 7:8 + L], op=ALU.add),
                             reads=['t1'], writes=['t2'])
                    wsum = t1 if g in (0, 2) else t2
                    wres = 't1' if g in (0, 2) else 't2'
                    OP('dve', lambda e, wsum=wsum, A=A, g=g, w=w: e.scalar_tensor_tensor(
                        out=pooled[:, g, 0:L], in0=wsum[:, 16:16 + L], scalar=1.0 / w, in1=A[:, 16:16 + L],
                        op0=ALU.mult, op1=ALU.subtract),
                         reads=[wres] + abres, writes=[('pooled', g)])
                    if s == 0:
                        OP('dve', lambda e, wsum=wsum, g=g: e.tensor_tensor(
                            out=tmp16[:, :], in0=wsum[:, 16:32], in1=invcnt[:, g, :], op=ALU.mult),
                             reads=[wres, 'invcnt'], writes=['tmp16'])
                        OP('dve', lambda e, A=A, g=g: e.tensor_tensor(
                            out=pooled[:, g, 0:16], in0=tmp16[:, :], in1=A[:, 16:32], op=ALU.subtract),
                             reads=['tmp16'] + abres, writes=[('pooled', g)])
                    if s < NSUP - 1:
                        OP('dve', lambda e, A=A, g=g: e.tensor_copy(out=halo[:, g, :], in_=A[:, L:L + 16]),
                             reads=abres, writes=[('halo', g)])
                    else:
                        def npp_ops(A=A, g=g, abres=abres):
                            b = bank()
                            P.op('pe', lambda e: e.transpose(out=PS[b][:15, 0:128], in_=A[:, L + 1:L + 16],
                                                             identity=ident_f[:, :]),
                                 reads=abres + ['ident_f'], writes=[('ps', b)])
                            P.op('act', lambda e: e.activation(out=nppbuf[:15, g * 128:(g + 1) * 128],
                                                               in_=PS[b][:15, 0:128], func=AF.Copy),
                                 reads=[], writes=[('nppbuf', g), ('ps', b)])
                        if defer:
                            deferred_ops.append(npp_ops)
                        else:
                            npp_pending.append(npp_ops)
                    if s == 0:
                        b = bank()
                        OP('pe', mm_group(PS[b][:, 0:NSMP],
                                            lambda i, g=g: stb[:, i, g * 128:(g + 1) * 128],
                                            lambda i, g=g: sel[:, i, g, :], 2),
                             reads=['stb', 'stb1', 'sel'], writes=[('ps', b)])
                        OP('dve', lambda e, b=b, g=g, w=w: e.scalar_tensor_tensor(
                            out=pooled[:, g, TPS:TW], in0=aTs[:, g, :], scalar=(1.0 / w - 1.0),
                            in1=PS[b][:, 0:NSMP], op0=ALU.mult, op1=ALU.add),
                             reads=[('aTs', g)], writes=[('pooled', g, 's'), ('ps', b)])

                pool_chain(defer=(g == 3))

                def b4(g=g):
                    for (c0, W, tl) in nts:
                        b = bank()
                        pres = [('pooled', g)] if c0 < TPS else [('pooled', g, 's')]
                        P.op('pe', lambda e, b=b, g=g, c0=c0, W=W: e.matmul(
                            out=PS[b][:, :W], lhsT=wpool_b[:, g, :], rhs=pooled[:, g, c0:c0 + W],
                            start=True, stop=True),
                             reads=pres + ['wpool'], writes=[('ps', b)])
                        P.op('act', lambda e, b=b, g=g, c0=c0, W=W: e.activation(
                            out=paT[:, g, c0:c0 + W], in_=PS[b][:, :W], func=AF.Copy, scale=spT[:, g:g + 1]),
                             reads=['spT'], writes=[('paT', g, c0), ('ps', b)])
                b4_list.append(b4)
            while npp_pending:
                npp_pending.pop(0)()
            ring_release(bj)
            if s == 0:
                b = bank()

                def atr(e, b=b):
                    ins = None
                    for g in range(4):
                        ins = e.transpose(out=PS[b][:NSMP, g * 128:(g + 1) * 128], in_=aTs[:, g, :],
                                          identity=ident_f[:, :])
                    return ins
                P.op('pe', atr, reads=[('aTs', g) for g in range(4)] + ['ident_f'], writes=[('ps', b)])
                P.op('act', lambda e, b=b: e.activation(out=npsbuf[:NSMP, :], in_=PS[b][:NSMP, :], func=AF.Copy),
                     reads=[], writes=['npsbuf', ('ps', b)])
                P.dma('sp', lambda e: e.dma_start(out=nps[:, 14, :], in_=npsbuf[:NSMP, :]), 'omisc',
                      reads=['npsbuf'])

            dj, dslot, dres = ring_next()
            wv = slot_view(dslot, 0, 8, 512)
            for g in range(4):
                for (c0, W, tl) in nts:
                    bU = bank()
                    P.op('pe', mm_group(PS[bU][:, :W],
                                        lambda i, g=g, wv=wv: wv[:, i, g * 128:(g + 1) * 128],
                                        lambda i, c0=c0, W=W: hT[:, i, c0:c0 + W], 8),
                         reads=dres + hT_res(tl), writes=[('ps', bU)])
                    fi = nxt('ft', 2)
                    P.op('act', lambda e, bU=bU, fi=fi, W=W: e.activation(out=ftmp[:, fi, :W], in_=PS[bU][:, :W],
                                                                         func=AF.Gelu_apprx_tanh),
                         reads=[], writes=[('ftmp', fi), ('ps', bU)])
                    if c0 < TPS:
                        bS = bank()

                        def spat(e, bS=bS, g=g, tl=tl):
                            ins = None
                            for j, tslot in enumerate(tl):
                                ins = e.matmul(out=PS[bS][:, j * 128:(j + 1) * 128],
                                               lhsT=vtok[:, tslot, g * 128:(g + 1) * 128], rhs=wsT[:, g, :],
                                               start=True, stop=True)
                            return ins
                        P.op('pe', spat, reads=[('vtok', t) for t in tl] + [('wsT', g)],
                             writes=[('ps', bS)])
                        P.op('dve', lambda e, bS=bS, g=g: e.tensor_tensor(
                            out=PS[bS][:, :].rearrange("p (j i) -> p j i", j=4),
                            in0=PS[bS][:, :].rearrange("p (j i) -> p j i", j=4),
                            in1=bs_rep[:, g:g + 1, :].to_broadcast([128, 4, 128]), op=ALU.add),
                             reads=['bs_rep'], writes=[('ps', bS)])
                        P.op('dve', lambda e, bS=bS, fi=fi, g=g, c0=c0, W=W: e.tensor_tensor(
                            out=sgT[:, g, c0:c0 + W], in0=ftmp[:, fi, :W], in1=PS[bS][:, :W], op=ALU.mult),
                             reads=[('ftmp', fi)], writes=[('sgT', g, c0), ('ps', bS)])
                    else:
                        bS = bank()
                        P.op('pe', lambda e, g=g, bS=bS: e.transpose(out=PSB[bS][:, 0, :NSMP],
                                                                     in_=vtok[:NSMP, 8, g * 128:(g + 1) * 128],
                                                                     identity=ident_b[:NSMP, :NSMP]),
                             reads=[('vtok', 8), 'ident_b'], writes=[('ps', bS)])
                        P.op('dve', lambda e, g=g, bS=bS: e.tensor_scalar(
                            out=stmp[:, :], in0=PSB[bS][:, 0, :NSMP], scalar1=w00rep[:, g:g + 1],
                            scalar2=b0rep[:, g:g + 1], op0=ALU.mult, op1=ALU.add),
                             reads=['w00', 'b0'], writes=['stmp', ('ps', bS)])
                        P.op('dve', lambda e, fi=fi, g=g, c0=c0, W=W: e.tensor_tensor(
                            out=sgT[:, g, c0:c0 + W], in0=stmp[:, :W], in1=ftmp[:, fi, :W], op=ALU.mult),
                             reads=['stmp', ('ftmp', fi)], writes=[('sgT', g, c0)])
                    if deferred_ops:
                        deferred_ops.pop(0)()
                if g < 3:
                    b4_list[g]()
            while deferred_ops:
                deferred_ops.pop(0)()
            b4_list[3]()
            ring_release(dj)
            if s == NSUP - 1:
                P.dma('sp', lambda e: e.dma_start(out=npp[:, :], in_=nppbuf[:15, :]), 'omisc',
                      reads=[('nppbuf', g) for g in range(4)])

            for q in range(4):
                ej, eslot, eres = ring_next()
                fj, fslot, fres = ring_next()
                gaw = slot_view(eslot, 0, 8, 256)
                gbw = slot_view(eslot, 2048, 8, 256)
                pow_ = slot_view(fslot, 0, 4, 256)
                gow = slot_view(fslot, 1024, 4, 256)
                for dl in range(2):
                    d = 2 * q + dl
                    for (c0, W, tl) in nts:
                        gi = nxt('ga', 2)
                        bA = bank(); bB = bank(); bP = bank(); bQ = bank()
                        P.op('pe', mm_group(PS[bA][:, :W],
                                            lambda i, dl=dl, gaw=gaw: gaw[:, i, dl * 128:(dl + 1) * 128],
                                            lambda i, c0=c0, W=W: hT[:, i, c0:c0 + W], 8),
                             reads=eres + hT_res(tl), writes=[('ps', bA)])
                        P.op('act', lambda e, bA=bA, gi=gi, W=W: e.activation(
                            out=gab[:, gi, 0, :W], in_=PS[bA][:, :W], func=AF.Sigmoid),
                             reads=[], writes=[('gab', gi, 0), ('ps', bA)])
                        P.op('pe', mm_group(PS[bB][:, :W],
                                            lambda i, dl=dl, gbw=gbw: gbw[:, i, dl * 128:(dl + 1) * 128],
                                            lambda i, c0=c0, W=W: hT[:, i, c0:c0 + W], 8),
                             reads=eres + hT_res(tl), writes=[('ps', bB)])
                        P.op('act', lambda e, bB=bB, gi=gi, W=W: e.activation(
                            out=gab[:, gi, 1, :W], in_=PS[bB][:, :W], func=AF.Sigmoid),
                             reads=[], writes=[('gab', gi, 1), ('ps', bB)])
                        P.op('pe', mm_group(PS[bP][:, :W],
                                            lambda i, dl=dl, pow_=pow_: pow_[:, i, dl * 128:(dl + 1) * 128],
                                            lambda i, c0=c0, W=W: paT[:, i, c0:c0 + W], 4),
                             reads=fres + [('paT', gg, c0) for gg in range(4)], writes=[('ps', bP)])
                        P.op('dve', lambda e, bP=bP, gi=gi, W=W: e.tensor_tensor(
                            out=gab[:, gi, 0, :W], in0=gab[:, gi, 0, :W], in1=PS[bP][:, :W], op=ALU.mult),
                             reads=[('gab', gi, 0)], writes=[('gab', gi, 0), ('ps', bP)])
                        P.op('pe', mm_group(PS[bQ][:, :W],
                                            lambda i, dl=dl, gow=gow: gow[:, i, dl * 128:(dl + 1) * 128],
                                            lambda i, c0=c0, W=W: sgT[:, i, c0:c0 + W], 4),
                             reads=fres + [('sgT', gg, c0) for gg in range(4)], writes=[('ps', bQ)])
                        P.op('dve', lambda e, bQ=bQ, gi=gi, W=W: e.tensor_tensor(
                            out=gab[:, gi, 1, :W], in0=gab[:, gi, 1, :W], in1=PS[bQ][:, :W], op=ALU.mult),
                             reads=[('gab', gi, 1)], writes=[('gab', gi, 1), ('ps', bQ)])
                        P.op('dve', lambda e, gi=gi, d=d, c0=c0, W=W: e.tensor_tensor(
                            out=mergedT[:, d, c0:c0 + W], in0=gab[:, gi, 0, :W], in1=gab[:, gi, 1, :W], op=ALU.add),
                             reads=[('gab', gi, 0), ('gab', gi, 1)], writes=[('mergedT', d, c0)])
                ring_release(ej)
                ring_release(fj)

            def h_mm(gw, uw, hres, fl, c0, W, tl):
                bG = bank(); bU = bank()
                P.op('pe', mm_group(PS[bG][:, :W],
                                    lambda i: gw[:, i, fl * 128:(fl + 1) * 128],
                                    lambda i: hT[:, i, c0:c0 + W], 8),
                     reads=hres + hT_res(tl), writes=[('ps', bG)])
                P.op('pe', mm_group(PS[bU][:, :W],
                                    lambda i: uw[:, i, fl * 128:(fl + 1) * 128],
                                    lambda i: hT[:, i, c0:c0 + W], 8),
                     reads=hres + hT_res(tl), writes=[('ps', bU)])
                return bG, bU

            def h_evac(f, c0, W, bG, bU):
                fi = nxt('ft', 2)
                P.op('act', lambda e: e.activation(out=ftmp[:, fi, :W], in_=PS[bG][:, :W], func=AF.Silu),
                     reads=[], writes=[('ftmp', fi), ('ps', bG)])
                P.op('dve', lambda e: e.tensor_tensor(
                    out=actT[:, f, c0:c0 + W], in0=ftmp[:, fi, :W], in1=PS[bU][:, :W], op=ALU.mult),
                     reads=[('ftmp', fi)], writes=[('actT', f, c0), ('ps', bU)])

            h_first = {}

            wj0, wslot0, wres0 = ring_next()
            wj1, wslot1, wres1 = ring_next()
            wvs = [slot_view(wslot0, 0, 8, 512), slot_view(wslot1, 0, 8, 512)]
            wress = [wres0, wres1]
            hj0, hslot0, hres0 = ring_next()
            h_first = {'j': hj0, 'gw': slot_view(hslot0, 0, 8, 256),
                       'uw': slot_view(hslot0, 2048, 8, 256), 'res': hres0}
            steps = []
            for (slot, R, col0) in tts:
                def st_mm(slot=slot, R=R, col0=col0):
                    c0n = 0 if slot < 4 else (512 if slot < 8 else TPS)
                    for half in range(2):
                        b = bank()
                        P.op('pe', mm_group(PS[b][:R, :],
                                            lambda i: mergedT[:, i, col0:col0 + R],
                                            lambda i, wv_=wvs[half]: wv_[:, i, :], 8),
                             reads=wress[half] + [('mergedT', dd, c0n) for dd in range(8)], writes=[('ps', b)])
                        P.op('dve', lambda e, b=b, half=half: e.tensor_tensor(
                            out=xbuf[:R, slot, half * 512:(half + 1) * 512],
                            in0=xbuf[:R, slot, half * 512:(half + 1) * 512], in1=PS[b][:R, :], op=ALU.add),
                             reads=[('x', slot, half)], writes=[('x', slot, half), ('ps', b)])
                ns = norm_stages(slot, R, col0, gffn_rep, 'gffn', copy_eng='act')

                def st0(st_mm=st_mm, sq=ns[0]):
                    st_mm()
                    sq()
                steps.append([st0, ns[1], ns[2], ns[3]])
            lags = [0, 1, 2, 3]
            n_ = len(steps)
            for i_ in range(n_ + 3):
                for si_ in reversed(range(len(lags))):
                    t_ = i_ - lags[si_]
                    if 0 <= t_ < n_:
                        steps[t_][si_]()
                if i_ == n_:
                    c0_, W_, tl_ = nts[0]
                    bG_, bU_ = h_mm(h_first['gw'], h_first['uw'], h_first['res'], 0, c0_, W_, tl_)
                    held.add(bG_); held.add(bU_)
            held.discard(bG_); held.discard(bU_)
            h_evac(0, c0_, W_, bG_, bU_)
            ring_release(wj0)
            ring_release(wj1)

            for j in range(NF // 2):
                if j == 0:
                    hj, gw, uw, hres = h_first['j'], h_first['gw'], h_first['uw'], h_first['res']
                else:
                    hj, hslot, hres = ring_next()
                    gw = slot_view(hslot, 0, 8, 256)
                    uw = slot_view(hslot, 2048, 8, 256)
                for fl in range(2):
                    f = 2 * j + fl
                    for ni, (c0, W, tl) in enumerate(nts):
                        if j == 0 and fl == 0 and ni == 0:
                            continue
                        bG, bU = h_mm(gw, uw, hres, fl, c0, W, tl)
                        h_evac(f, c0, W, bG, bU)
                ring_release(hj)

            fgroups = ((0, 6), (6, 8), (14, 8))
            prefetch = (s + 1 < NSUP)
            for bk, (f0, nf) in enumerate(fgroups[:2]):
                for half in range(2):
                    ij, islot, ires = ring_next()
                    wv = slot_view(islot, 0, 8, 512)
                    pidx = bk * 2 + half
                    if prefetch and pidx == 2:
                        for i_ in range(4):
                            src_ = xp[(s + 1) * TPS + i_ * 128:(s + 1) * TPS + (i_ + 1) * 128, :]
                            P.dma('sp', (lambda e, i_=i_, src_=src_: e.dma_start(out=xstage[i_], in_=src_)),
                                  'xs%d' % i_, reads=(), writes=list(xstage_res[i_]) + [('xstage', i_)])
                    steps = []
                    for (slot, R, col0) in tts:
                        def st_mm(slot=slot, R=R, col0=col0, wv=wv, ires=ires, half=half, f0=f0, nf=nf):
                            c0n = 0 if slot < 4 else (512 if slot < 8 else TPS)
                            b = bank()
                            P.op('pe', mm_group(PS[b][:R, :],
                                                lambda i: actT[:, f0 + i, col0:col0 + R],
                                                lambda i: wv[:, i, :], nf),
                                 reads=ires + [('actT', f0 + i, c0n) for i in range(nf)], writes=[('ps', b)])
                            P.op('dve', lambda e: e.tensor_tensor(
                                out=xbuf[:R, slot, half * 512:(half + 1) * 512],
                                in0=xbuf[:R, slot, half * 512:(half + 1) * 512], in1=PS[b][:R, :], op=ALU.add),
                                 reads=[('x', slot, half)], writes=[('x', slot, half), ('ps', b)])
                        if prefetch and pidx == 3:
                            if slot < 4:
                                ns_ = norm_stages(slot, 128, slot * 128, gmix_rep, 'gmix',
                                                  src=xstage[slot],
                                                  src_res=list(xstage_res[slot]) + [('xstage', slot)])
                            else:
                                ns_ = [(lambda: None)] * 4
                            steps.append([st_mm] + ns_)
                        else:
                            steps.append([st_mm])
                    if prefetch and pidx == 3:
                        pipeline(steps, [0, 0, 1, 2, 3])
                    else:
                        pipeline(steps, [0])
                    ring_release(ij)

            f0, nf = fgroups[2]
            ij0, islot0, ires0 = ring_next()
            ij1, islot1, ires1 = ring_next()
            wvs_ = [slot_view(islot0, 0, 8, 512), slot_view(islot1, 0, 8, 512)]
            iress = [ires0, ires1]
            steps = []
            for (slot, R, col0) in (tts[4:] + tts[:4]):
                def st_mm(slot=slot, R=R, col0=col0, f0=f0, nf=nf, wvs_=wvs_, iress=iress):
                    c0n = 0 if slot < 4 else (512 if slot < 8 else TPS)
                    for half in range(2):
                        b = bank()
                        P.op('pe', mm_group(PS[b][:R, :],
                                            lambda i: actT[:, f0 + i, col0:col0 + R],
                                            lambda i, wv_=wvs_[half]: wv_[:, i, :], nf),
                             reads=iress[half] + [('actT', f0 + i, c0n) for i in range(nf)],
                             writes=[('ps', b)])
                        P.op('dve', lambda e, b=b, half=half: e.tensor_tensor(
                            out=xbuf[:R, slot, half * 512:(half + 1) * 512],
                            in0=xbuf[:R, slot, half * 512:(half + 1) * 512], in1=PS[b][:R, :], op=ALU.add),
                             reads=[('x', slot, half)], writes=[('x', slot, half), ('ps', b)])
                xres = [('x', slot, 0), ('x', slot, 1)]
                k, st_sq, st_rs = rstd_stages(xbuf[:R, slot, :], R, D, xres)

                def st0(st_mm=st_mm, st_sq=st_sq):
                    st_mm()
                    st_sq()

                def st_y(slot=slot, R=R, k=k, xres=xres):
                    yi = nxt('yb', 5)
                    if yi < 3:
                        ysl = ybuf[:R, yi, :]
                        yres = [('ybuf', yi)]
                    elif yi == 3:
                        ysl = shr[:R, 0:D]
                        yres = ['t1', ('gab', 0, 0), ('gab', 0, 1), ('xstage', 2)]
                    else:
                        ysl = shr[:R, 1056:1056 + D]
                        yres = ['t2', ('gab', 1, 0), ('gab', 1, 1), ('xstage', 3)]
                    P.op('dve', lambda e: e.scalar_tensor_tensor(
                        out=ysl, in0=xbuf[:R, slot, :], scalar=rs[:R, k:k + 1],
                        in1=gfin_rep[:R, :], op0=ALU.mult, op1=ALU.mult),
                         reads=xres + [('rs', k), 'gfin'], writes=yres)
                    dst = yp[t0 + slot * 128:t0 + (slot + 1) * 128, :] if slot < 8 else ys[:, :]
                    P.dma('sp', (lambda e: e.dma_start(out=dst, in_=ysl)),
                          'o%d' % yi, reads=yres)
                    if s + 1 < NSUP and slot < 8:
                        xload(s + 1, slot)
                steps.append([st0, st_rs, st_y])
            if prefetch:
                nsd45 = {t_: norm_stages(t_, 128, t_ * 128, gmix_rep, 'gmix') for t_ in (4, 5)}
                carry['nsd45'] = nsd45
                lags = [0, 1, 2]
                n_ = len(steps)
                for i_ in range(n_ + 2):
                    for si_ in range(3):
                        t_ = i_ - lags[si_]
                        if 0 <= t_ < n_:
                            steps[t_][si_]()
                    k_ = i_ - (n_ - 2)
                    if 0 <= k_ < 3:
                        nsd45[4][k_]()
                        nsd45[5][k_]()
            else:
                pipeline(steps, [0, 1, 2])
            ring_release(ij0)
            ring_release(ij1)

        P.wait_all('sp', ['o0', 'o1', 'o2', 'o3', 'o4', 'omisc'])

        sems = {}
        for key in P.sem_keys():
            sems[key] = es.enter_context(nc.semaphore(key))
        with nc.Block() as block:
            @block.sync
            def _(e):
                P.run('sp', e, sems)

            @block.gpsimd
            def _(e):
                P.run('pool', e, sems)

            @block.scalar
            def _(e):
                P.run('act', e, sems)

            @block.vector
            def _(e):
                P.run('dve', e, sems)

            @block.tensor
            def _(e):
                P.run('pe', e, sems)
    return nc


def _consts():
    ident = np.eye(128, dtype=np.float32)
    mask = np.triu(np.ones((128, 128), dtype=np.float32))
    invcnt = np.zeros((128, 4, 16), dtype=np.float32)
    sel = np.zeros((120, 2, 4, 16), dtype=np.float32)
    for g in range(4):
        w = 2 ** (g + 1)
        for t in range(16):
            invcnt[:, g, t] = 1.0 / min(w, t + 1)
        for bl in range(8):
            for k in range(15):
                if k >= 16 - w:
                    for h in range(2):
                        sel[bl * 15 + k, h, g, 8 * h + bl] = 1.0 / w
    return ident, mask, invcnt.reshape(128, 64), sel.reshape(120, 128)


_NC_CACHE = {}


def kernel(x_prompt, x_sample, state_pool, w_in, g_mix, w_pool, s_pool, w_s, b_s, g_v,
           w_pool_out, w_gmlp_out, w_out, g_ffn, w_gate, w_up, w_down, g_final):
    f = lambda a: np.ascontiguousarray(np.asarray(a, dtype=np.float32))
    if 'nc' not in _NC_CACHE:
        _NC_CACHE['nc'] = build_nc()
    nc = _NC_CACHE['nc']
    ident, mask, invcnt, sel = _consts()
    x_prompt = f(x_prompt); x_sample = f(x_sample); state_pool = f(state_pool)
    shared = {
        "w_in": f(w_in)[0], "g_mix": f(g_mix).reshape(1, D), "w_pool": f(w_pool)[0],
        "s_pool": f(s_pool).reshape(1, 512), "w_s": f(w_s)[0], "b_s": f(b_s).reshape(1, 512),
        "g_v": f(g_v).reshape(1, 512), "w_po": f(w_pool_out)[0], "w_go": f(w_gmlp_out)[0],
        "w_out": f(w_out)[0], "g_ffn": f(g_ffn).reshape(1, D), "w_gate": f(w_gate)[0],
        "w_up": f(w_up)[0], "w_down": f(w_down)[0], "g_fin": f(g_final).reshape(1, D),
        "c_ident": ident, "c_mask": mask, "c_invcnt": invcnt, "c_sel": sel,
    }
    in_maps = []
    for c in range(NCORES):
        m = dict(shared)
        m["xp"] = x_prompt[c]
        m["xs"] = np.ascontiguousarray(x_sample[c * NSMP:(c + 1) * NSMP, 0, :])
        m["st"] = np.ascontiguousarray(state_pool[0, c * NSMP:(c + 1) * NSMP].reshape(NSMP * 15, 512))
        in_maps.append(m)
    res = run_bass_kernel_spmd(nc, in_maps, core_ids=list(range(NCORES)))
    rr = res.results
    y_prompt = np.stack([rr[c]["yp"] for c in range(NCORES)], axis=0).astype(np.float32)
    y_sample = np.concatenate([rr[c]["ys"] for c in range(NCORES)], axis=0).reshape(128, 1, D).astype(np.float32)
    new_pool_prompt = np.stack([rr[c]["npp"] for c in range(NCORES)], axis=0)[None].astype(np.float32)
    new_pool_sample = np.concatenate([rr[c]["nps"] for c in range(NCORES)], axis=0)[None].astype(np.float32)
    new_v_sample = np.concatenate([rr[c]["nvs"] for c in range(NCORES)], axis=0).reshape(1, 128, 1, 512).astype(np.float32)
    return (y_prompt, y_sample, new_pool_prompt, new_pool_sample, new_v_sample)
```

```python
import numpy as np
from contextlib import ExitStack
import concourse.bass as bass
import concourse.mybir as mybir
from concourse.bass_utils import run_bass_kernel_spmd

F32 = mybir.dt.float32
BF16 = mybir.dt.bfloat16
AF = mybir.ActivationFunctionType
ALU = mybir.AluOpType

D = 1024
DFF = 2816
NF = DFF // 128
SEQ = 2048
NSMP = 16
NCORES = 8
TPS = 1024
NSUP = SEQ // TPS
TW = TPS + NSMP
EPS = 1e-6
NSLOT = 4
SLOT_ELEMS = 4096
NBANK = 8
ENGS = ('sp', 'pool', 'act', 'dve', 'pe')


class Prog:
    def __init__(self):
        self.streams = {e: [] for e in ENGS}
        self.seq = {e: 0 for e in ENGS}
        self.waited = {e: {} for e in ENGS}
        self.lw = {}
        self.rd = {}
        self.dcnt = {}

    def _deps(self, eng, reads, writes):
        deps = []
        for r in reads:
            t = self.lw.get(r)
            if t is not None:
                deps.append(t)
        for w in writes:
            t = self.lw.get(w)
            if t is not None:
                deps.append(t)
            deps.extend(self.rd.get(w, ()))
        need = {}
        for (sk, val, peng) in deps:
            if peng == 'pe' and eng == 'pe':
                continue
            if self.waited[eng].get(sk, 0) >= val:
                continue
            if need.get(sk, 0) < val:
                need[sk] = val
        for sk, val in need.items():
            self.waited[eng][sk] = val
        return list(need.items())

    def _commit(self, tok, reads, writes):
        for r in reads:
            self.rd.setdefault(r, []).append(tok)
        for w in writes:
            self.lw[w] = tok
            self.rd[w] = []

    def op(self, eng, fn, reads=(), writes=()):
        waits = self._deps(eng, reads, writes)
        self.seq[eng] += 1
        sk = 'E_' + eng
        tok = (sk, self.seq[eng], eng)
        self._commit(tok, reads, writes)
        self.streams[eng].append((waits, fn, sk, 1))
        return tok

    def dma(self, q, fn, sk, reads=(), writes=()):
        waits = self._deps(q, reads, writes)
        self.dcnt[sk] = self.dcnt.get(sk, 0) + 16
        tok = (sk, self.dcnt[sk], None)
        self._commit(tok, reads, writes)
        self.streams[q].append((waits, fn, sk, 16))
        return tok

    def wait_all(self, eng, sks):
        waits = [(sk, self.dcnt[sk]) for sk in sks if self.dcnt.get(sk, 0) > 0]
        self.streams[eng].append((waits, None, None, 0))

    def sem_keys(self):
        keys = ['E_' + e for e in ('pool', 'act', 'dve', 'pe')]
        keys += sorted(self.dcnt.keys())
        return keys

    def run(self, eng, e, sems):
        for (waits, fn, sk, inc) in self.streams[eng]:
            for (wk, val) in waits:
                e.wait_ge(sems[wk], val)
            if fn is None:
                continue
            ins = fn(e)
            ins.then_inc(sems[sk], inc)


def build_nc():
    nc = bass.Bass("TRN2", target_bir_lowering=False)

    def din(name, shape):
        return nc.dram_tensor(name, list(shape), F32, kind="ExternalInput").ap()

    def dout(name, shape):
        return nc.dram_tensor(name, list(shape), F32, kind="ExternalOutput").ap()

    xp = din("xp", [SEQ, D])
    xs = din("xs", [NSMP, D])
    st = din("st", [NSMP * 15, 512])
    w_in = din("w_in", [D, 3584])
    g_mix = din("g_mix", [1, D])
    w_pool = din("w_pool", [4, 128, 128])
    s_pool = din("s_pool", [1, 512])
    w_s = din("w_s", [4, 128, 128])
    b_s = din("b_s", [1, 512])
    g_v = din("g_v", [1, 512])
    w_po = din("w_po", [512, D])
    w_go = din("w_go", [512, D])
    w_out = din("w_out", [D, D])
    g_ffn = din("g_ffn", [1, D])
    w_gate = din("w_gate", [D, DFF])
    w_up = din("w_up", [D, DFF])
    w_down = din("w_down", [DFF, D])
    g_fin = din("g_fin", [1, D])
    c_ident = din("c_ident", [128, 128])
    c_mask = din("c_mask", [128, 128])
    c_invcnt = din("c_invcnt", [128, 64])
    c_sel = din("c_sel", [120, 128])

    yp = dout("yp", [SEQ, D])
    ys = dout("ys", [NSMP, D])
    npp = dout("npp", [15, 512])
    nps = dout("nps", [NSMP, 15, 512])
    nvs = dout("nvs", [NSMP, 512])

    win_v = w_in.rearrange("(k p) m -> p k m", p=128)
    wpo_v = w_po.rearrange("(k p) m -> p k m", p=128)
    wgo_v = w_go.rearrange("(k p) m -> p k m", p=128)
    wout_v = w_out.rearrange("(k p) m -> p k m", p=128)
    wgate_v = w_gate.rearrange("(k p) m -> p k m", p=128)
    wup_v = w_up.rearrange("(k p) m -> p k m", p=128)
    wdown_v = w_down.rearrange("(f p) m -> p f m", p=128)
    st3 = st.rearrange("(b k) c -> b k c", k=15)

    es = ExitStack()
    with es:
        def sb(name, shape, dt):
            return es.enter_context(nc.sbuf_tensor(name, list(shape), dt))

        def ps(name, shape, dt):
            return es.enter_context(nc.psum_tensor(name, list(shape), dt))

        xbuf = sb("xbuf", [128, 9, D], F32)
        hT = sb("hT", [128, 8, TW], BF16)
        ovl = sb("ovl", [128, 25408], BF16)
        AB = sb("AB", [128, 2, 16 + TPS], F32)
        shr = sb("shr", [128, 2 * (16 + TPS)], F32)
        t1 = shr[:, 0:16 + TPS]
        t2 = shr[:, 16 + TPS:2 * (16 + TPS)]
        ybuf = sb("ybuf", [128, 3, D], F32)
        ABf = AB[:, :, :].rearrange("p a t -> p (a t)")
        xstage = [ABf[:, 0:D], ABf[:, D:2 * D], shr[:, 0:D], shr[:, D:2 * D]]
        AB_RES = [('ab', a_, k_) for a_ in range(2) for k_ in ('h', 0, 512)]
        SHR_RES = ['t1', 't2'] + [('gab', a_, b_) for a_ in range(2) for b_ in range(2)]
        xstage_res = [AB_RES, AB_RES, SHR_RES, SHR_RES]
        ring = sb("ring", [128, NSLOT, SLOT_ELEMS], BF16)
        gmix_rep = sb("gmix_rep", [128, D], F32)
        gffn_rep = sb("gffn_rep", [128, D], F32)
        gfin_rep = sb("gfin_rep", [128, D], F32)
        gv_rep = sb("gv_rep", [128, 512], F32)
        ident_f = sb("ident_f", [128, 128], F32)
        ident_b = sb("ident_b", [128, 128], BF16)
        mask_f = sb("mask_f", [128, 128], F32)
        invcnt = sb("invcnt", [128, 4, 16], F32)
        sel = sb("sel", [120, 2, 4, 16], F32)
        stb = sb("stb", [120, 2, 512], F32)
        spT = sb("spT", [128, 4], F32)
        w00rep = sb("w00rep", [128, 4], F32)
        b0rep = sb("b0rep", [128, 4], F32)
        wpool_b = sb("wpool_b", [128, 4, 128], BF16)
        ws_nat = sb("ws_nat", [128, 4, 128], BF16)
        wsT = sb("wsT", [128, 4, 128], BF16)
        bs_rep = sb("bs_rep", [128, 4, 128], BF16)
        mhalf = sb("mhalf", [128, 1], F32)
        NCOL = 96
        ss = sb("ss", [128, NCOL], F32)
        ms = sb("ms", [128, NCOL], F32)
        rs = sb("rs", [128, NCOL], F32)
        halo = sb("halo", [128, 4, 16], F32)
        aTs = sb("aTs", [128, 4, 16], F32)
        stmp = sb("stmp", [128, 16], F32)
        tmp16 = sb("tmp16", [128, 16], F32)
        hb = sb("hb", [128, 2, D], BF16)
        junk = sb("junk", [128, 2, D], BF16)
        ftmp = sb("ftmp", [128, 2, 512], F32)
        gab = shr[:, 0:2048].rearrange("p (a b c) -> p a b c", a=2, b=2)
        vout = sb("vout", [16, 512], F32)
        nppbuf = sb("nppbuf", [16, 512], F32)
        npsbuf = sb("npsbuf", [16, 512], F32)

        o = 0
        pooled = ovl[:, o:o + 4 * TW].rearrange("p (g t) -> p g t", g=4); o += 4 * TW
        paT = ovl[:, o:o + 4 * TW].rearrange("p (g t) -> p g t", g=4); o += 4 * TW
        vtok = ovl[:, o:o + 9 * 512].rearrange("p (n c) -> p n c", n=9); o += 9 * 512
        sgT = ovl[:, o:o + 4 * TW].rearrange("p (g t) -> p g t", g=4); o += 4 * TW
        mergedT = ovl[:, o:o + 8 * TW].rearrange("p (g t) -> p g t", g=8); o += 8 * TW
        assert o <= 25408
        actT = ovl[:, 0:NF * TW].rearrange("p (f t) -> p f t", f=NF)

        PS = [ps("ps%d" % i, [128, 512], F32) for i in range(NBANK)]
        PSB = [p[:, :].bitcast(BF16).rearrange("p (k t) -> p k t", k=8) for p in PS]

        P = Prog()
        state = {'bank': 0, 'col': 0, 'hb': 0, 'ft': 0, 'ga': 0, 'cp': 0, 'jk': 0, 'yb': 0}

        held = set()

        def bank():
            while True:
                b = state['bank']
                state['bank'] = (b + 1) % NBANK
                if b not in held:
                    return b

        def newcol():
            c = state['col']
            state['col'] += 1
            assert c < NCOL
            return c

        def nxt(key, n):
            v = state[key]
            state[key] = (v + 1) % n
            return v

        def slot_view(slot, off, k, m):
            return ring[:, slot, off:off + k * m].rearrange("p (k m) -> p k m", k=k)

        blocks = []
        carry = {}
        for s in range(NSUP):
            for cb in (2, 0, 1):
                blocks.append([(0, 8, 512, win_v[:, :, cb * 512:(cb + 1) * 512])])
            for q in range(4):
                blocks.append([(0, 8, 256, win_v[:, :, 1536 + q * 256:1536 + (q + 1) * 256]),
                               (2048, 8, 256, win_v[:, :, 2560 + q * 256:2560 + (q + 1) * 256])])
                blocks.append([(0, 4, 256, wpo_v[:, :, q * 256:(q + 1) * 256]),
                               (1024, 4, 256, wgo_v[:, :, q * 256:(q + 1) * 256])])
            for half in range(2):
                blocks.append([(0, 8, 512, wout_v[:, :, half * 512:(half + 1) * 512])])
            for j in range(NF // 2):
                blocks.append([(0, 8, 256, wgate_v[:, :, j * 256:(j + 1) * 256]),
                               (2048, 8, 256, wup_v[:, :, j * 256:(j + 1) * 256])])
            for (f0, nf) in ((0, 6), (6, 8), (14, 8)):
                for half in range(2):
                    blocks.append([(0, nf, 512, wdown_v[:, f0:f0 + nf, half * 512:(half + 1) * 512])])
        rstate = {'issued': 0, 'next': 0}

        def ring_issue(j):
            slot = j % NSLOT
            for pi, (off, k, m, src) in enumerate(blocks[j]):
                dst = slot_view(slot, off, k, m)
                P.dma('pool',
                      (lambda e, dst=dst, src=src: e.dma_start(out=dst, in_=src)),
                      'w%d' % slot, reads=(), writes=[('ring', slot, pi)])
            fin = ('w%d' % slot, P.dcnt['w%d' % slot], None)
            for pi in range(len(blocks[j])):
                P.lw[('ring', slot, pi)] = fin
            rstate['issued'] = j + 1

        def ring_next():
            j = rstate['next']
            rstate['next'] += 1
            assert j < rstate['issued']
            slot = j % NSLOT
            return j, slot, [('ring', slot, 0), ('ring', slot, 1)]

        def ring_release(j):
            if j + NSLOT < len(blocks):
                assert rstate['issued'] == j + NSLOT
                ring_issue(j + NSLOT)

        cres = []

        def cdma(q, dst, src, res, noncontig=False, own=None):
            def fn(e, dst=dst, src=src):
                if noncontig:
                    with nc.allow_non_contiguous_dma(reason="tiny constant"):
                        return e.dma_start(out=dst, in_=src)
                return e.dma_start(out=dst, in_=src)
            if own is not None:
                P.dma(q, fn, own, reads=(), writes=[res])
                return
            sk = {'sp': 'cst', 'act': 'csta', 'pool': 'cstp'}[q]
            P.dma(q, fn, sk, reads=(), writes=[res])
            cres.append((sk, res))

        def xload(sidx, slot):
            R = 128 if slot < 8 else NSMP
            src = xp[sidx * TPS + slot * 128:sidx * TPS + (slot + 1) * 128, :] if slot < 8 else xs[:, :]
            P.dma('sp', (lambda e: e.dma_start(out=xbuf[:R, slot, :], in_=src)),
                  'x%d' % slot, reads=(), writes=[('x', slot, 0), ('x', slot, 1)])

        cdma('act', gmix_rep[:, :], g_mix[0:1, :].partition_broadcast(128), 'gmix', own='c_gmix')
        for slot in range(4):
            xload(0, slot)
        cdma('pool', ident_b[:, :], c_ident[:, :], 'ident_b', own='c_idb')
        for slot in range(4, 9):
            xload(0, slot)
        cdma('act', ident_f[:, :], c_ident[:, :], 'ident_f')
        cdma('act', mask_f[:, :], c_mask[:, :], 'mask_f')
        cdma('act', invcnt[:, :, :], c_invcnt.rearrange("p (g t) -> p g t", g=4), 'invcnt')
        cdma('act', sel[:, :, :, :], c_sel.rearrange("p (h g t) -> p h g t", h=2, g=4), 'sel')
        cdma('act', stb[:, 0, :], st[0:120, :], 'stb')
        cdma('act', stb[:, 1, :], st[120:240, :], 'stb1')
        cdma('sp', spT[:, :], s_pool.rearrange("o (g p) -> p (o g)", p=128), 'spT', True)
        cdma('sp', gv_rep[:, :], g_v[0:1, :].partition_broadcast(128), 'gv')
        cdma('sp', w00rep[:, :], w_s[:, 0, 0:1].rearrange("g o -> o g").partition_broadcast(128), 'w00', True)
        cdma('sp', b0rep[:, :], b_s.rearrange("o (g i) -> o g i", g=4)[:, :, 0].partition_broadcast(128), 'b0', True)
        def late_pool_consts():
            cdma('pool', wpool_b[:, :, :], w_pool.rearrange("g c d -> c g d"), 'wpool', own='c_wpool')
            cdma('pool', ws_nat[:, :, :], w_s.rearrange("g i j -> i g j"), 'ws_nat', own='c_wsnat')
            cdma('pool', bs_rep[:, :, :].rearrange("p g i -> p (g i)"), b_s[0:1, :].partition_broadcast(128),
                 'bs_rep', own='c_bsrep')
            for j in range(1, NSLOT):
                ring_issue(j)
        cdma('sp', gffn_rep[:, :], g_ffn[0:1, :].partition_broadcast(128), 'gffn')
        cdma('sp', gfin_rep[:, :], g_fin[0:1, :].partition_broadcast(128), 'gfin')
        for (sk, res) in cres:
            P.lw[res] = (sk, P.dcnt[sk], None)

        ring_issue(0)

        P.dma('sp', lambda e: e.dma_start(out=nps[:, 0:14, :], in_=st3[:, 1:15, :]), 'omisc')

        P.op('dve', lambda e: e.memset(ss[:, :], 0.0), writes=[('ss', c) for c in range(NCOL)])
        P.op('dve', lambda e: e.memset(mhalf[:, :], -0.5), writes=['mhalf'])

        def mm_group(out, lhs_fn, rhs_fn, n):
            def fn(e):
                ins = None
                for i in range(n):
                    ins = e.matmul(out=out, lhsT=lhs_fn(i), rhs=rhs_fn(i), start=(i == 0), stop=(i == n - 1))
                return ins
            return fn

        def rstd_stages(src_ap, R, width, src_res):
            k = newcol()

            def st_sq():
                ji = nxt('jk', 2)
                P.op('act', lambda e: e.activation(out=junk[:R, ji, 0:width], in_=src_ap, func=AF.Square,
                                                   accum_out=ss[:R, k:k + 1]),
                     reads=list(src_res), writes=[('ss', k), ('junk', ji)])

            def st_rs():
                P.op('dve', lambda e: e.tensor_scalar(out=ms[:R, k:k + 1], in0=ss[:R, k:k + 1],
                                                      scalar1=1.0 / width, scalar2=EPS,
                                                      op0=ALU.mult, op1=ALU.add),
                     reads=[('ss', k)], writes=[('ms', k)])
                P.op('pool', lambda e: e.tensor_tensor(out=rs[:R, k:k + 1], in0=ms[:R, k:k + 1],
                                                       in1=mhalf[:R, 0:1], op=ALU.pow),
                     reads=[('ms', k), 'mhalf'], writes=[('rs', k)])
            return k, st_sq, st_rs

        def rstd_chain(src_ap, R, width, src_res):
            k, a, b = rstd_stages(src_ap, R, width, src_res)
            a()
            b()
            return k

        def norm_stages(slot, R, col0, grep, gres, copy_eng=None, src=None, src_res=None):
            xres = [('x', slot, 0), ('x', slot, 1)] if src_res is None else list(src_res)
            xsrc = xbuf[:R, slot, :] if src is None else src
            k, st_sq, st_rs = rstd_stages(xsrc, R, D, xres)
            box = {}

            def st_h():
                hi = nxt('hb', 2)
                box['hi'] = hi
                P.op('dve', lambda e: e.scalar_tensor_tensor(out=hb[:R, hi, :], in0=xsrc,
                                                             scalar=rs[:R, k:k + 1], in1=grep[:R, :],
                                                             op0=ALU.mult, op1=ALU.mult),
                     reads=xres + [('rs', k), gres], writes=[('hb', hi)])

            def st_tr():
                hi = box['hi']
                b = bank()

                def trf(e):
                    ins = None
                    for kc in range(8):
                        ins = e.transpose(out=PSB[b][:, kc, :R], in_=hb[:R, hi, kc * 128:(kc + 1) * 128],
                                          identity=ident_b[:R, :R])
                    return ins
                P.op('pe', trf, reads=[('hb', hi), 'ident_b'], writes=[('ps', b)])
                src = PSB[b][:, :, :R]
                dst = hT[:, :, col0:col0 + R]
                ce = copy_eng if copy_eng is not None else 'act'
                if ce == 'act':
                    P.op('act', lambda e: e.activation(out=dst, in_=src, func=AF.Copy),
                         writes=[('hT', slot, 0), ('hT', slot, 1), ('ps', b)])
                else:
                    P.op('dve', lambda e: e.tensor_copy(out=dst, in_=src),
                         writes=[('hT', slot, 0), ('hT', slot, 1), ('ps', b)])
            return [st_sq, st_rs, st_h, st_tr]

        def pipeline(steps, lags):
            n = len(steps)
            for i in range(n + max(lags)):
                for si in range(len(lags)):
                    t = i - lags[si]
                    if 0 <= t < n:
                        steps[t][si]()

        def hT_res(tl):
            r = []
            for t in tl:
                r += [('hT', t, 0), ('hT', t, 1)]
            return r

        def ws_prep():
            wb_ = bank()

            def ws_tr(e):
                ins = None
                for g in range(4):
                    ins = e.transpose(out=PSB[wb_][:, g, :], in_=ws_nat[:, g, :], identity=ident_b[:, :])
                return ins
            P.op('pe', ws_tr, reads=['ws_nat', 'ident_b'], writes=[('ps', wb_)])
            for g in range(4):
                P.op('dve', lambda e, g=g: e.tensor_tensor(out=wsT[:, g, :], in0=PSB[wb_][:, g, :], in1=mask_f[:, :],
                                                           op=ALU.mult),
                     reads=['mask_f'], writes=[('wsT', g), ('ps', wb_)])

        for s in range(NSUP):
            tts = [(i, 128, i * 128) for i in range(8)]
            nts = [(0, 512, [0, 1, 2, 3]), (512, 512, [4, 5, 6, 7])]
            if s == 0:
                tts.append((8, NSMP, TPS))
                nts.append((TPS, NSMP, [8]))
            t0 = s * TPS

            cj, cslot, cres_ = ring_next()
            wv = slot_view(cslot, 0, 8, 512)

            def c_tile(slot, R, col0, wv=wv, cres_=cres_):
                b = bank()
                P.op('pe', mm_group(PS[b][:R, :],
                                    lambda i, col0=col0, R=R: hT[:, i, col0:col0 + R],
                                    lambda i, wv=wv: wv[:, i, :], 8),
                     reads=cres_ + hT_res([slot]), writes=[('ps', b)])
                fi = nxt('ft', 2)
                P.op('act', lambda e, b=b, fi=fi, R=R: e.activation(out=ftmp[:R, fi, :], in_=PS[b][:R, :],
                                                                   func=AF.Gelu_apprx_tanh),
                     reads=[], writes=[('ftmp', fi), ('ps', b)])
                k = rstd_chain(ftmp[:R, fi, :], R, 512, [('ftmp', fi)])
                if slot < 8:
                    P.op('dve', lambda e, fi=fi, R=R, k=k, slot=slot: e.scalar_tensor_tensor(
                        out=vtok[:R, slot, :], in0=ftmp[:R, fi, :], scalar=rs[:R, k:k + 1], in1=gv_rep[:R, :],
                        op0=ALU.mult, op1=ALU.mult),
                         reads=[('ftmp', fi), ('rs', k), 'gv'], writes=[('vtok', slot)])
                else:
                    P.op('dve', lambda e, fi=fi, R=R, k=k: e.scalar_tensor_tensor(
                        out=vout[:R, :], in0=ftmp[:R, fi, :], scalar=rs[:R, k:k + 1], in1=gv_rep[:R, :],
                        op0=ALU.mult, op1=ALU.mult),
                         reads=[('ftmp', fi), ('rs', k), 'gv'], writes=['vout'])
                    P.op('dve', lambda e, R=R, slot=slot: e.tensor_copy(out=vtok[:R, slot, :], in_=vout[:R, :]),
                         reads=['vout'], writes=[('vtok', slot)])
                    P.dma('sp', lambda e: e.dma_start(out=nvs[:, :], in_=vout[:NSMP, :]), 'omisc', reads=['vout'])


            if s == 0:
                pipeline([norm_stages(slot, R, col0, gmix_rep, 'gmix') for (slot, R, col0) in tts],
                         [0, 1, 2, 3])
                late_pool_consts()
            if s == 0:
                for (slot, R, col0) in tts:
                    c_tile(slot, R, col0)
                ws_prep()
            else:
                nsd = carry['nsd45']
                for c_ in (0, 1):
                    c_tile(*tts[c_])
                for t_ in (6, 7):
                    nsd[t_][3]()
                for (slot, R, col0) in tts[2:]:
                    c_tile(slot, R, col0)
            ring_release(cj)

            bj, bslot, bres = ring_next()
            wv = slot_view(bslot, 0, 8, 512)
            b4_list = []
            deferred_ops = []
            npp_pending = []
            for g in range(4):
                w = 2 ** (g + 1)
                ai = g % 2
                abres = [('ab', ai, 'h'), ('ab', ai, 0), ('ab', ai, 512)]
                if s == 0:
                    P.op('dve', lambda e, ai=ai: e.memset(AB[:, ai, 0:16], 0.0), writes=[('ab', ai, 'h')])
                else:
                    P.op('dve', lambda e, ai=ai, g=g: e.tensor_copy(out=AB[:, ai, 0:16], in_=halo[:, g, :]),
                         reads=[('halo', g)], writes=[('ab', ai, 'h')])
                for (c0, W, tl) in nts:
                    b = bank()
                    P.op('pe', mm_group(PS[b][:, :W],
                                        lambda i, g=g, wv=wv: wv[:, i, g * 128:(g + 1) * 128],
                                        lambda i, c0=c0, W=W: hT[:, i, c0:c0 + W], 8),
                         reads=bres + hT_res(tl), writes=[('ps', b)])
                    if c0 < TPS:
                        P.op('act', lambda e, b=b, ai=ai, c0=c0, W=W: e.activation(
                            out=AB[:, ai, 16 + c0:16 + c0 + W], in_=PS[b][:, :W], func=AF.Copy),
                             reads=[], writes=[('ab', ai, c0), ('ps', b)])
                    else:
                        P.op('act', lambda e, b=b, g=g, W=W: e.activation(
                            out=aTs[:, g, :], in_=PS[b][:, :W], func=AF.Copy),
                             reads=[], writes=[('aTs', g), ('ps', b)])
                while npp_pending:
                    npp_pending.pop(0)()

                def pool_chain(g=g, w=w, ai=ai, abres=abres, defer=False):
                    def OP(*a_, **k_):
                        if defer:
                            deferred_ops.append(lambda: P.op(*a_, **k_))
                        else:
                            P.op(*a_, **k_)
                    L = TPS
                    A = AB[:, ai, :]
                    OP('dve', lambda e, A=A: e.tensor_tensor(out=t1[:, 1:16 + L], in0=A[:, 1:16 + L],
                                                               in1=A[:, 0:15 + L], op=ALU.add),
                         reads=abres, writes=['t1'])
                    if g >= 1:
                        OP('dve', lambda e: e.tensor_tensor(out=t2[:, 3:16 + L], in0=t1[:, 3:16 + L],
                                                              in1=t1[:, 1:14 + L], op=ALU.add),
                             reads=['t1'], writes=['t2'])
                    if g >= 2:
                        OP('dve', lambda e: e.tensor_tensor(out=t1[:, 7:16 + L], in0=t2[:, 7:16 + L],
                                                              in1=t2[:, 3:12 + L], op=ALU.add),
                             reads=['t2'], writes=['t1'])
                    if g >= 3:
                        OP('dve', lambda e: e.tensor_tensor(out=t2[:, 15:16 + L], in0=t1[:, 15:16 + L],
                                                              in1=t1[:, 7:8 + L], op=ALU.add),
                             reads=['t1'], writes=['t2'])
                    wsum = t1 if g in (0, 2) else t2
                    wres = 't1' if g in (0, 2) else 't2'
                    OP('dve', lambda e, wsum=wsum, A=A, g=g, w=w: e.scalar_tensor_tensor(
                        out=pooled[:, g, 0:L], in0=wsum[:, 16:16 + L], scalar=1.0 / w, in1=A[:, 16:16 + L],
                        op0=ALU.mult, op1=ALU.subtract),
                         reads=[wres] + abres, writes=[('pooled', g)])
                    if s == 0:
                        OP('dve', lambda e, wsum=wsum, g=g: e.tensor_tensor(
                            out=tmp16[:, :], in0=wsum[:, 16:32], in1=invcnt[:, g, :], op=ALU.mult),
                             reads=[wres, 'invcnt'], writes=['tmp16'])
                        OP('dve', lambda e, A=A, g=g: e.tensor_tensor(
                            out=pooled[:, g, 0:16], in0=tmp16[:, :], in1=A[:, 16:32], op=ALU.subtract),
                             reads=['tmp16'] + abres, writes=[('pooled', g)])
                    if s < NSUP - 1:
                        OP('dve', lambda e, A=A, g=g: e.tensor_copy(out=halo[:, g, :], in_=A[:, L:L + 16]),
                             reads=abres, writes=[('halo', g)])
                    else:
                        def npp_ops(A=A, g=g, abres=abres):
                            b = bank()
                            P.op('pe', lambda e: e.transpose(out=PS[b][:15, 0:128], in_=A[:, L + 1:L + 16],
                                                             identity=ident_f[:, :]),
                                 reads=abres + ['ident_f'], writes=[('ps', b)])
                            P.op('act', lambda e: e.activation(out=nppbuf[:15, g * 128:(g + 1) * 128],
                                                               in_=PS[b][:15, 0:128], func=AF.Copy),
                                 reads=[], writes=[('nppbuf', g), ('ps', b)])
                        if defer:
                            deferred_ops.append(npp_ops)
                        else:
                            npp_pending.append(npp_ops)
                    if s == 0:
                        b = bank()
                        OP('pe', mm_group(PS[b][:, 0:NSMP],
                                            lambda i, g=g: stb[:, i, g * 128:(g + 1) * 128],
                                            lambda i, g=g: sel[:, i, g, :], 2),
                             reads=['stb', 'stb1', 'sel'], writes=[('ps', b)])
                        OP('dve', lambda e, b=b, g=g, w=w: e.scalar_tensor_tensor(
                            out=pooled[:, g, TPS:TW], in0=aTs[:, g, :], scalar=(1.0 / w - 1.0),
                            in1=PS[b][:, 0:NSMP], op0=ALU.mult, op1=ALU.add),
                             reads=[('aTs', g)], writes=[('pooled', g, 's'), ('ps', b)])

                pool_chain(defer=(g == 3))

                def b4(g=g):
                    for (c0, W, tl) in nts:
                        b = bank()
                        pres = [('pooled', g)] if c0 < TPS else [('pooled', g, 's')]
                        P.op('pe', lambda e, b=b, g=g, c0=c0, W=W: e.matmul(
                            out=PS[b][:, :W], lhsT=wpool_b[:, g, :], rhs=pooled[:, g, c0:c0 + W],
                            start=True, stop=True),
                             reads=pres + ['wpool'], writes=[('ps', b)])
                        P.op('act', lambda e, b=b, g=g, c0=c0, W=W: e.activation(
                            out=paT[:, g, c0:c0 + W], in_=PS[b][:, :W], func=AF.Copy, scale=spT[:, g:g + 1]),
                             reads=['spT'], writes=[('paT', g, c0), ('ps', b)])
                b4_list.append(b4)
            while npp_pending:
                npp_pending.pop(0)()
            ring_release(bj)
            if s == 0:
                b = bank()

                def atr(e, b=b):
                    ins = None
                    for g in range(4):
                        ins = e.transpose(out=PS[b][:NSMP, g * 128:(g + 1) * 128], in_=aTs[:, g, :],
                                          identity=ident_f[:, :])
                    return ins
                P.op('pe', atr, reads=[('aTs', g) for g in range(4)] + ['ident_f'], writes=[('ps', b)])
                P.op('act', lambda e, b=b: e.activation(out=npsbuf[:NSMP, :], in_=PS[b][:NSMP, :], func=AF.Copy),
                     reads=[], writes=['npsbuf', ('ps', b)])
                P.dma('sp', lambda e: e.dma_start(out=nps[:, 14, :], in_=npsbuf[:NSMP, :]), 'omisc',
                      reads=['npsbuf'])

            dj, dslot, dres = ring_next()
            wv = slot_view(dslot, 0, 8, 512)
            for g in range(4):
                for (c0, W, tl) in nts:
                    bU = bank()
                    P.op('pe', mm_group(PS[bU][:, :W],
                                        lambda i, g=g, wv=wv: wv[:, i, g * 128:(g + 1) * 128],
                                        lambda i, c0=c0, W=W: hT[:, i, c0:c0 + W], 8),
                         reads=dres + hT_res(tl), writes=[('ps', bU)])
                    fi = nxt('ft', 2)
                    P.op('act', lambda e, bU=bU, fi=fi, W=W: e.activation(out=ftmp[:, fi, :W], in_=PS[bU][:, :W],
                                                                         func=AF.Gelu_apprx_tanh),
                         reads=[], writes=[('ftmp', fi), ('ps', bU)])
                    if c0 < TPS:
                        bS = bank()

                        def spat(e, bS=bS, g=g, tl=tl):
                            ins = None
                            for j, tslot in enumerate(tl):
                                ins = e.matmul(out=PS[bS][:, j * 128:(j + 1) * 128],
                                               lhsT=vtok[:, tslot, g * 128:(g + 1) * 128], rhs=wsT[:, g, :],
                                               start=True, stop=True)
                            return ins
                        P.op('pe', spat, reads=[('vtok', t) for t in tl] + [('wsT', g)],
                             writes=[('ps', bS)])
                        P.op('dve', lambda e, bS=bS, g=g: e.tensor_tensor(
                            out=PS[bS][:, :].rearrange("p (j i) -> p j i", j=4),
                            in0=PS[bS][:, :].rearrange("p (j i) -> p j i", j=4),
                            in1=bs_rep[:, g:g + 1, :].to_broadcast([128, 4, 128]), op=ALU.add),
                             reads=['bs_rep'], writes=[('ps', bS)])
                        P.op('dve', lambda e, bS=bS, fi=fi, g=g, c0=c0, W=W: e.tensor_tensor(
                            out=sgT[:, g, c0:c0 + W], in0=ftmp[:, fi, :W], in1=PS[bS][:, :W], op=ALU.mult),
                             reads=[('ftmp', fi)], writes=[('sgT', g, c0), ('ps', bS)])
                    else:
                        bS = bank()
                        P.op('pe', lambda e, g=g, bS=bS: e.transpose(out=PSB[bS][:, 0, :NSMP],
                                                                     in_=vtok[:NSMP, 8, g * 128:(g + 1) * 128],
                                                                     identity=ident_b[:NSMP, :NSMP]),
                             reads=[('vtok', 8), 'ident_b'], writes=[('ps', bS)])
                        P.op('dve', lambda e, g=g, bS=bS: e.tensor_scalar(
                            out=stmp[:, :], in0=PSB[bS][:, 0, :NSMP], scalar1=w00rep[:, g:g + 1],
                            scalar2=b0rep[:, g:g + 1], op0=ALU.mult, op1=ALU.add),
                             reads=['w00', 'b0'], writes=['stmp', ('ps', bS)])
                        P.op('dve', lambda e, fi=fi, g=g, c0=c0, W=W: e.tensor_tensor(
                            out=sgT[:, g, c0:c0 + W], in0=stmp[:, :W], in1=ftmp[:, fi, :W], op=ALU.mult),
                             reads=['stmp', ('ftmp', fi)], writes=[('sgT', g, c0)])
                    if deferred_ops:
                        deferred_ops.pop(0)()
                if g < 3:
                    b4_list[g]()
            while deferred_ops:
                deferred_ops.pop(0)()
            b4_list[3]()
            ring_release(dj)
            if s == NSUP - 1:
                P.dma('sp', lambda e: e.dma_start(out=npp[:, :], in_=nppbuf[:15, :]), 'omisc',
                      reads=[('nppbuf', g) for g in range(4)])

            for q in range(4):
                ej, eslot, eres = ring_next()
                fj, fslot, fres = ring_next()
                gaw = slot_view(eslot, 0, 8, 256)
                gbw = slot_view(eslot, 2048, 8, 256)
                pow_ = slot_view(fslot, 0, 4, 256)
                gow = slot_view(fslot, 1024, 4, 256)
                for dl in range(2):
                    d = 2 * q + dl
                    for (c0, W, tl) in nts:
                        gi = nxt('ga', 2)
                        bA = bank(); bB = bank(); bP = bank(); bQ = bank()
                        P.op('pe', mm_group(PS[bA][:, :W],
                                            lambda i, dl=dl, gaw=gaw: gaw[:, i, dl * 128:(dl + 1) * 128],
                                            lambda i, c0=c0, W=W: hT[:, i, c0:c0 + W], 8),
                             reads=eres + hT_res(tl), writes=[('ps', bA)])
                        P.op('act', lambda e, bA=bA, gi=gi, W=W: e.activation(
                            out=gab[:, gi, 0, :W], in_=PS[bA][:, :W], func=AF.Sigmoid),
                             reads=[], writes=[('gab', gi, 0), ('ps', bA)])
                        P.op('pe', mm_group(PS[bB][:, :W],
                                            lambda i, dl=dl, gbw=gbw: gbw[:, i, dl * 128:(dl + 1) * 128],
                                            lambda i, c0=c0, W=W: hT[:, i, c0:c0 + W], 8),
                             reads=eres + hT_res(tl), writes=[('ps', bB)])
                        P.op('act', lambda e, bB=bB, gi=gi, W=W: e.activation(
                            out=gab[:, gi, 1, :W], in_=PS[bB][:, :W], func=AF.Sigmoid),
                             reads=[], writes=[('gab', gi, 1), ('ps', bB)])
                        P.op('pe', mm_group(PS[bP][:, :W],
                                            lambda i, dl=dl, pow_=pow_: pow_[:, i, dl * 128:(dl + 1) * 128],
                                            lambda i, c0=c0, W=W: paT[:, i, c0:c0 + W], 4),
                             reads=fres + [('paT', gg, c0) for gg in range(4)], writes=[('ps', bP)])
                        P.op('dve', lambda e, bP=bP, gi=gi, W=W: e.tensor_tensor(
                            out=gab[:, gi, 0, :W], in0=gab[:, gi, 0, :W], in1=PS[bP][:, :W], op=ALU.mult),
                             reads=[('gab', gi, 0)], writes=[('gab', gi, 0), ('ps', bP)])
                        P.op('pe', mm_group(PS[bQ][:, :W],
                                            lambda i, dl=dl, gow=gow: gow[:, i, dl * 128:(dl + 1) * 128],
                                            lambda i, c0=c0, W=W: sgT[:, i, c0:c0 + W], 4),
                             reads=fres + [('sgT', gg, c0) for gg in range(4)], writes=[('ps', bQ)])
                        P.op('dve', lambda e, bQ=bQ, gi=gi, W=W: e.tensor_tensor(
                            out=gab[:, gi, 1, :W], in0=gab[:, gi, 1, :W], in1=PS[bQ][:, :W], op=ALU.mult),
                             reads=[('gab', gi, 1)], writes=[('gab', gi, 1), ('ps', bQ)])
                        P.op('dve', lambda e, gi=gi, d=d, c0=c0, W=W: e.tensor_tensor(
                            out=mergedT[:, d, c0:c0 + W], in0=gab[:, gi, 0, :W], in1=gab[:, gi, 1, :W], op=ALU.add),
                             reads=[('gab', gi, 0), ('gab', gi, 1)], writes=[('mergedT', d, c0)])
                ring_release(ej)
                ring_release(fj)

            def h_mm(gw, uw, hres, fl, c0, W, tl):
                bG = bank(); bU = bank()
                P.op('pe', mm_group(PS[bG][:, :W],
                                    lambda i: gw[:, i, fl * 128:(fl + 1) * 128],
                                    lambda i: hT[:, i, c0:c0 + W], 8),
                     reads=hres + hT_res(tl), writes=[('ps', bG)])
                P.op('pe', mm_group(PS[bU][:, :W],
                                    lambda i: uw[:, i, fl * 128:(fl + 1) * 128],
                                    lambda i: hT[:, i, c0:c0 + W], 8),
                     reads=hres + hT_res(tl), writes=[('ps', bU)])
                return bG, bU

            def h_evac(f, c0, W, bG, bU):
                fi = nxt('ft', 2)
                P.op('act', lambda e: e.activation(out=ftmp[:, fi, :W], in_=PS[bG][:, :W], func=AF.Silu),
                     reads=[], writes=[('ftmp', fi), ('ps', bG)])
                P.op('dve', lambda e: e.tensor_tensor(
                    out=actT[:, f, c0:c0 + W], in0=ftmp[:, fi, :W], in1=PS[bU][:, :W], op=ALU.mult),
                     reads=[('ftmp', fi)], writes=[('actT', f, c0), ('ps', bU)])

            h_first = {}

            wj0, wslot0, wres0 = ring_next()
            wj1, wslot1, wres1 = ring_next()
            wvs = [slot_view(wslot0, 0, 8, 512), slot_view(wslot1, 0, 8, 512)]
            wress = [wres0, wres1]
            hj0, hslot0, hres0 = ring_next()
            h_first = {'j': hj0, 'gw': slot_view(hslot0, 0, 8, 256),
                       'uw': slot_view(hslot0, 2048, 8, 256), 'res': hres0}
            steps = []
            for (slot, R, col0) in tts:
                def st_mm(slot=slot, R=R, col0=col0):
                    c0n = 0 if slot < 4 else (512 if slot < 8 else TPS)
                    for half in range(2):
                        b = bank()
                        P.op('pe', mm_group(PS[b][:R, :],
                                            lambda i: mergedT[:, i, col0:col0 + R],
                                            lambda i, wv_=wvs[half]: wv_[:, i, :], 8),
                             reads=wress[half] + [('mergedT', dd, c0n) for dd in range(8)], writes=[('ps', b)])
                        P.op('dve', lambda e, b=b, half=half: e.tensor_tensor(
                            out=xbuf[:R, slot, half * 512:(half + 1) * 512],
                            in0=xbuf[:R, slot, half * 512:(half + 1) * 512], in1=PS[b][:R, :], op=ALU.add),
                             reads=[('x', slot, half)], writes=[('x', slot, half), ('ps', b)])
                ns = norm_stages(slot, R, col0, gffn_rep, 'gffn', copy_eng='act')

                def st0(st_mm=st_mm, sq=ns[0]):
                    st_mm()
                    sq()
                steps.append([st0, ns[1], ns[2], ns[3]])
            lags = [0, 1, 2, 3]
            n_ = len(steps)
            for i_ in range(n_ + 3):
                for si_ in reversed(range(len(lags))):
                    t_ = i_ - lags[si_]
                    if 0 <= t_ < n_:
                        steps[t_][si_]()
                if i_ == n_:
                    c0_, W_, tl_ = nts[0]
                    bG_, bU_ = h_mm(h_first['gw'], h_first['uw'], h_first['res'], 0, c0_, W_, tl_)
                    held.add(bG_); held.add(bU_)
            held.discard(bG_); held.discard(bU_)
            h_evac(0, c0_, W_, bG_, bU_)
            ring_release(wj0)
            ring_release(wj1)

            for j in range(NF // 2):
                if j == 0:
                    hj, gw, uw, hres = h_first['j'], h_first['gw'], h_first['uw'], h_first['res']
                else:
                    hj, hslot, hres = ring_next()
                    gw = slot_view(hslot, 0, 8, 256)
                    uw = slot_view(hslot, 2048, 8, 256)
                for fl in range(2):
                    f = 2 * j + fl
                    for ni, (c0, W, tl) in enumerate(nts):
                        if j == 0 and fl == 0 and ni == 0:
                            continue
                        bG, bU = h_mm(gw, uw, hres, fl, c0, W, tl)
                        h_evac(f, c0, W, bG, bU)
                ring_release(hj)

            fgroups = ((0, 6), (6, 8), (14, 8))
            prefetch = (s + 1 < NSUP)
            for bk, (f0, nf) in enumerate(fgroups[:2]):
                for half in range(2):
                    ij, islot, ires = ring_next()
                    wv = slot_view(islot, 0, 8, 512)
                    pidx = bk * 2 + half
                    if prefetch and pidx == 2:
                        for i_ in range(4):
                            src_ = xp[(s + 1) * TPS + i_ * 128:(s + 1) * TPS + (i_ + 1) * 128, :]
                            P.dma('sp', (lambda e, i_=i_, src_=src_: e.dma_start(out=xstage[i_], in_=src_)),
                                  'xs%d' % i_, reads=(), writes=list(xstage_res[i_]) + [('xstage', i_)])
                    steps = []
                    for (slot, R, col0) in tts:
                        def st_mm(slot=slot, R=R, col0=col0, wv=wv, ires=ires, half=half, f0=f0, nf=nf):
                            c0n = 0 if slot < 4 else (512 if slot < 8 else TPS)
                            b = bank()
                            P.op('pe', mm_group(PS[b][:R, :],
                                                lambda i: actT[:, f0 + i, col0:col0 + R],
                                                lambda i: wv[:, i, :], nf),
                                 reads=ires + [('actT', f0 + i, c0n) for i in range(nf)], writes=[('ps', b)])
                            P.op('dve', lambda e: e.tensor_tensor(
                                out=xbuf[:R, slot, half * 512:(half + 1) * 512],
                                in0=xbuf[:R, slot, half * 512:(half + 1) * 512], in1=PS[b][:R, :], op=ALU.add),
                                 reads=[('x', slot, half)], writes=[('x', slot, half), ('ps', b)])
                        if prefetch and pidx == 3:
                            if slot < 4:
                                ns_ = norm_stages(slot, 128, slot * 128, gmix_rep, 'gmix',
                                                  src=xstage[slot],
                                                  src_res=list(xstage_res[slot]) + [('xstage', slot)])
                            else:
                                ns_ = [(lambda: None)] * 4
                            steps.append([st_mm] + ns_)
                        else:
                            steps.append([st_mm])
                    if prefetch and pidx == 3:
                        pipeline(steps, [0, 0, 1, 2, 3])
                    else:
                        pipeline(steps, [0])
                    ring_release(ij)

            f0, nf = fgroups[2]
            ij0, islot0, ires0 = ring_next()
            ij1, islot1, ires1 = ring_next()
            wvs_ = [slot_view(islot0, 0, 8, 512), slot_view(islot1, 0, 8, 512)]
            iress = [ires0, ires1]
            steps = []
            for (slot, R, col0) in (tts[4:] + tts[:4]):
                def st_mm(slot=slot, R=R, col0=col0, f0=f0, nf=nf, wvs_=wvs_, iress=iress):
                    c0n = 0 if slot < 4 else (512 if slot < 8 else TPS)
                    for half in range(2):
                        b = bank()
                        P.op('pe', mm_group(PS[b][:R, :],
                                            lambda i: actT[:, f0 + i, col0:col0 + R],
                                            lambda i, wv_=wvs_[half]: wv_[:, i, :], nf),
                             reads=iress[half] + [('actT', f0 + i, c0n) for i in range(nf)],
                             writes=[('ps', b)])
                        P.op('dve', lambda e, b=b, half=half: e.tensor_tensor(
                            out=xbuf[:R, slot, half * 512:(half + 1) * 512],
                            in0=xbuf[:R, slot, half * 512:(half + 1) * 512], in1=PS[b][:R, :], op=ALU.add),
                             reads=[('x', slot, half)], writes=[('x', slot, half), ('ps', b)])
                xres = [('x', slot, 0), ('x', slot, 1)]
                k, st_sq, st_rs = rstd_stages(xbuf[:R, slot, :], R, D, xres)

                def st0(st_mm=st_mm, st_sq=st_sq):
                    st_mm()
                    st_sq()

                def st_y(slot=slot, R=R, k=k, xres=xres):
                    yi = nxt('yb', 5)
                    if yi < 3:
                        ysl = ybuf[:R, yi, :]
                        yres = [('ybuf', yi)]
                    elif yi == 3:
                        ysl = shr[:R, 0:D]
                        yres = ['t1', ('gab', 0, 0), ('gab', 0, 1), ('xstage', 2)]
                    else:
                        ysl = shr[:R, 1056:1056 + D]
                        yres = ['t2', ('gab', 1, 0), ('gab', 1, 1), ('xstage', 3)]
                    P.op('dve', lambda e: e.scalar_tensor_tensor(
                        out=ysl, in0=xbuf[:R, slot, :], scalar=rs[:R, k:k + 1],
                        in1=gfin_rep[:R, :], op0=ALU.mult, op1=ALU.mult),
                         reads=xres + [('rs', k), 'gfin'], writes=yres)
                    dst = yp[t0 + slot * 128:t0 + (slot + 1) * 128, :] if slot < 8 else ys[:, :]
                    P.dma('sp', (lambda e: e.dma_start(out=dst, in_=ysl)),
                          'o%d' % yi, reads=yres)
                    if s + 1 < NSUP and slot < 8:
                        xload(s + 1, slot)
                steps.append([st0, st_rs, st_y])
            if prefetch:
                nsd45 = {t_: norm_stages(t_, 128, t_ * 128, gmix_rep, 'gmix') for t_ in (4, 5, 6, 7)}
                carry['nsd45'] = nsd45
                lags = [0, 1, 2]
                n_ = len(steps)
                sched = {-4: [(4, 0), (5, 0)], -3: [(4, 1), (5, 1)], -2: [(4, 2), (5, 2), (6, 0)],
                         -1: [(4, 3), (5, 3), (6, 1), (7, 0)], 0: [(6, 2), (7, 1)], 1: [(7, 2)]}
                for i_ in range(n_ + 2):
                    for si_ in range(3):
                        t_ = i_ - lags[si_]
                        if 0 <= t_ < n_:
                            steps[t_][si_]()
                    for (tt_, st_) in sched.get(i_ - n_, ()):
                        nsd45[tt_][st_]()
            else:
                pipeline(steps, [0, 1, 2])
            ring_release(ij0)
            ring_release(ij1)

        P.wait_all('sp', ['o0', 'o1', 'o2', 'o3', 'o4', 'omisc'])

        sems = {}
        for key in P.sem_keys():
            sems[key] = es.enter_context(nc.semaphore(key))
        with nc.Block() as block:
            @block.sync
            def _(e):
                P.run('sp', e, sems)

            @block.gpsimd
            def _(e):
                P.run('pool', e, sems)

            @block.scalar
            def _(e):
                P.run('act', e, sems)

            @block.vector
            def _(e):
                P.run('dve', e, sems)

            @block.tensor
            def _(e):
                P.run('pe', e, sems)
    return nc


def _consts():
    ident = np.eye(128, dtype=np.float32)
    mask = np.triu(np.ones((128, 128), dtype=np.float32))
    invcnt = np.zeros((128, 4, 16), dtype=np.float32)
    sel = np.zeros((120, 2, 4, 16), dtype=np.float32)
    for g in range(4):
        w = 2 ** (g + 1)
        for t in range(16):
            invcnt[:, g, t] = 1.0 / min(w, t + 1)
        for bl in range(8):
            for k in range(15):
                if k >= 16 - w:
                    for h in range(2):
                        sel[bl * 15 + k, h, g, 8 * h + bl] = 1.0 / w
    return ident, mask, invcnt.reshape(128, 64), sel.reshape(120, 128)


_NC_CACHE = {}


def kernel(x_prompt, x_sample, state_pool, w_in, g_mix, w_pool, s_pool, w_s, b_s, g_v,
           w_pool_out, w_gmlp_out, w_out, g_ffn, w_gate, w_up, w_down, g_final):
    f = lambda a: np.ascontiguousarray(np.asarray(a, dtype=np.float32))
    if 'nc' not in _NC_CACHE:
        _NC_CACHE['nc'] = build_nc()
    nc = _NC_CACHE['nc']
    ident, mask, invcnt, sel = _consts()
    x_prompt = f(x_prompt); x_sample = f(x_sample); state_pool = f(state_pool)
    shared = {
        "w_in": f(w_in)[0], "g_mix": f(g_mix).reshape(1, D), "w_pool": f(w_pool)[0],
        "s_pool": f(s_pool).reshape(1, 512), "w_s": f(w_s)[0], "b_s": f(b_s).reshape(1, 512),
        "g_v": f(g_v).reshape(1, 512), "w_po": f(w_pool_out)[0], "w_go": f(w_gmlp_out)[0],
        "w_out": f(w_out)[0], "g_ffn": f(g_ffn).reshape(1, D), "w_gate": f(w_gate)[0],
        "w_up": f(w_up)[0], "w_down": f(w_down)[0], "g_fin": f(g_final).reshape(1, D),
        "c_ident": ident, "c_mask": mask, "c_invcnt": invcnt, "c_sel": sel,
    }
    in_maps = []
    for c in range(NCORES):
        m = dict(shared)
        m["xp"] = x_prompt[c]
        m["xs"] = np.ascontiguousarray(x_sample[c * NSMP:(c + 1) * NSMP, 0, :])
        m["st"] = np.ascontiguousarray(state_pool[0, c * NSMP:(c + 1) * NSMP].reshape(NSMP * 15, 512))
        in_maps.append(m)
    res = run_bass_kernel_spmd(nc, in_maps, core_ids=list(range(NCORES)))
    rr = res.results
    y_prompt = np.stack([rr[c]["yp"] for c in range(NCORES)], axis=0).astype(np.float32)
    y_sample = np.concatenate([rr[c]["ys"] for c in range(NCORES)], axis=0).reshape(128, 1, D).astype(np.float32)
    new_pool_prompt = np.stack([rr[c]["npp"] for c in range(NCORES)], axis=0)[None].astype(np.float32)
    new_pool_sample = np.concatenate([rr[c]["nps"] for c in range(NCORES)], axis=0)[None].astype(np.float32)
    new_v_sample = np.concatenate([rr[c]["nvs"] for c in range(NCORES)], axis=0).reshape(1, 128, 1, 512).astype(np.float32)
    return (y_prompt, y_sample, new_pool_prompt, new_pool_sample, new_v_sample)
```
